# Optimizing a Trainium2 kernel written in Bass

```python
import math
import jax
import jax.numpy as jnp
from jax import lax
import numpy as np

D_MODEL = 1024
BATCH = 2
SEQ = 8192
DEPTH = 2

GRID_W = 64
CTX_LEN = 256

F32 = jnp.float32
NORM_EPS = 1e-6

DN_ALPHA = (2.0 * DEPTH) ** 0.25
DN_BETA = (8.0 * DEPTH) ** -0.25

MLA_HEADS = 8
MLA_NOPE = 64
MLA_ROPE = 32
MLA_V = 64
MLA_Q_LORA = 384
MLA_KV_LORA = 256
MLA_SCALE = (MLA_NOPE + MLA_ROPE) ** -0.5
ROPE_BASE = 10000.0
Q_BLOCK = 128

S5_WIDTH = 512
S5_GROUP = 16
S5_GROUPS = S5_WIDTH // S5_GROUP
S5_STATE = 64
S5_DT_MIN = 1e-3
S5_DT_MAX = 1e-1

EVEN_IN = MLA_Q_LORA + MLA_KV_LORA + MLA_ROPE + S5_WIDTH
EVEN_MIX = MLA_HEADS * MLA_V + S5_WIDTH

HG_HEADS = 8
HG_DK = 128
HG_DV = D_MODEL // HG_HEADS
HG_WIDTH = HG_HEADS * HG_DK
HG_VWIDTH = HG_HEADS * HG_DV
HG_IN = 3 * HG_WIDTH + 2 * HG_VWIDTH
HG_CHUNK = 64

FFN_HIDDEN = 2816
CONV_WIDTH = 3

N_EVEN = (DEPTH + 1) // 2
N_ODD = DEPTH // 2

kernel_name = "hybrid_mla_s5_hgrn2_convffn_prefix_dit"


def _layer_norm(x, g, b):
    xf = x.astype(F32)
    mu = jnp.mean(xf, -1, keepdims=True)
    var = jnp.mean(jnp.square(xf - mu), -1, keepdims=True)
    y = (xf - mu) * lax.rsqrt(var + NORM_EPS)
    return (y * g.astype(F32) + b.astype(F32)).astype(x.dtype)


def _rms_norm(x, g):
    xf = x.astype(F32)
    y = xf * lax.rsqrt(jnp.mean(jnp.square(xf), -1, keepdims=True) + NORM_EPS)
    return (y * g.astype(F32)).astype(x.dtype)


def _axial_rope_tables(length):
    rows = length // GRID_W
    row = jnp.repeat(jnp.arange(rows, dtype=F32), GRID_W)
    col = jnp.tile(jnp.arange(GRID_W, dtype=F32), rows)
    n_freq = MLA_ROPE // 4
    inv = ROPE_BASE ** (-jnp.arange(n_freq, dtype=F32) / n_freq)
    ar = row[:, None] * inv
    ac = col[:, None] * inv
    ang = jnp.concatenate([ar, ar, ac, ac], axis=-1)
    return jnp.cos(ang), jnp.sin(ang)


def _rope(x, cos, sin):
    xs = x.reshape(x.shape[:-1] + (2, 2, MLA_ROPE // 4))
    rot = jnp.stack([-xs[..., 1, :], xs[..., 0, :]], axis=-2).reshape(x.shape)
    return (x * cos + rot * sin).astype(x.dtype)


def _mla_qkv(cq, ckv, kr, q_norm, w_uq, kv_norm, w_ukv, rope):
    b, l = cq.shape[:2]
    q = (_rms_norm(cq, q_norm) @ w_uq).reshape(b, l, MLA_HEADS, MLA_NOPE + MLA_ROPE)
    kv = (_rms_norm(ckv, kv_norm) @ w_ukv).reshape(b, l, MLA_HEADS, MLA_NOPE + MLA_V)
    q_nope, q_rope = q[..., :MLA_NOPE], q[..., MLA_NOPE:]
    k_nope, v = kv[..., :MLA_NOPE], kv[..., MLA_NOPE:]
    if rope is not None:
        cos, sin = rope
        q_rope = _rope(q_rope, cos[:, None], sin[:, None])
        kr = _rope(kr, cos, sin)
    q = jnp.concatenate([q_nope, q_rope], -1)
    k = jnp.concatenate([k_nope, jnp.broadcast_to(kr[:, :, None, :], (b, l, MLA_HEADS, MLA_ROPE))], -1)
    return q, k, v


def _softmax_attend(q, k, v):
    s = jnp.einsum('bqhd,bkhd->bhqk', q, k, preferred_element_type=F32) * MLA_SCALE
    p = jax.nn.softmax(s, axis=-1).astype(v.dtype)
    return jnp.einsum('bhqk,bkhd->bqhd', p, v)


def _blocked_attend(q, k, v):
    b, l, h, d = q.shape
    qb = q.reshape(b, l // Q_BLOCK, Q_BLOCK, h, d).transpose(1, 0, 2, 3, 4)
    ob = lax.map(lambda qq: _softmax_attend(qq, k, v), qb)
    return ob.transpose(1, 0, 2, 3, 4).reshape(b, l, h, v.shape[-1])


def _s5_discretize(lam_re, lam_im, log_dt, b_re, b_im):
    lam_re, lam_im = lam_re.astype(F32), lam_im.astype(F32)
    dt = jnp.exp(log_dt.astype(F32))[:, None]
    mag = jnp.exp(lam_re * dt)
    lb_re = mag * jnp.cos(lam_im * dt)
    lb_im = mag * jnp.sin(lam_im * dt)
    den = lam_re * lam_re + lam_im * lam_im
    nr = lb_re - 1.0
    fr = (nr * lam_re + lb_im * lam_im) / den
    fi = (lb_im * lam_re - nr * lam_im) / den
    b_re, b_im = b_re.astype(F32), b_im.astype(F32)
    bb_re = fr[..., None] * b_re - fi[..., None] * b_im
    bb_im = fr[..., None] * b_im + fi[..., None] * b_re
    return lb_re, lb_im, bb_re, bb_im


def _s5_combine(e1, e2):
    a1r, a1i, b1r, b1i = e1
    a2r, a2i, b2r, b2i = e2
    return (a2r * a1r - a2i * a1i,
            a2r * a1i + a2i * a1r,
            a2r * b1r - a2i * b1i + b2r,
            a2r * b1i + a2i * b1r + b2i)


def _s5_states(u, lb_re, lb_im, bb_re, bb_im, h0):
    x_re = jnp.einsum('blgp,gnp->blgn', u, bb_re)
    x_im = jnp.einsum('blgp,gnp->blgn', u, bb_im)
    if h0 is not None:
        h_re, h_im = h0
        x_re = x_re.at[:, 0].add(lb_re * h_re - lb_im * h_im)
        x_im = x_im.at[:, 0].add(lb_re * h_im + lb_im * h_re)
    a_re = jnp.broadcast_to(lb_re, x_re.shape)
    a_im = jnp.broadcast_to(lb_im, x_im.shape)
    _, _, s_re, s_im = lax.associative_scan(_s5_combine, (a_re, a_im, x_re, x_im), axis=1)
    return s_re, s_im


def _s5_bidirectional(u, disc, c_re, c_im, h0s, need_out):
    y = None
    finals = []
    for d in range(2):
        ud = u if d == 0 else jnp.flip(u, 1)
        s_re, s_im = _s5_states(ud, *disc[d], None if h0s is None else h0s[d])
        finals.append((s_re[:, -1], s_im[:, -1]))
        if need_out:
            yd = (jnp.einsum('blgn,gpn->blgp', s_re, c_re[d].astype(F32))
                  - jnp.einsum('blgn,gpn->blgp', s_im, c_im[d].astype(F32)))
            yd = yd if d == 0 else jnp.flip(yd, 1)
            y = yd if y is None else y + yd
    return y, finals


def _even_mixer(u_ctx, u_lat, w_in, q_norm, w_uq, kv_norm, w_ukv, lam_re, lam_im, log_dt,
                b_re, b_im, c_re, c_im, d_skip, w_glu, w_out, need_ctx):
    cuts = [MLA_Q_LORA, MLA_Q_LORA + MLA_KV_LORA, MLA_Q_LORA + MLA_KV_LORA + MLA_ROPE]
    cq_c, ckv_c, kr_c, s_c = jnp.split(u_ctx @ w_in, cuts, axis=-1)
    cq_l, ckv_l, kr_l, s_l = jnp.split(u_lat @ w_in, cuts, axis=-1)
    b, l = u_lat.shape[:2]
    lc = u_ctx.shape[1]
    rope = _axial_rope_tables(l)
    q_c, k_c, v_c = _mla_qkv(cq_c, ckv_c, kr_c, q_norm, w_uq, kv_norm, w_ukv, None)
    q_l, k_l, v_l = _mla_qkv(cq_l, ckv_l, kr_l, q_norm, w_uq, kv_norm, w_ukv, rope)
    k_all = jnp.concatenate([k_c, k_l], axis=1)
    v_all = jnp.concatenate([v_c, v_l], axis=1)
    att_l = _blocked_attend(q_l, k_all, v_all).reshape(b, l, MLA_HEADS * MLA_V)
    disc = [_s5_discretize(lam_re[d], lam_im[d], log_dt[d], b_re[d], b_im[d]) for d in range(2)]
    us_c = s_c.astype(F32).reshape(b, lc, S5_GROUPS, S5_GROUP)
    us_l = s_l.astype(F32).reshape(b, l, S5_GROUPS, S5_GROUP)
    ys_c, fin_c = _s5_bidirectional(us_c, disc, c_re, c_im, None, need_ctx)
    ys_l, _ = _s5_bidirectional(us_l, disc, c_re, c_im, fin_c, True)
    d_g = d_skip.astype(F32).reshape(S5_GROUPS, S5_GROUP)

    def s5_out(y, us):
        z = (y + d_g * us).reshape(us.shape[0], us.shape[1], S5_WIDTH)
        z = jax.nn.gelu(z).astype(u_lat.dtype)
        return z * jax.nn.sigmoid(z @ w_glu)

    y_lat = jnp.concatenate([att_l, s5_out(ys_l, us_l)], -1) @ w_out
    y_ctx = None
    if need_ctx:
        att_c = _softmax_attend(q_c, k_c, v_c).reshape(b, lc, MLA_HEADS * MLA_V)
        y_ctx = jnp.concatenate([att_c, s5_out(ys_c, us_c)], -1) @ w_out
    return y_ctx, y_lat


def _hgrn_chunk_scan(q, k, v, logf, s0):
    b, l, h, _ = q.shape
    nc = l // HG_CHUNK

    def to_chunks(t):
        return t.reshape(b, nc, HG_CHUNK, h, t.shape[-1]).transpose(1, 0, 3, 2, 4)

    tri = jnp.tril(jnp.ones((HG_CHUNK, HG_CHUNK), dtype=bool))[:, :, None]

    def step(state, xs):
        qc, kc, vc, gc = xs
        cum = jnp.cumsum(gc, axis=-2)
        last = cum[..., -1:, :]
        dec = jnp.exp(jnp.where(tri, cum[..., :, None, :] - cum[..., None, :, :], -jnp.inf))
        scores = jnp.einsum('bhtk,bhsk,bhtsk->bhts', qc, kc, dec)
        o = (jnp.einsum('bhts,bhsv->bhtv', scores, vc)
             + jnp.einsum('bhtk,bhkv->bhtv', qc * jnp.exp(cum), state))
        state = (jnp.exp(last)[..., 0, :, None] * state
                 + jnp.einsum('bhsk,bhsv->bhkv', kc * jnp.exp(last - cum), vc))
        return state, o

    s_fin, o = lax.scan(step, s0, (to_chunks(q), to_chunks(k), to_chunks(v), to_chunks(logf)))
    o = o.transpose(1, 0, 3, 2, 4).reshape(b, l, h, v.shape[-1])
    return o, s_fin


def _hgrn_final_state(k, v, logf):
    tail = jnp.flip(jnp.cumsum(jnp.flip(logf, 1), axis=1), 1) - logf
    return jnp.einsum('blhk,blhv->bhkv', k * jnp.exp(tail), v)


def _odd_mixer(u_ctx, u_lat, w_in, lb, norm_g, w_out, need_ctx):
    cuts = [HG_WIDTH, 2 * HG_WIDTH, 3 * HG_WIDTH, 3 * HG_WIDTH + HG_VWIDTH]

    def prep(u):
        bb, ll = u.shape[:2]
        q, ff, fb, i, g = jnp.split(u @ w_in, cuts, axis=-1)
        heads = lambda t: t.astype(F32).reshape(bb, ll, HG_HEADS, -1)
        dirs = []
        for d, fpre in enumerate((ff, fb)):
            lbd = lb[d].reshape(HG_HEADS, HG_DK)
            f = lbd + (1.0 - lbd) * jax.nn.sigmoid(heads(fpre))
            dirs.append((1.0 - f, jnp.log(f)))
        return heads(q), heads(i), g, dirs

    q_c, v_c, g_c, dirs_c = prep(u_ctx)
    q_l, v_l, g_l, dirs_l = prep(u_lat)
    ident = lambda t: t
    flip = lambda t: jnp.flip(t, 1)
    o_c = None
    o_l = None
    for d in range(2):
        fl = ident if d == 0 else flip
        k_c, lf_c = dirs_c[d]
        k_l, lf_l = dirs_l[d]
        if need_ctx:
            s0 = jnp.zeros((u_ctx.shape[0], HG_HEADS, HG_DK, HG_DV), F32)
            oc, s_c = _hgrn_chunk_scan(fl(q_c), fl(k_c), fl(v_c), fl(lf_c), s0)
            o_c = fl(oc) if o_c is None else o_c + fl(oc)
        else:
            s_c = _hgrn_final_state(fl(k_c), fl(v_c), fl(lf_c))
        ol, _ = _hgrn_chunk_scan(fl(q_l), fl(k_l), fl(v_l), fl(lf_l), s_c)
        o_l = fl(ol) if o_l is None else o_l + fl(ol)

    def out(o, g):
        bb, ll = o.shape[:2]
        o = _rms_norm(o, norm_g).reshape(bb, ll, HG_VWIDTH)
        return (o * jax.nn.silu(g.astype(F32))).astype(g.dtype) @ w_out

    y_ctx = out(o_c, g_c) if need_ctx else None
    return y_ctx, out(o_l, g_l)


def _conv_ffn(u, w_in, conv_w, conv_b, w_out):
    a, gt = jnp.split(u @ w_in, 2, axis=-1)
    l = a.shape[1]
    pad = CONV_WIDTH // 2
    ap = jnp.pad(a, ((0, 0), (pad, pad), (0, 0)))
    conv = conv_b
    for j in range(CONV_WIDTH):
        conv = conv + conv_w[j] * ap[:, j:j + l]
    return (jax.nn.silu(conv) * gt) @ w_out


def setup_inputs(seed: int = 0) -> dict:
    key = jax.random.key(seed)
    keys = iter(jax.random.split(key, 48))

    def nrm(shape, scale):
        return jax.random.normal(next(keys), shape, F32) * scale

    def gain(shape):
        return 1.0 + nrm(shape, 0.02)

    D = D_MODEL
    x = nrm((BATCH, SEQ, D), 1.0)
    c = nrm((BATCH, D), 1.0)
    ctx = nrm((BATCH, CTX_LEN, D), 1.0)
    c_ctx = nrm((D,), 1.0)
    ada_w = nrm((DEPTH, D, 6 * D), D ** -0.5)
    ada_b = nrm((DEPTH, 6 * D), 0.01)
    ln_g = gain((DEPTH, 2, D))
    ln_b = nrm((DEPTH, 2, D), 0.01)
    ffn_w_in = nrm((DEPTH, D, 2 * FFN_HIDDEN), D ** -0.5)
    ffn_conv_w = nrm((DEPTH, CONV_WIDTH, FFN_HIDDEN), CONV_WIDTH ** -0.5)
    ffn_conv_b = nrm((DEPTH, FFN_HIDDEN), 0.01)
    ffn_w_out = nrm((DEPTH, FFN_HIDDEN, D), DN_BETA * FFN_HIDDEN ** -0.5)
    ev_w_in = nrm((N_EVEN, D, EVEN_IN), D ** -0.5)
    mla_q_norm = gain((N_EVEN, MLA_Q_LORA))
    mla_w_uq = nrm((N_EVEN, MLA_Q_LORA, MLA_HEADS * (MLA_NOPE + MLA_ROPE)), MLA_Q_LORA ** -0.5)
    mla_kv_norm = gain((N_EVEN, MLA_KV_LORA))
    mla_w_ukv = nrm((N_EVEN, MLA_KV_LORA, MLA_HEADS * (MLA_NOPE + MLA_V)), MLA_KV_LORA ** -0.5)
    s5_lam_re = -0.5 + nrm((N_EVEN, 2, S5_GROUPS, S5_STATE), 0.01)
    s5_lam_im = math.pi * jnp.arange(S5_STATE, dtype=F32) + nrm((N_EVEN, 2, S5_GROUPS, S5_STATE), 0.01)
    s5_log_dt = jax.random.uniform(next(keys), (N_EVEN, 2, S5_GROUPS), F32,
                                   math.log(S5_DT_MIN), math.log(S5_DT_MAX))
    s5_b_re = nrm((N_EVEN, 2, S5_GROUPS, S5_STATE, S5_GROUP), (2 * S5_GROUP) ** -0.5)
    s5_b_im = nrm((N_EVEN, 2, S5_GROUPS, S5_STATE, S5_GROUP), (2 * S5_GROUP) ** -0.5)
    s5_c_re = nrm((N_EVEN, 2, S5_GROUPS, S5_GROUP, S5_STATE), S5_STATE ** -0.5)
    s5_c_im = nrm((N_EVEN, 2, S5_GROUPS, S5_GROUP, S5_STATE), S5_STATE ** -0.5)
    s5_d = nrm((N_EVEN, S5_WIDTH), 1.0)
    s5_w_glu = nrm((N_EVEN, S5_WIDTH, S5_WIDTH), S5_WIDTH ** -0.5)
    ev_w_out = nrm((N_EVEN, EVEN_MIX, D), DN_BETA * EVEN_MIX ** -0.5)
    hg_w_in = nrm((N_ODD, D, HG_IN), D ** -0.5)
    hg_lb = nrm((DEPTH, 2, HG_WIDTH), 0.1)
    hg_norm = gain((N_ODD, HG_DV))
    hg_w_out = nrm((N_ODD, HG_VWIDTH, D), DN_BETA * HG_VWIDTH ** -0.5)
    return {"x": x, "c": c, "ctx": ctx, "c_ctx": c_ctx, "ada_w": ada_w, "ada_b": ada_b,
            "ln_g": ln_g, "ln_b": ln_b, "ffn_w_in": ffn_w_in, "ffn_conv_w": ffn_conv_w,
            "ffn_conv_b": ffn_conv_b, "ffn_w_out": ffn_w_out, "ev_w_in": ev_w_in,
            "mla_q_norm": mla_q_norm, "mla_w_uq": mla_w_uq, "mla_kv_norm": mla_kv_norm,
            "mla_w_ukv": mla_w_ukv, "s5_lam_re": s5_lam_re, "s5_lam_im": s5_lam_im,
            "s5_log_dt": s5_log_dt, "s5_b_re": s5_b_re, "s5_b_im": s5_b_im,
            "s5_c_re": s5_c_re, "s5_c_im": s5_c_im, "s5_d": s5_d, "s5_w_glu": s5_w_glu,
            "ev_w_out": ev_w_out, "hg_w_in": hg_w_in, "hg_lb": hg_lb, "hg_norm": hg_norm,
            "hg_w_out": hg_w_out}


def reference(x, c, ctx, c_ctx, ada_w, ada_b, ln_g, ln_b, ffn_w_in, ffn_conv_w, ffn_conv_b,
              ffn_w_out, ev_w_in, mla_q_norm, mla_w_uq, mla_kv_norm, mla_w_ukv, s5_lam_re,
              s5_lam_im, s5_log_dt, s5_b_re, s5_b_im, s5_c_re, s5_c_im, s5_d, s5_w_glu,
              ev_w_out, hg_w_in, hg_lb, hg_norm, hg_w_out):
    sm = jax.nn.softmax(hg_lb.astype(F32), axis=0)
    lower_bounds = jnp.cumsum(sm, axis=0) - sm[0]
    s_lat = jax.nn.silu(c)
    s_ctx = jax.nn.silu(c_ctx)
    for layer in range(DEPTH):
        need_ctx = layer < DEPTH - 1
        mod_l = s_lat @ ada_w[layer] + ada_b[layer]
        mod_c = s_ctx @ ada_w[layer] + ada_b[layer]
        sh_m, sc_m, g_m, sh_f, sc_f, g_f = jnp.split(mod_l[:, None, :], 6, axis=-1)
        csh_m, csc_m, cg_m, csh_f, csc_f, cg_f = jnp.split(mod_c, 6, axis=-1)
        u_lat = x * (1.0 + sc_m) + sh_m
        u_ctx = ctx * (1.0 + csc_m) + csh_m
        if layer % 2 == 0:
            e = layer // 2
            y_ctx, y_lat = _even_mixer(u_ctx, u_lat, ev_w_in[e], mla_q_norm[e], mla_w_uq[e],
                                       mla_kv_norm[e], mla_w_ukv[e], s5_lam_re[e], s5_lam_im[e],
                                       s5_log_dt[e], s5_b_re[e], s5_b_im[e], s5_c_re[e],
                                       s5_c_im[e], s5_d[e], s5_w_glu[e], ev_w_out[e], need_ctx)
        else:
            o = layer // 2
            y_ctx, y_lat = _odd_mixer(u_ctx, u_lat, hg_w_in[o], lower_bounds[layer], hg_norm[o],
                                      hg_w_out[o], need_ctx)
        x = _layer_norm(DN_ALPHA * x + g_m * y_lat, ln_g[layer, 0], ln_b[layer, 0])
        f_lat = _conv_ffn(x * (1.0 + sc_f) + sh_f, ffn_w_in[layer], ffn_conv_w[layer],
                          ffn_conv_b[layer], ffn_w_out[layer])
        x = _layer_norm(DN_ALPHA * x + g_f * f_lat, ln_g[layer, 1], ln_b[layer, 1])
        if need_ctx:
            ctx = _layer_norm(DN_ALPHA * ctx + cg_m * y_ctx, ln_g[layer, 0], ln_b[layer, 0])
            f_ctx = _conv_ffn(ctx * (1.0 + csc_f) + csh_f, ffn_w_in[layer], ffn_conv_w[layer],
                              ffn_conv_b[layer], ffn_w_out[layer])
            ctx = _layer_norm(DN_ALPHA * ctx + cg_f * f_ctx, ln_g[layer, 1], ln_b[layer, 1])
    return x
```

```python
import contextlib
import numpy as np
import concourse.bass as bass
import concourse.mybir as mybir
from concourse.bass_utils import run_bass_kernel_spmd
from concourse.ap import AP

F32 = mybir.dt.float32
BF16 = mybir.dt.bfloat16
I32 = mybir.dt.int32
AF = mybir.ActivationFunctionType
ALU = mybir.AluOpType

P = 128
D_MODEL = 1024
NCT = 8
SEQ = 8192
CTX = 256
LTOT = SEQ + CTX
DEPTH = 2
DN_ALPHA = (2.0 * DEPTH) ** 0.25
NORM_EPS = 1e-6
FFN_H = 2816
NHT = FFN_H // P
MLA_SCALE = 96 ** -0.5
QTR = SEQ // 4
SPAN = 512


class Buf:
    __slots__ = ("w", "r", "name")

    def __init__(self, name=""):
        self.w = {}
        self.r = {}
        self.name = name


class Sched:
    NDMA = 24

    def __init__(self, nc, es):
        self.nc = nc
        self.engs = {"pe": nc.tensor, "dve": nc.vector, "act": nc.scalar, "pool": nc.gpsimd, "sp": nc.sync}
        self.sems = {}
        for k in self.engs:
            self.sems[k] = es.enter_context(nc.semaphore("s_" + k))
        self.cnt = {k: 0 for k in self.engs}
        self.seen = {k: {} for k in self.engs}
        for i in range(self.NDMA):
            self.sems[("d", i)] = es.enter_context(nc.semaphore("d%d" % i))
            self.cnt[("d", i)] = 0
        self.rr = 0
        self.out_deps = []

    def _wait(self, e, deps):
        best = {}
        for (k, c) in deps:
            if k == e and e == "pe":
                continue
            if c > best.get(k, 0):
                best[k] = c
        seen = self.seen[e]
        for k, c in best.items():
            if seen.get(k, 0) >= c:
                continue
            self.engs[e].wait_ge(self.sems[k], c)
            seen[k] = c

    def _deps(self, reads, writes):
        deps = []
        for b in reads:
            deps.extend(b.w.items())
        for b in writes:
            deps.extend(b.w.items())
            deps.extend(b.r.items())
        return deps

    def _mark(self, me, reads, writes):
        k, c = me
        for b in reads:
            if b.r.get(k, 0) < c:
                b.r[k] = c
        for b in writes:
            b.w[k] = c
            b.r = {}

    mute = False

    def op(self, e, fn, reads=(), writes=()):
        if self.mute:
            return
        self._wait(e, self._deps(reads, writes))
        ins = fn(self.engs[e])
        self.cnt[e] += 1
        ins.then_inc(self.sems[e], 1)
        self._mark((e, self.cnt[e]), reads, writes)

    def dma(self, out, in_, reads=(), writes=(), q="sp", is_output=False):
        if self.mute:
            return
        i = self.rr
        self.rr = (self.rr + 1) % self.NDMA
        key = ("d", i)
        deps = self._deps(reads, writes)
        if self.cnt[key] > 0:
            deps.append((key, self.cnt[key]))
        self._wait(q, deps)
        ins = self.engs[q].dma_start(out=out, in_=in_)
        self.cnt[key] += 16
        ins.then_inc(self.sems[key], 16)
        me = (key, self.cnt[key])
        self._mark(me, reads, writes)
        if is_output:
            self.out_deps.append(me)

    def barrier(self):
        allc = [(k, c) for k, c in self.cnt.items() if c > 0]
        for e in self.engs:
            self._wait(e, allc)

    def finish(self):
        self._wait("sp", self.out_deps)
        self.out_deps = []


class Ctx:
    def __init__(self, nc):
        self.nc = nc
        self.es = contextlib.ExitStack()
        self.S = Sched(nc, self.es)
        self.n = 0
        self.ps_tiles = []
        self.ps_i = 0

    def sb(self, shape, dt, name=None, es=None):
        self.n += 1
        t = (es or self.es).enter_context(self.nc.sbuf_tensor("%s_%d" % (name or "t", self.n), list(shape), dt))
        return t

    def init_psum(self, es=None):
        self.ps_tiles = []
        for i in range(8):
            t = (es or self.es).enter_context(self.nc.psum_tensor("ps%d_%d" % (i, self.n), [P, 512], F32))
            self.ps_tiles.append((t, Buf("ps%d" % i)))
        self.ps_i = 0
        self.ps_n = len(self.ps_tiles)

    def ps(self):
        t = self.ps_tiles[self.ps_i]
        self.ps_i = (self.ps_i + 1) % self.ps_n
        return t

    def ps_fixed(self, i):
        return self.ps_tiles[i]

    def dram_in(self, name, shape, dt=F32):
        return self.nc.dram_tensor(name, list(shape), dt, kind="ExternalInput").ap()

    def dram_out(self, name, shape, dt=F32):
        return self.nc.dram_tensor(name, list(shape), dt, kind="ExternalOutput").ap()


def bc_last(ap2, n):
    return ap2.to_broadcast([ap2.shape[0], n])


def emit_mods(C, cS_d, adaw_d, adab_d, MODS, MODS_b):
    S = C.S
    nc = C.nc
    with contextlib.ExitStack() as es:
        cS = C.sb([P, 8, 2], F32, "cS", es)
        sil = C.sb([P, 8, 2], F32, "sil", es)
        adab = C.sb([P, 48], F32, "adab", es)
        CH = 768
        wbuf = [C.sb([P, 8, CH], F32, "adaw", es) for _ in range(2)]
        wb = [Buf() for _ in range(2)]
        b_cS, b_sil, b_adab = Buf(), Buf(), Buf()
        pst, psb = C.ps()
        S.dma(cS[:], cS_d[:, :, :], writes=[b_cS])
        S.dma(adab[:], adab_d[:, :], writes=[b_adab])
        S.op("act", lambda e: e.activation(out=sil[:], in_=cS[:], func=AF.Silu), reads=[b_cS], writes=[b_sil])
        wv = adaw_d.rearrange("(kt p) n -> p kt n", p=P)
        nch = 6144 // CH
        for ch in range(nch):
            wt, wbf = wbuf[ch % 2], wb[ch % 2]
            S.dma(wt[:], wv[:, :, ch * CH:(ch + 1) * CH], writes=[wbf])
            for cc in range(CH // P):
                gcc = ch * (CH // P) + cc
                for kt in range(8):
                    S.op("pe", lambda e, kt=kt, cc=cc, gcc=gcc, wt=wt: e.matmul(
                        pst[:, gcc * 2:gcc * 2 + 2], lhsT=wt[:, kt, cc * P:(cc + 1) * P], rhs=sil[:, kt, :],
                        start=(kt == 0), stop=(kt == 7)), reads=[wbf, b_sil], writes=[psb])
        pv = pst[:, 0:96]
        pv3 = AP(pv.tensor, pv.offset, [list(pv.ap[0]), [2, 48], [1, 2]])
        ab = adab[:]
        ab3 = AP(ab.tensor, ab.offset, [list(ab.ap[0]), [1, 48], [0, 2]])
        S.op("dve", lambda e: e.tensor_tensor(out=MODS[:], in0=pv3, in1=ab3, op=ALU.add),
             reads=[psb, b_adab], writes=[MODS_b])
        S.barrier()


class LNRes:
    def __init__(self, C, es):
        self.onesM = C.sb([P, P], F32, "onesM", es)
        self.b_ones = Buf()
        self.tmp = C.sb([P, 2, 512], F32, "lntmp", es)
        self.tmpb = [Buf(), Buf()]
        self.stat = C.sb([P, 2, 3, 512], F32, "lnstat", es)
        self.statb = [[Buf(), Buf(), Buf()], [Buf(), Buf(), Buf()]]
        self.k = 0
        C.S.op("pool", lambda e: e.memset(self.onesM[:], 1.0 / D_MODEL), writes=[self.b_ones])


def emit_ln(C, L, V, Vb, c0, n, lng, lnb, b_lnp, which):
    S = C.S
    assert n <= 512
    mps, mpsb = C.ps()
    qps, qpsb = C.ps()
    par = L.k % 2
    L.k += 1
    onesM, b_ones = L.onesM, L.b_ones
    sl = slice(c0, c0 + n)
    for ct in range(NCT):
        S.op("pe", lambda e, ct=ct: e.matmul(mps[:, :n], lhsT=onesM[:], rhs=V[:, ct, sl], start=(ct == 0), stop=(ct == NCT - 1)),
             reads=[b_ones, Vb[ct]], writes=[mpsb])
    for ct in range(NCT):
        t, tb = L.tmp[:, ct % 2, :n], L.tmpb[ct % 2]
        S.op("act", lambda e, ct=ct, t=t: e.activation(out=t, in_=V[:, ct, sl], func=AF.Square), reads=[Vb[ct]], writes=[tb])
        S.op("pe", lambda e, ct=ct, t=t: e.matmul(qps[:, :n], lhsT=onesM[:], rhs=t, start=(ct == 0), stop=(ct == NCT - 1)),
             reads=[b_ones, tb], writes=[qpsb])
    stat, statb = L.stat, L.statb[par]
    mean, var, m2 = stat[:, par, 0, :n], stat[:, par, 1, :n], stat[:, par, 2, :n]
    S.op("act", lambda e: e.copy(out=mean, in_=mps[:, :n]), reads=[mpsb], writes=[statb[0]])
    S.op("pool", lambda e: e.tensor_tensor(out=m2, in0=mean, in1=mean, op=ALU.mult), reads=[statb[0]], writes=[statb[2]])
    S.op("dve", lambda e: e.tensor_tensor(out=var, in0=qps[:, :n], in1=m2, op=ALU.subtract), reads=[qpsb, statb[2]], writes=[statb[1]])
    S.op("act", lambda e: e.activation(out=var, in_=var, func=AF.Sqrt, bias=NORM_EPS), reads=[statb[1]], writes=[statb[1]])
    S.op("dve", lambda e: e.reciprocal(out=var, in_=var), reads=[statb[1]], writes=[statb[1]])
    for ct in range(NCT):
        S.op("dve", lambda e, ct=ct: e.tensor_tensor(out=V[:, ct, sl], in0=V[:, ct, sl], in1=mean, op=ALU.subtract),
             reads=[Vb[ct], statb[0]], writes=[Vb[ct]])
        S.op("pool", lambda e, ct=ct: e.tensor_tensor(out=V[:, ct, sl], in0=V[:, ct, sl], in1=var, op=ALU.mult),
             reads=[Vb[ct], statb[1]], writes=[Vb[ct]])
        S.op("act", lambda e, ct=ct: e.activation(out=V[:, ct, sl], in_=V[:, ct, sl], func=AF.Identity,
                                                  scale=lng[:, which, ct:ct + 1], bias=lnb[:, which, ct:ct + 1]),
             reads=[Vb[ct], b_lnp], writes=[Vb[ct]])


def load_cast(C, dst_bf, dst_b, src_ap3, shape, es_tmp, eng="pool", nm="stg"):
    S = C.S
    stg = C.sb(shape, F32, nm, es_tmp)
    sb_ = Buf()
    S.dma(stg[:], src_ap3, writes=[sb_])
    S.op(eng, lambda e: e.tensor_copy(out=dst_bf, in_=stg[:]), reads=[sb_], writes=[dst_b])


def chunks_of(n):
    if n <= 512:
        return [(0, n)]
    h = n // 2
    return [(0, h), (h, n - h)]


def emit_tokenlocal(C, cfg):
    S = C.S
    nc = C.nc
    es = contextlib.ExitStack()
    has_glu = cfg.get("wglu_d") is not None
    MODS, b_mods = cfg["MODS"], cfg["b_mods"]
    NW = SPAN + 2
    woutb = C.sb([P, NCT, D_MODEL], BF16, "woutb", es); b_wout = Buf()
    wob = C.sb([P, NHT, D_MODEL], BF16, "wob", es); b_wob = [Buf() for _ in range(NHT // 2)]
    convw = C.sb([P, NHT, 3], F32, "convw", es); convb = C.sb([P, NHT], F32, "convb", es); b_conv = Buf()
    lng = C.sb([P, 2, NCT], F32, "lng", es); lnb = C.sb([P, 2, NCT], F32, "lnb", es); b_lnp = Buf()
    hmask = C.sb([P, 2], F32, "hmask", es); b_hm = Buf()
    M1P = C.sb([P, 48, 2], F32, "M1P", es); b_m1p = Buf()
    if has_glu:
        wglub = C.sb([P, 4, 512], BF16, "wglub", es); b_wglu = Buf()
    S.dma(convw[:], cfg["convw_d"][:, :, :], writes=[b_conv])
    S.dma(convb[:], cfg["convb_d"][:, :], writes=[b_conv])
    S.dma(lng[:], cfg["lng_d"][:, :, :], writes=[b_lnp])
    S.dma(lnb[:], cfg["lnb_d"][:, :, :], writes=[b_lnp])
    S.dma(hmask[:], cfg["hmask_d"][:, :], writes=[b_hm])
    S.op("dve", lambda e: e.tensor_scalar(out=M1P[:], in0=MODS[:], scalar1=1.0, scalar2=None, op0=ALU.add),
         reads=[b_mods], writes=[b_m1p])
    with contextlib.ExitStack() as es_t:
        wv = cfg["wout_d"].rearrange("(kt p) n -> p kt n", p=P)
        for hlf in range(2):
            load_cast(C, woutb[:, hlf * 4:(hlf + 1) * 4, :], b_wout, wv[:, hlf * 4:(hlf + 1) * 4, :], [P, 4, D_MODEL], es_t)
        if has_glu:
            gv = cfg["wglu_d"].rearrange("(kt p) n -> p kt n", p=P)
            load_cast(C, wglub[:], b_wglu, gv[:, :, :], [P, 4, 512], es_t)
        ov = cfg["wo_d"].rearrange("(kt p) n -> p kt n", p=P)
        stg2 = [C.sb([P, 2, D_MODEL], F32, "wostg", es_t) for _ in range(2)]
        stg2b = [Buf(), Buf()]
        for i in range(NHT // 2):
            S.dma(stg2[i % 2][:], ov[:, 2 * i:2 * i + 2, :], writes=[stg2b[i % 2]])
            S.op("pool", lambda e, i=i: e.tensor_copy(out=wob[:, 2 * i:2 * i + 2, :], in_=stg2[i % 2][:]),
                 reads=[stg2b[i % 2]], writes=[b_wob[i]])
        S.barrier()
    A0 = C.sb([P, NCT, NW], F32, "A0", es); A0b = [Buf() for _ in range(NCT)]
    A1 = C.sb([P, NCT, NW], F32, "A1", es); A1b = [Buf() for _ in range(NCT)]
    B0 = C.sb([P, NCT, NW], BF16, "B0", es); B0b = [Buf() for _ in range(NCT)]
    H = C.sb([P, NHT, SPAN], BF16, "H", es); Hb = [Buf() for _ in range(NHT)]
    L = LNRes(C, es)
    if has_glu:
        S5O = C.sb([P, 4, NW], BF16, "S5O", es); S5Ob = [Buf() for _ in range(4)]
        sig = C.sb([P, 2, 512], F32, "sig", es); sigb = [Buf(), Buf()]
    wst = [C.sb([P, NCT, 256], F32, "wst", es) for _ in range(2)]; wstb = [[Buf(), Buf()], [Buf(), Buf()]]
    w16 = [C.sb([P, NCT, 256], BF16, "w16", es) for _ in range(2)]; w16b = [Buf(), Buf()]
    a_sb = C.sb([P, 2, NW + 2], F32, "a_sb", es); a_b = [Buf(), Buf()]
    c_sb = C.sb([P, 2, SPAN], F32, "c_sb", es); c_b = [Buf(), Buf()]
    winv = cfg["win_d"]
    mixv = cfg["mix_d"].rearrange("(ct p) n -> p ct n", p=P)
    xrv = cfg["xres_d"].rearrange("(ct p) n -> p ct n", p=P)
    xov = cfg["xo_d"].rearrange("(ct p) n -> p ct n", p=P)
    step = 0
    for sp in cfg["spans"]:
        c0, n, nin, lh, var = sp["c0"], sp["n"], sp["nin"], sp["lh"], sp["var"]
        halo = lh == 1
        ch_all = chunks_of(n)
        ch_in = chunks_of(nin)
        vs = slice(var, var + 1)
        S.dma(A0[:, :, 0:n], xrv[:, :, c0:c0 + n], writes=A0b)
        S.dma(A1[:, :, 0:n], mixv[:, :, c0:c0 + n], writes=A1b)
        for ct in range(NCT):
            S.op("pool", lambda e, ct=ct: e.tensor_copy(out=B0[:, ct, 0:n], in_=A1[:, ct, 0:n]), reads=[A1b[ct]], writes=[B0b[ct]])
        if has_glu:
            k = 0
            for ot in range(4):
                for (q0, qn) in ch_all:
                    pt, pb = C.ps()
                    for kt in range(4):
                        S.op("pe", lambda e, kt=kt, ot=ot, q0=q0, qn=qn, pt=pt: e.matmul(
                            pt[:, :qn], lhsT=wglub[:, kt, ot * P:(ot + 1) * P], rhs=B0[:, 4 + kt, q0:q0 + qn],
                            start=(kt == 0), stop=(kt == 3)), reads=[b_wglu, B0b[4 + kt]], writes=[pb])
                    sg, sgb = sig[:, k % 2, :qn], sigb[k % 2]
                    k += 1
                    S.op("act", lambda e, sg=sg, pt=pt, qn=qn: e.activation(out=sg, in_=pt[:, :qn], func=AF.Sigmoid), reads=[pb], writes=[sgb])
                    S.op("dve", lambda e, sg=sg, ot=ot, q0=q0, qn=qn: e.tensor_tensor(
                        out=S5O[:, ot, q0:q0 + qn], in0=A1[:, 4 + ot, q0:q0 + qn], in1=sg, op=ALU.mult),
                        reads=[sgb, A1b[4 + ot]], writes=[S5Ob[ot]])
            opnd = [(B0, kt, B0b[kt]) for kt in range(4)] + [(S5O, kt, S5Ob[kt]) for kt in range(4)]
        else:
            opnd = [(B0, kt, B0b[kt]) for kt in range(NCT)]
        for ct in range(NCT):
            S.op("act", lambda e, ct=ct: e.mul(out=A0[:, ct, 0:n], in_=A0[:, ct, 0:n], mul=DN_ALPHA), reads=[A0b[ct]], writes=[A0b[ct]])
        for (q0, qn) in ch_all:
            for ot in range(NCT):
                pt, pb = C.ps()
                for kt in range(NCT):
                    T, ti, tb = opnd[kt]
                    S.op("pe", lambda e, kt=kt, ot=ot, T=T, ti=ti, pt=pt, q0=q0, qn=qn: e.matmul(
                        pt[:, :qn], lhsT=woutb[:, kt, ot * P:(ot + 1) * P], rhs=T[:, ti, q0:q0 + qn],
                        start=(kt == 0), stop=(kt == NCT - 1)), reads=[b_wout, tb], writes=[pb])
                S.op("dve", lambda e, ot=ot, pt=pt, q0=q0, qn=qn: e.scalar_tensor_tensor(
                    out=A0[:, ot, q0:q0 + qn], in0=pt[:, :qn], scalar=MODS[:, 16 + ot, vs], in1=A0[:, ot, q0:q0 + qn],
                    op0=ALU.mult, op1=ALU.add), reads=[pb, b_mods, A0b[ot]], writes=[A0b[ot]])
            emit_ln(C, L, A0, A0b, q0, qn, lng, lnb, b_lnp, 0)
        for ct in range(NCT):
            S.op("act", lambda e, ct=ct: e.activation(out=B0[:, ct, 0:n], in_=A0[:, ct, 0:n], func=AF.Identity,
                                                      scale=M1P[:, 32 + ct, vs], bias=MODS[:, 24 + ct, vs]),
                 reads=[A0b[ct], b_m1p, b_mods], writes=[B0b[ct]])
        off = 0 if halo else 1
        for ht in range(NHT):
            par = step % 2
            step += 1
            S.dma(wst[par][:], winv[ht, :, :, :], writes=wstb[par])
            S.op("pool", lambda e, par=par: e.tensor_copy(out=w16[par][:], in_=wst[par][:]), reads=wstb[par], writes=[w16b[par]])
            av = a_sb[:, par, :]
            for (q0, qn) in ch_all:
                pt, pb = C.ps()
                for kt in range(NCT):
                    S.op("pe", lambda e, kt=kt, par=par, pt=pt, q0=q0, qn=qn: e.matmul(
                        pt[:, :qn], lhsT=w16[par][:, kt, 0:P], rhs=B0[:, kt, q0:q0 + qn],
                        start=(kt == 0), stop=(kt == NCT - 1)), reads=[w16b[par], B0b[kt]], writes=[pb])
                S.op("act", lambda e, par=par, pt=pt, q0=q0, qn=qn: e.copy(out=a_sb[:, par, off + q0:off + q0 + qn], in_=pt[:, :qn]),
                     reads=[pb], writes=[a_b[par]])
            if not halo:
                S.op("pool", lambda e, par=par: e.memset(a_sb[:, par, 0:1], 0.0), writes=[a_b[par]])
                S.op("pool", lambda e, par=par: e.memset(a_sb[:, par, nin + 1:nin + 2], 0.0), writes=[a_b[par]])
            if sp.get("maskL"):
                S.op("dve", lambda e, par=par: e.tensor_scalar(out=a_sb[:, par, 0:1], in0=a_sb[:, par, 0:1], scalar1=hmask[:, 0:1],
                                                               scalar2=None, op0=ALU.mult), reads=[a_b[par], b_hm], writes=[a_b[par]])
            if sp.get("maskR"):
                S.op("dve", lambda e, par=par: e.tensor_scalar(out=a_sb[:, par, nin + 1:nin + 2], in0=a_sb[:, par, nin + 1:nin + 2],
                                                               scalar1=hmask[:, 1:2], scalar2=None, op0=ALU.mult),
                     reads=[a_b[par], b_hm], writes=[a_b[par]])
            cv = c_sb[:, par, 0:nin]
            S.op("dve", lambda e, par=par, ht=ht, cv=cv: e.tensor_scalar(
                out=cv, in0=a_sb[:, par, 1:nin + 1], scalar1=convw[:, ht, 1:2], scalar2=convb[:, ht:ht + 1],
                op0=ALU.mult, op1=ALU.add), reads=[a_b[par], b_conv], writes=[c_b[par]])
            S.op("dve", lambda e, par=par, ht=ht, cv=cv: e.scalar_tensor_tensor(
                out=cv, in0=a_sb[:, par, 0:nin], scalar=convw[:, ht, 0:1], in1=cv, op0=ALU.mult, op1=ALU.add),
                reads=[a_b[par], b_conv, c_b[par]], writes=[c_b[par]])
            S.op("dve", lambda e, par=par, ht=ht, cv=cv: e.scalar_tensor_tensor(
                out=cv, in0=a_sb[:, par, 2:nin + 2], scalar=convw[:, ht, 2:3], in1=cv, op0=ALU.mult, op1=ALU.add),
                reads=[a_b[par], b_conv, c_b[par]], writes=[c_b[par]])
            S.op("act", lambda e, cv=cv: e.activation(out=cv, in_=cv, func=AF.Silu), reads=[c_b[par]], writes=[c_b[par]])
            for (q0, qn) in ch_in:
                pt, pb = C.ps()
                for kt in range(NCT):
                    S.op("pe", lambda e, kt=kt, par=par, pt=pt, q0=q0, qn=qn: e.matmul(
                        pt[:, :qn], lhsT=w16[par][:, kt, P:2 * P], rhs=B0[:, kt, lh + q0:lh + q0 + qn],
                        start=(kt == 0), stop=(kt == NCT - 1)), reads=[w16b[par], B0b[kt]], writes=[pb])
                S.op("dve", lambda e, par=par, ht=ht, pt=pt, q0=q0, qn=qn: e.tensor_tensor(
                    out=H[:, ht, q0:q0 + qn], in0=c_sb[:, par, q0:q0 + qn], in1=pt[:, :qn], op=ALU.mult),
                    reads=[c_b[par], pb], writes=[Hb[ht]])
        for (q0, qn) in ch_in:
            for ot in range(NCT):
                S.op("act", lambda e, ot=ot, q0=q0, qn=qn: e.mul(out=A1[:, ot, q0:q0 + qn], in_=A0[:, ot, lh + q0:lh + q0 + qn], mul=DN_ALPHA),
                     reads=[A0b[ot]], writes=[A1b[ot]])
                pt, pb = C.ps()
                for ht in range(NHT):
                    S.op("pe", lambda e, ht=ht, ot=ot, pt=pt, q0=q0, qn=qn: e.matmul(
                        pt[:, :qn], lhsT=wob[:, ht, ot * P:(ot + 1) * P], rhs=H[:, ht, q0:q0 + qn],
                        start=(ht == 0), stop=(ht == NHT - 1)), reads=[b_wob[ht // 2], Hb[ht]], writes=[pb])
                S.op("dve", lambda e, ot=ot, pt=pt, q0=q0, qn=qn: e.scalar_tensor_tensor(
                    out=A1[:, ot, q0:q0 + qn], in0=pt[:, :qn], scalar=MODS[:, 40 + ot, vs], in1=A1[:, ot, q0:q0 + qn],
                    op0=ALU.mult, op1=ALU.add), reads=[pb, b_mods, A1b[ot]], writes=[A1b[ot]])
            emit_ln(C, L, A1, A1b, q0, qn, lng, lnb, b_lnp, 1)
        S.dma(xov[:, :, sp["o0"]:sp["o0"] + nin], A1[:, :, 0:nin], reads=A1b, is_output=True)
    return es


def lay_vec_cols(v, ncol):
    return np.ascontiguousarray(np.asarray(v, np.float32).reshape(ncol, P).T)


def lay_ln(ln_g_l):
    return np.ascontiguousarray(np.asarray(ln_g_l, np.float32).reshape(2, NCT, P).transpose(2, 0, 1))


def lay_conv(conv_w_l):
    return np.ascontiguousarray(np.asarray(conv_w_l, np.float32).reshape(3, NHT, P).transpose(2, 1, 0))


def lay_mods(mod_l_b, mod_c):
    return np.ascontiguousarray(np.stack([lay_vec_cols(mod_l_b, 48), lay_vec_cols(mod_c, 48)], axis=-1))


def lay_ffn_win(w_in_l):
    w = np.asarray(w_in_l, np.float32)
    a = w[:, :FFN_H].reshape(NCT, P, NHT, P)
    g = w[:, FFN_H:].reshape(NCT, P, NHT, P)
    ag = np.concatenate([a, g], axis=3)
    return np.ascontiguousarray(ag.transpose(2, 1, 0, 3))


def lat_slices(fullT, j):
    pad = np.pad(fullT, ((0, 0), (1, 1)))
    return np.ascontiguousarray(pad[:, QTR * j:QTR * j + QTR + 2])


def hmask_for(j):
    m = np.ones((P, 2), np.float32)
    if j == 0:
        m[:, 0] = 0.0
    if j == 3:
        m[:, 1] = 0.0
    return m


def lat_spans(base, obase):
    sp = []
    for i in range(QTR // SPAN):
        sp.append(dict(c0=base + SPAN * i, n=SPAN + 2, nin=SPAN, lh=1, var=0, maskL=(i == 0), maskR=(i == QTR // SPAN - 1),
                       o0=obase + SPAN * i))
    return sp


def build_phase4():
    nc = bass.Bass("TRN2", target_bir_lowering=False)
    C = Ctx(nc)
    C.init_psum()
    S = C.S
    d = {}
    d["mix_d"] = C.dram_in("mix", [D_MODEL, QTR + 2])
    d["xres_d"] = C.dram_in("xres", [D_MODEL, QTR + 2])
    d["xo_d"] = C.dram_out("xo", [D_MODEL, QTR])
    d["wout_d"] = C.dram_in("wout", [D_MODEL, D_MODEL])
    d["wglu_d"] = None
    d["lng_d"] = C.dram_in("lng", [P, 2, NCT])
    d["lnb_d"] = C.dram_in("lnb", [P, 2, NCT])
    d["win_d"] = C.dram_in("win", [NHT, P, NCT, 256])
    d["convw_d"] = C.dram_in("convw", [P, NHT, 3])
    d["convb_d"] = C.dram_in("convb", [P, NHT])
    d["wo_d"] = C.dram_in("wo", [FFN_H, D_MODEL])
    d["hmask_d"] = C.dram_in("hmask", [P, 2])
    mods_d = C.dram_in("mods", [P, 48, 2])
    MODS = C.sb([P, 48, 2], F32, "MODS")
    b_mods = Buf()
    S.dma(MODS[:], mods_d[:, :, :], writes=[b_mods])
    d["MODS"], d["b_mods"] = MODS, b_mods
    d["spans"] = lat_spans(0, 0)
    es = emit_tokenlocal(C, d)
    S.finish()
    es.close()
    C.es.close()
    return nc


def phase4_inputs(b, j, ogT_b, x2T_b, mods1_b, inp):
    return {
        "mix": lat_slices(ogT_b, j), "xres": lat_slices(x2T_b, j),
        "wout": np.ascontiguousarray(inp["hg_w_out"][0]), "lng": lay_ln(inp["ln_g"][1]), "lnb": lay_ln(inp["ln_b"][1]),
        "win": lay_ffn_win(inp["ffn_w_in"][1]), "convw": lay_conv(inp["ffn_conv_w"][1]),
        "convb": lay_vec_cols(inp["ffn_conv_b"][1], NHT), "wo": np.ascontiguousarray(inp["ffn_w_out"][1]),
        "hmask": hmask_for(j), "mods": mods1_b,
    }


HG_CH = 64
BLK = 512


def ap_chunk_last(t2, n, bcast):
    nchk = n // HG_CH
    if bcast:
        return AP(t2.tensor, t2.offset + HG_CH - 1, [list(t2.ap[0]), [HG_CH, nchk], [0, HG_CH]])
    return AP(t2.tensor, t2.offset + HG_CH - 1, [list(t2.ap[0]), [HG_CH, nchk]])


def ap_3d(t2, n):
    return AP(t2.tensor, t2.offset, [list(t2.ap[0]), [HG_CH, n // HG_CH], [1, HG_CH]])


def build_phase3():
    nc = bass.Bass("TRN2", target_bir_lowering=False)
    C = Ctx(nc)
    C.init_psum()
    S = C.S
    es = C.es
    x_d = C.dram_in("xT", [D_MODEL, LTOT])
    w_d = C.dram_in("w3", [D_MODEL, 1280])
    mods_d = C.dram_in("mods", [P, 48, 2])
    lb_d = C.dram_in("hglb", [P, 2, 4])
    ng_d = C.dram_in("hgnorm", [P, 1])
    cst_d = C.dram_in("cst3", [P, 4, 128])
    rm_d = C.dram_in("rmask", [P, BLK])
    og_d = C.dram_out("ogT", [2 * P, SEQ])

    MODS = C.sb([P, 48, 2], F32, "MODS"); b_mods = Buf()
    M1P = C.sb([P, 48, 2], F32, "M1P"); b_m1p = Buf()
    S.dma(MODS[:], mods_d[:, :, :], writes=[b_mods])
    S.op("dve", lambda e: e.tensor_scalar(out=M1P[:], in0=MODS[:], scalar1=1.0, scalar2=None, op0=ALU.add), reads=[b_mods], writes=[b_m1p])
    cst = C.sb([P, 4, 128], F32, "cst"); b_cst = Buf()
    S.dma(cst[:], cst_d[:, :, :], writes=[b_cst])
    identB = C.sb([P, P], BF16, "identB"); b_id = Buf()
    S.op("dve", lambda e: e.tensor_copy(out=identB[:], in_=cst[:, 0, :]), reads=[b_cst], writes=[b_id])
    onesM = C.sb([P, P], F32, "ones3"); b_ones = Buf()
    S.op("pool", lambda e: e.memset(onesM[:], 1.0 / P), writes=[b_ones])
    rmask = C.sb([P, BLK], F32, "rmask"); b_rm = Buf()
    S.dma(rmask[:], rm_d[:, :], writes=[b_rm])
    ng = C.sb([P, 1], F32, "ng"); b_ng = Buf()
    S.dma(ng[:], ng_d[:, :], writes=[b_ng])
    lbr = C.sb([P, 2, 4], F32, "lbr"); b_lbr = Buf()
    S.dma(lbr[:], lb_d[:, :, :], writes=[b_lbr])
    LB = C.sb([P, 4], F32, "LB"); OML = C.sb([P, 4], F32, "OML"); b_lb = Buf()
    den = C.sb([P, 4], F32, "den"); b_den = Buf()
    S.op("act", lambda e: e.activation(out=lbr[:], in_=lbr[:], func=AF.Exp), reads=[b_lbr], writes=[b_lbr])
    S.op("dve", lambda e: e.tensor_tensor(out=den[:], in0=lbr[:, 0, :], in1=lbr[:, 1, :], op=ALU.add), reads=[b_lbr], writes=[b_den])
    S.op("dve", lambda e: e.reciprocal(out=den[:], in_=den[:]), reads=[b_den], writes=[b_den])
    S.op("dve", lambda e: e.tensor_tensor(out=LB[:], in0=lbr[:, 1, :], in1=den[:], op=ALU.mult), reads=[b_lbr, b_den], writes=[b_lb])
    S.op("dve", lambda e: e.tensor_scalar(out=OML[:], in0=LB[:], scalar1=-1.0, scalar2=1.0, op0=ALU.mult, op1=ALU.add), reads=[b_lb], writes=[b_lb])
    Wb = C.sb([P, NCT, 1280], BF16, "W3b"); b_w = Buf()
    with contextlib.ExitStack() as es_t:
        wv = w_d.rearrange("(kt p) n -> p kt n", p=P)
        for q in range(4):
            load_cast(C, Wb[:, 2 * q:2 * q + 2, :], b_w, wv[:, 2 * q:2 * q + 2, :], [P, 2, 1280], es_t)
        S.barrier()
    OF = C.sb([P, 2, SEQ], F32, "OF"); OFb = [[Buf() for _ in range(SEQ // BLK)] for _ in range(2)]
    xs = C.sb([P, NCT, BLK], F32, "xs"); xsb = Buf()
    ub = [C.sb([P, NCT, BLK], BF16, "ub") for _ in range(2)]; ubb = [Buf(), Buf()]
    NT_ = 8
    T = [[C.sb([P, BLK], F32, "T%d" % i) for i in range(NT_)] for _ in range(2)]
    Tb = [[Buf() for _ in range(NT_)] for _ in range(2)]
    QE = [C.sb([P, BLK], BF16, "QE") for _ in range(2)]; KE = [C.sb([P, BLK], BF16, "KE") for _ in range(2)]
    KD = [C.sb([P, BLK], BF16, "KD") for _ in range(2)]
    QEb = [Buf(), Buf()]; KEb = [Buf(), Buf()]; KDb = [Buf(), Buf()]
    VT = [C.sb([P, BLK], BF16, "VT") for _ in range(2)]; VTb = [Buf(), Buf()]
    KDT = [C.sb([P, BLK], BF16, "KDT") for _ in range(2)]; KDTb = [Buf(), Buf()]
    SCT = [C.sb([P, 2, P], BF16, "SCT") for _ in range(2)]; SCTb = [[Buf(), Buf()], [Buf(), Buf()]]
    EL = [C.sb([P, 8], F32, "EL") for _ in range(2)]; ELb = [Buf(), Buf()]
    Sf = [C.sb([P, P], F32, "Sf") for _ in range(2)]; Sfb = [Buf(), Buf()]
    Sb = [C.sb([P, P], BF16, "Sb") for _ in range(2)]; Sbb = [Buf(), Buf()]
    xv = x_d.rearrange("(ct p) n -> p ct n", p=P)
    nblk = SEQ // BLK
    bi = 0
    C.ps_n = 6
    C.ps_i = 0
    for d in range(2):
        for hh in range(2):
            S.op("dve", lambda e, hh=hh: e.memset(Sf[hh][:], 0.0), writes=[Sfb[hh]])
            S.op("dve", lambda e, hh=hh: e.memset(Sb[hh][:], 0.0), writes=[Sbb[hh]])
        blocks = [(0, CTX, 1, None)]
        order = range(nblk) if d == 0 else range(nblk - 1, -1, -1)
        blocks += [(CTX + BLK * i, BLK, 0, i) for i in order]
        mask = cst[:, 1 + d, :]
        for (c0, n, var, li) in blocks:
            u = ub[bi % 2]; u_b = ubb[bi % 2]
            bi += 1
            vs = slice(var, var + 1)
            ntile = n // P
            S.dma(xs[:, :, 0:n], xv[:, :, c0:c0 + n], writes=[xsb])
            for ct in range(NCT):
                S.op("act", lambda e, ct=ct, u=u: e.activation(out=u[:, ct, 0:n], in_=xs[:, ct, 0:n], func=AF.Identity,
                                                              scale=M1P[:, 8 + ct, vs], bias=MODS[:, ct, vs]),
                     reads=[xsb, b_m1p, b_mods], writes=[u_b])
            hst = {}
            for hh in range(2):
                Th, Tbh = T[hh], Tb[hh]
                wc = hh * 640
                qps, qpb = C.ps()
                fps, fpb = C.ps()
                vps, vpb = C.ps()
                for kt in range(NCT):
                    S.op("pe", lambda e, kt=kt: e.matmul(qps[:, :n], lhsT=Wb[:, kt, wc:wc + P], rhs=u[:, kt, 0:n],
                                                         start=(kt == 0), stop=(kt == NCT - 1)), reads=[b_w, u_b], writes=[qpb])
                fo = wc + P * (1 + d)
                for kt in range(NCT):
                    S.op("pe", lambda e, kt=kt: e.matmul(fps[:, :n], lhsT=Wb[:, kt, fo:fo + P], rhs=u[:, kt, 0:n],
                                                         start=(kt == 0), stop=(kt == NCT - 1)), reads=[b_w, u_b], writes=[fpb])
                for tt in range(ntile):
                    for kt in range(NCT):
                        S.op("pe", lambda e, kt=kt, tt=tt: e.matmul(vps[:, tt * P:(tt + 1) * P], lhsT=u[:, kt, tt * P:(tt + 1) * P],
                                                                    rhs=Wb[:, kt, wc + 3 * P:wc + 4 * P],
                                                                    start=(kt == 0), stop=(kt == NCT - 1)), reads=[b_w, u_b], writes=[vpb])
                need_o = li is not None
                if need_o and d == 1:
                    gps, gpb = C.ps()
                    for kt in range(NCT):
                        S.op("pe", lambda e, kt=kt: e.matmul(gps[:, :n], lhsT=Wb[:, kt, wc + 4 * P:wc + 5 * P], rhs=u[:, kt, 0:n],
                                                             start=(kt == 0), stop=(kt == NCT - 1)), reads=[b_w, u_b], writes=[gpb])
                f_, lf, kk, cum, E, X6, X7, G = [Th[i][:, 0:n] for i in range(8)]
                bf, blf, bk, bcum, bE, b6, b7, bG = Tbh
                li4 = d * 2 + hh
                S.op("act", lambda e: e.activation(out=f_, in_=fps[:, :n], func=AF.Sigmoid), reads=[fpb], writes=[bf])
                S.op("dve", lambda e: e.tensor_scalar(out=f_, in0=f_, scalar1=OML[:, li4:li4 + 1], scalar2=LB[:, li4:li4 + 1],
                                                      op0=ALU.mult, op1=ALU.add), reads=[bf, b_lb], writes=[bf])
                S.op("act", lambda e: e.activation(out=lf, in_=f_, func=AF.Ln), reads=[bf], writes=[blf])
                S.op("dve", lambda e: e.tensor_scalar(out=kk, in0=f_, scalar1=-1.0, scalar2=1.0, op0=ALU.mult, op1=ALU.add),
                     reads=[bf], writes=[bk])
                S.op("dve", lambda e: e.tensor_tensor_scan(out=cum, data0=rmask[:, 0:n], data1=lf, initial=0.0, op0=ALU.mult, op1=ALU.add),
                     reads=[b_rm, blf], writes=[bcum])
                last_b = ap_chunk_last(cum, n, True)
                last_s = ap_chunk_last(cum, n, False)
                nchk = n // HG_CH
                S.op("act", lambda e: e.activation(out=EL[hh][:, 0:nchk], in_=last_s, func=AF.Exp), reads=[bcum], writes=[ELb[hh]])
                if d == 0:
                    cq = cum
                    bcq = bcum
                else:
                    S.op("dve", lambda e: e.tensor_tensor(out=ap_3d(X6, n), in0=last_b, in1=ap_3d(lf, n), op=ALU.add),
                         reads=[bcum, blf], writes=[b6])
                    S.op("dve", lambda e: e.tensor_tensor(out=X6, in0=X6, in1=cum, op=ALU.subtract), reads=[b6, bcum], writes=[b6])
                    cq = X6
                    bcq = b6
                if need_o:
                    S.op("act", lambda e: e.activation(out=E, in_=cq, func=AF.Exp), reads=[bcq], writes=[bE])
                    S.op("dve", lambda e: e.tensor_tensor(out=QE[hh][:, 0:n], in0=qps[:, :n], in1=E, op=ALU.mult),
                         reads=[qpb, bE], writes=[QEb[hh]])
                    S.op("act", lambda e: e.activation(out=E, in_=cq, func=AF.Exp, scale=-1.0), reads=[bcq, QEb[hh]], writes=[bE])
                    S.op("dve", lambda e: e.tensor_tensor(out=KE[hh][:, 0:n], in0=kk, in1=E, op=ALU.mult),
                         reads=[bk, bE], writes=[KEb[hh]])
                if d == 0:
                    S.op("dve", lambda e: e.tensor_tensor(out=ap_3d(X7, n), in0=last_b, in1=ap_3d(cum, n), op=ALU.subtract),
                         reads=[bcum], writes=[b7])
                else:
                    S.op("dve", lambda e: e.tensor_tensor(out=X7, in0=cum, in1=lf, op=ALU.subtract), reads=[bcum, blf], writes=[b7])
                S.op("act", lambda e: e.activation(out=X7, in_=X7, func=AF.Exp), reads=[b7], writes=[b7])
                S.op("dve", lambda e: e.tensor_tensor(out=KD[hh][:, 0:n], in0=kk, in1=X7, op=ALU.mult), reads=[bk, b7], writes=[KDb[hh]])
                S.op("act", lambda e: e.copy(out=VT[hh][:, 0:n], in_=vps[:, 0:n]), reads=[vpb], writes=[VTb[hh]])
                tps, tpb = C.ps()
                for tt in range(ntile):
                    S.op("pe", lambda e, tt=tt: e.matmul(tps[:, tt * P:(tt + 1) * P], lhsT=KD[hh][:, tt * P:(tt + 1) * P], rhs=identB[:],
                                                         start=True, stop=True), reads=[KDb[hh], b_id], writes=[tpb])
                S.op("dve", lambda e: e.tensor_copy(out=KDT[hh][:, 0:n], in_=tps[:, 0:n]), reads=[tpb], writes=[KDTb[hh]])
                if need_o and d == 1:
                    S.op("act", lambda e: e.activation(out=G, in_=gps[:, :n], func=AF.Silu), reads=[gpb], writes=[bG])
                hst[hh] = dict(X6=X6, X7=X7, G=G, b6=b6, b7=b7, bG=bG)
            need_o = li is not None
            tiles = range(ntile) if d == 0 else range(ntile - 1, -1, -1)
            chs = (0, 1) if d == 0 else (1, 0)
            for tt in tiles:
                cs = tt * P
                tl = {}
                if need_o:
                    for hh in range(2):
                        sps, spb = C.ps()
                        ops_, opb = C.ps_fixed(6 + hh)
                        sct, sctb = SCT[hh][:, tt % 2, :], SCTb[hh][tt % 2]
                        S.op("pe", lambda e: e.matmul(sps[:, 0:P], lhsT=KE[hh][:, cs:cs + P], rhs=QE[hh][:, cs:cs + P],
                                                      start=True, stop=True), reads=[KEb[hh], QEb[hh]], writes=[spb])
                        S.op("dve", lambda e: e.tensor_tensor(out=sct, in0=sps[:, 0:P], in1=mask, op=ALU.mult),
                             reads=[spb, b_cst], writes=[sctb])
                        tl[hh] = (ops_, opb, sct, sctb)
                    for hh in range(2):
                        ops_, opb, sct, sctb = tl[hh]
                        S.op("pe", lambda e: e.matmul(ops_[:, 0:P], lhsT=VT[hh][:, tt * P:(tt + 1) * P], rhs=sct, start=True, stop=False),
                             reads=[VTb[hh], sctb], writes=[opb])
                for ci, c in enumerate(chs):
                    col = cs + c * HG_CH
                    gch = col // HG_CH
                    kl = {}
                    for hh in range(2):
                        if need_o:
                            ops_, opb, sct, sctb = tl[hh]
                            S.op("pe", lambda e: e.matmul(ops_[:, c * HG_CH:(c + 1) * HG_CH], lhsT=Sb[hh][:],
                                                          rhs=QE[hh][:, col:col + HG_CH], start=False, stop=(ci == 1)),
                                 reads=[Sbb[hh], QEb[hh]], writes=[opb])
                        kps, kpb = C.ps()
                        S.op("pe", lambda e: e.matmul(kps[:, 0:P], lhsT=KDT[hh][c * HG_CH:(c + 1) * HG_CH, tt * P:(tt + 1) * P],
                                                      rhs=VT[hh][c * HG_CH:(c + 1) * HG_CH, tt * P:(tt + 1) * P], start=True, stop=True),
                             reads=[KDTb[hh], VTb[hh]], writes=[kpb])
                        kl[hh] = (kps, kpb)
                    for hh in range(2):
                        kps, kpb = kl[hh]
                        S.op("dve", lambda e: e.scalar_tensor_tensor(out=Sf[hh][:], in0=Sf[hh][:], scalar=EL[hh][:, gch:gch + 1],
                                                                      in1=kps[:, 0:P], op0=ALU.mult, op1=ALU.add),
                             reads=[Sfb[hh], ELb[hh], kpb], writes=[Sfb[hh]])
                        S.op("act", lambda e: e.copy(out=Sb[hh][:], in_=Sf[hh][:]), reads=[Sfb[hh]], writes=[Sbb[hh]])
                if need_o:
                    oc = li * BLK + cs
                    for hh in range(2):
                        ops_, opb, sct, sctb = tl[hh]
                        if d == 0:
                            S.op("act", lambda e: e.copy(out=OF[:, hh, oc:oc + P], in_=ops_[:, 0:P]), reads=[opb], writes=[OFb[hh][li]])
                        else:
                            S.op("dve", lambda e: e.tensor_tensor(out=OF[:, hh, oc:oc + P], in0=ops_[:, 0:P], in1=OF[:, hh, oc:oc + P],
                                                                  op=ALU.add), reads=[opb, OFb[hh][li]], writes=[OFb[hh][li]])
            if need_o and d == 1:
                for hh in range(2):
                    X6, X7, G, b6, b7, bG = [hst[hh][k_] for k_ in ("X6", "X7", "G", "b6", "b7", "bG")]
                    o_ = OF[:, hh, li * BLK:(li + 1) * BLK]
                    ob = OFb[hh][li]
                    mps, mpb = C.ps()
                    S.op("act", lambda e: e.activation(out=X6, in_=o_, func=AF.Square), reads=[ob], writes=[b6])
                    S.op("pe", lambda e: e.matmul(mps[:, :n], lhsT=onesM[:], rhs=X6, start=True, stop=True), reads=[b_ones, b6], writes=[mpb])
                    S.op("act", lambda e: e.activation(out=X7, in_=mps[:, :n], func=AF.Sqrt, bias=NORM_EPS), reads=[mpb], writes=[b7])
                    S.op("dve", lambda e: e.reciprocal(out=X7, in_=X7), reads=[b7], writes=[b7])
                    S.op("dve", lambda e: e.tensor_tensor(out=X6, in0=o_, in1=X7, op=ALU.mult), reads=[ob, b7], writes=[b6])
                    S.op("dve", lambda e: e.scalar_tensor_tensor(out=X6, in0=X6, scalar=ng[:, 0:1], in1=G, op0=ALU.mult, op1=ALU.mult),
                         reads=[b6, b_ng, bG], writes=[b6])
                    S.dma(og_d[hh * P:(hh + 1) * P, li * BLK:(li + 1) * BLK], X6, reads=[b6], q="pool", is_output=True)
    S._wait("sp", [])
    S._wait("pool", S.out_deps)
    S.finish()
    C.es.close()
    return nc


def tri_masks():
    m = np.zeros((P, 4, P), np.float32)
    m[:, 0, :] = np.eye(P, dtype=np.float32)
    s = np.arange(P)[:, None]
    t = np.arange(P)[None, :]
    same = (s // HG_CH) == (t // HG_CH)
    m[:, 1, :] = (same & (s <= t)).astype(np.float32)
    m[:, 2, :] = (same & (s >= t)).astype(np.float32)
    return m


def reset_mask():
    r = np.ones((P, BLK), np.float32)
    r[:, ::HG_CH] = 0.0
    return r


def phase3_inputs(b, j, x2T_full_b, mods1_b, inp):
    hg_w = np.asarray(inp["hg_w_in"][0])
    cols = []
    for hh in range(2):
        h = 2 * j + hh
        for part in range(5):
            cols.append(hg_w[:, part * 1024 + h * P: part * 1024 + (h + 1) * P])
    w3 = np.ascontiguousarray(np.concatenate(cols, axis=1))
    lb = np.asarray(inp["hg_lb"])
    hglb = np.zeros((P, 2, 4), np.float32)
    for d in range(2):
        for hh in range(2):
            h = 2 * j + hh
            hglb[:, :, d * 2 + hh] = lb[:, d, h * P:(h + 1) * P].T
    return {"xT": np.ascontiguousarray(x2T_full_b), "w3": w3, "mods": mods1_b, "hglb": hglb,
            "hgnorm": np.ascontiguousarray(np.asarray(inp["hg_norm"][0]).reshape(P, 1)), "cst3": tri_masks(), "rmask": reset_mask()}


NT2 = CTX + QTR + 2


def build_phase2():
    nc = bass.Bass("TRN2", target_bir_lowering=False)
    C = Ctx(nc)
    C.init_psum()
    S = C.S
    d = {}
    d["mix_d"] = C.dram_in("mix", [D_MODEL, NT2])
    d["xres_d"] = C.dram_in("xres", [D_MODEL, NT2])
    d["xo_d"] = C.dram_out("xo", [D_MODEL, CTX + QTR])
    d["wout_d"] = C.dram_in("wout", [D_MODEL, D_MODEL])
    d["wglu_d"] = C.dram_in("wglu", [512, 512])
    d["lng_d"] = C.dram_in("lng", [P, 2, NCT])
    d["lnb_d"] = C.dram_in("lnb", [P, 2, NCT])
    d["win_d"] = C.dram_in("win", [NHT, P, NCT, 256])
    d["convw_d"] = C.dram_in("convw", [P, NHT, 3])
    d["convb_d"] = C.dram_in("convb", [P, NHT])
    d["wo_d"] = C.dram_in("wo", [FFN_H, D_MODEL])
    d["hmask_d"] = C.dram_in("hmask", [P, 2])
    mods_d = C.dram_in("mods", [P, 48, 2])
    cS_d = C.dram_in("cS", [P, 8, 2])
    adaw_d = C.dram_in("adaw", [D_MODEL, 6 * D_MODEL])
    adab_d = C.dram_in("adab", [P, 48])
    mods1_d = C.dram_out("mods1", [P, 48, 2])
    MODS = C.sb([P, 48, 2], F32, "MODS")
    b_mods = Buf()
    S.dma(MODS[:], mods_d[:, :, :], writes=[b_mods])
    MODS1 = C.sb([P, 48, 2], F32, "MODS1")
    b_mods1 = Buf()
    emit_mods(C, cS_d, adaw_d, adab_d, MODS1, b_mods1)
    S.dma(mods1_d[:, :, :], MODS1[:], reads=[b_mods1], is_output=True)
    d["MODS"], d["b_mods"] = MODS, b_mods
    d["spans"] = [dict(c0=0, n=CTX, nin=CTX, lh=0, var=1, o0=0)] + lat_spans(CTX, CTX)
    es = emit_tokenlocal(C, d)
    S.finish()
    es.close()
    C.es.close()
    return nc


def lay_cS(c_b, c_ctx):
    return np.ascontiguousarray(np.stack([lay_vec_cols(c_b, 8), lay_vec_cols(c_ctx, 8)], axis=-1))


def phase2_inputs(b, j, attzT_b, xresT_b, mods0_b, inp):
    def cols(fullT):
        return np.ascontiguousarray(np.concatenate([fullT[:, :CTX], lat_slices(fullT[:, CTX:], j)], axis=1))
    return {
        "mix": cols(attzT_b), "xres": cols(xresT_b),
        "wout": np.ascontiguousarray(inp["ev_w_out"][0]), "wglu": np.ascontiguousarray(inp["s5_w_glu"][0]),
        "lng": lay_ln(inp["ln_g"][0]), "lnb": lay_ln(inp["ln_b"][0]),
        "win": lay_ffn_win(inp["ffn_w_in"][0]), "convw": lay_conv(inp["ffn_conv_w"][0]),
        "convb": lay_vec_cols(inp["ffn_conv_b"][0], NHT), "wo": np.ascontiguousarray(inp["ffn_w_out"][0]),
        "hmask": hmask_for(j), "mods": mods0_b,
        "cS": lay_cS(inp["c"][b], inp["c_ctx"]), "adaw": np.ascontiguousarray(inp["ada_w"][1]),
        "adab": lay_vec_cols(inp["ada_b"][1], 48),
    }


TWO_PI = float(2.0 * np.pi)
TCOL = 513


def rev_ap(t2, n):
    st = t2.ap[-1][0]
    return AP(t2.tensor, t2.offset + (n - 1) * st, [list(t2.ap[0]), [-st, n]])


def emit_sin(C, es_t, ANG, b_ang, OUT, b_out, ncol, shift):
    S = C.S
    t = C.sb([P, ncol], F32, "sr_t", es_t); bt = Buf()
    ki = C.sb([P, ncol], I32, "sr_k", es_t); bk = Buf()
    S.op("dve", lambda e: e.tensor_scalar(out=t[:], in0=ANG, scalar1=shift, scalar2=1.0 / TWO_PI, op0=ALU.add, op1=ALU.mult),
         reads=[b_ang], writes=[bt])
    S.op("dve", lambda e: e.tensor_copy(out=ki[:], in_=t[:]), reads=[bt], writes=[bk])
    S.op("dve", lambda e: e.tensor_copy(out=t[:], in_=ki[:]), reads=[bk], writes=[bt])
    S.op("dve", lambda e: e.scalar_tensor_tensor(out=t[:], in0=t[:], scalar=-TWO_PI, in1=ANG, op0=ALU.mult, op1=ALU.add),
         reads=[bt, b_ang], writes=[bt])
    S.op("dve", lambda e: e.tensor_scalar(out=t[:], in0=t[:], scalar1=shift, scalar2=-3.1415925, op0=ALU.add, op1=ALU.max),
         reads=[bt], writes=[bt])
    S.op("dve", lambda e: e.tensor_scalar(out=t[:], in0=t[:], scalar1=3.1415925, scalar2=None, op0=ALU.min), reads=[bt], writes=[bt])
    S.op("act", lambda e: e.activation(out=OUT, in_=t[:], func=AF.Sin), reads=[bt], writes=[b_out])


def build_phase1a(debug=False, nlat=SEQ // BLK):
    nc = bass.Bass("TRN2", target_bir_lowering=False)
    C = Ctx(nc)
    C.init_psum()
    S = C.S
    def dump(name, ap, bufs, shape, cast=False):
        if not debug:
            return
        o = C.dram_out("dbg_" + name, shape)
        if cast:
            tmpd = C.sb(shape, F32, "dbgc")
            tb_ = Buf()
            S.op("dve", lambda e: e.tensor_copy(out=tmpd[:], in_=ap), reads=bufs, writes=[tb_])
            S.dma(o, tmpd[:], reads=[tb_], is_output=True)
        else:
            S.dma(o, ap, reads=bufs, is_output=True)
    x_d = C.dram_in("xT", [D_MODEL, LTOT])
    cS_d = C.dram_in("cS", [P, 8, 2])
    adaw_d = C.dram_in("adaw", [D_MODEL, 6 * D_MODEL])
    adab_d = C.dram_in("adab", [P, 48])
    ws_d = C.dram_in("ws", [D_MODEL, P])
    par_d = C.dram_in("s5par", [P, 3, 8])
    bc_d = C.dram_in("s5bc", [P, 4, 8, P])
    dsk_d = C.dram_in("dskip", [P, 1])
    iota_d = C.dram_in("iota", [P, TCOL])
    id_d = C.dram_in("ident", [P, P])
    z_d = C.dram_out("zT", [P, LTOT])
    mods_o = C.dram_out("mods0", [P, 48, 2])

    MODS = C.sb([P, 48, 2], F32, "MODS"); b_mods = Buf()
    emit_mods(C, cS_d, adaw_d, adab_d, MODS, b_mods)
    S.dma(mods_o[:, :, :], MODS[:], reads=[b_mods], is_output=True)
    M1P = C.sb([P, 48, 2], F32, "M1P"); b_m1p = Buf()
    S.op("dve", lambda e: e.tensor_scalar(out=M1P[:], in0=MODS[:], scalar1=1.0, scalar2=None, op0=ALU.add), reads=[b_mods], writes=[b_m1p])

    COS = C.sb([P, 8, TCOL], F32, "COS"); SIN = C.sb([P, 8, TCOL], F32, "SIN"); b_cos = Buf(); b_sin = Buf()
    RR = C.sb([P, 8], F32, "RR"); b_rr = Buf()
    LBre = C.sb([P, 8, P], BF16, "LBre"); LBim = C.sb([P, 8, P], BF16, "LBim"); b_lb = Buf()
    CRE = C.sb([P, 8, P], BF16, "CRE"); CIMn = C.sb([P, 8, P], BF16, "CIMn"); b_c = Buf()
    dsk = C.sb([P, 1], F32, "dsk"); b_dsk = Buf()
    S.dma(dsk[:], dsk_d[:, :], writes=[b_dsk])
    wsb = C.sb([P, NCT, P], BF16, "wsb"); b_ws = Buf()
    with contextlib.ExitStack() as es_t:
        load_cast(C, wsb[:], b_ws, ws_d.rearrange("(kt p) n -> p kt n", p=P)[:, :, :], [P, NCT, P], es_t)
        par = C.sb([P, 3, 8], F32, "par", es_t); b_par = Buf()
        S.dma(par[:], par_d[:, :, :], writes=[b_par])
        bc = C.sb([P, 4, 8, P], F32, "bc", es_t); b_bc = Buf()
        S.dma(bc[:], bc_d[:, :, :, :], writes=[b_bc])
        ident = C.sb([P, P], F32, "ident", es_t); b_id = Buf()
        S.dma(ident[:], id_d[:, :], writes=[b_id])
        iota = C.sb([P, TCOL], F32, "iota", es_t); b_io = Buf()
        S.dma(iota[:], iota_d[:, :], writes=[b_io])
        sm = C.sb([P, 16, 8], F32, "sm", es_t)
        smb = [Buf() for _ in range(16)]
        DT, A_, TH, C1, S1, LRE, LIM, DEN, NR, FR, FI, NFI, T0, T1, THR, _ = [sm[:, i, :] for i in range(16)]
        bDT, bA, bTH, bC1, bS1, bLRE, bLIM, bDEN, bNR, bFR, bFI, bNFI, bT0, bT1, bTHR, _ = smb
        lre, lim, ldt = par[:, 0, :], par[:, 1, :], par[:, 2, :]
        S.op("act", lambda e: e.activation(out=DT, in_=ldt, func=AF.Exp), reads=[b_par], writes=[bDT])
        S.op("dve", lambda e: e.tensor_tensor(out=A_, in0=lre, in1=DT, op=ALU.mult), reads=[b_par, bDT], writes=[bA])
        S.op("act", lambda e: e.activation(out=RR[:], in_=A_, func=AF.Exp), reads=[bA], writes=[b_rr])
        S.op("dve", lambda e: e.tensor_tensor(out=TH, in0=lim, in1=DT, op=ALU.mult), reads=[b_par, bDT], writes=[bTH])
        emit_sin(C, es_t, TH, bTH, S1, bS1, 8, 0.0)
        emit_sin(C, es_t, TH, bTH, C1, bC1, 8, float(np.pi / 2))
        ki = C.sb([P, 8], I32, "ki8", es_t); bki = Buf()
        S.op("dve", lambda e: e.tensor_scalar(out=T0, in0=TH, scalar1=1.0 / TWO_PI, scalar2=None, op0=ALU.mult), reads=[bTH], writes=[bT0])
        S.op("dve", lambda e: e.tensor_copy(out=ki[:], in_=T0), reads=[bT0], writes=[bki])
        S.op("dve", lambda e: e.tensor_copy(out=T0, in_=ki[:]), reads=[bki], writes=[bT0])
        S.op("dve", lambda e: e.scalar_tensor_tensor(out=THR, in0=T0, scalar=-TWO_PI, in1=TH, op0=ALU.mult, op1=ALU.add),
             reads=[bT0, bTH], writes=[bTHR])
        S.op("dve", lambda e: e.tensor_tensor(out=LRE, in0=RR[:], in1=C1, op=ALU.mult), reads=[b_rr, bC1], writes=[bLRE])
        S.op("dve", lambda e: e.tensor_tensor(out=LIM, in0=RR[:], in1=S1, op=ALU.mult), reads=[b_rr, bS1], writes=[bLIM])
        S.op("dve", lambda e: e.tensor_tensor(out=DEN, in0=lre, in1=lre, op=ALU.mult), reads=[b_par], writes=[bDEN])
        S.op("dve", lambda e: e.tensor_tensor(out=T0, in0=lim, in1=lim, op=ALU.mult), reads=[b_par, bT0], writes=[bT0])
        S.op("dve", lambda e: e.tensor_tensor(out=DEN, in0=DEN, in1=T0, op=ALU.add), reads=[bDEN, bT0], writes=[bDEN])
        S.op("dve", lambda e: e.reciprocal(out=DEN, in_=DEN), reads=[bDEN], writes=[bDEN])
        S.op("dve", lambda e: e.tensor_scalar(out=NR, in0=LRE, scalar1=-1.0, scalar2=None, op0=ALU.add), reads=[bLRE], writes=[bNR])
        S.op("dve", lambda e: e.tensor_tensor(out=T0, in0=NR, in1=lre, op=ALU.mult), reads=[bNR, b_par, bT0], writes=[bT0])
        S.op("dve", lambda e: e.tensor_tensor(out=T1, in0=LIM, in1=lim, op=ALU.mult), reads=[bLIM, b_par], writes=[bT1])
        S.op("dve", lambda e: e.tensor_tensor(out=FR, in0=T0, in1=T1, op=ALU.add), reads=[bT0, bT1], writes=[bFR])
        S.op("dve", lambda e: e.tensor_tensor(out=FR, in0=FR, in1=DEN, op=ALU.mult), reads=[bFR, bDEN], writes=[bFR])
        S.op("dve", lambda e: e.tensor_tensor(out=T0, in0=LIM, in1=lre, op=ALU.mult), reads=[bLIM, b_par, bT0], writes=[bT0])
        S.op("dve", lambda e: e.tensor_tensor(out=T1, in0=NR, in1=lim, op=ALU.mult), reads=[bNR, b_par, bT1], writes=[bT1])
        S.op("dve", lambda e: e.tensor_tensor(out=FI, in0=T0, in1=T1, op=ALU.subtract), reads=[bT0, bT1], writes=[bFI])
        S.op("dve", lambda e: e.tensor_tensor(out=FI, in0=FI, in1=DEN, op=ALU.mult), reads=[bFI, bDEN], writes=[bFI])
        S.op("dve", lambda e: e.tensor_scalar(out=NFI, in0=FI, scalar1=-1.0, scalar2=None, op0=ALU.mult), reads=[bFI], writes=[bNFI])
        bb = C.sb([P, 2, P], F32, "bb", es_t); bbb = [Buf(), Buf()]
        for ti in range(8):
            bre, bim = bc[:, 0, ti, :], bc[:, 1, ti, :]
            S.op("dve", lambda e: e.tensor_scalar(out=bb[:, 0, :], in0=bre, scalar1=FR[:, ti:ti + 1], scalar2=None, op0=ALU.mult),
                 reads=[b_bc, bFR], writes=[bbb[0]])
            S.op("dve", lambda e: e.scalar_tensor_tensor(out=bb[:, 0, :], in0=bim, scalar=NFI[:, ti:ti + 1], in1=bb[:, 0, :], op0=ALU.mult, op1=ALU.add),
                 reads=[b_bc, bNFI, bbb[0]], writes=[bbb[0]])
            S.op("dve", lambda e: e.tensor_scalar(out=bb[:, 1, :], in0=bim, scalar1=FR[:, ti:ti + 1], scalar2=None, op0=ALU.mult),
                 reads=[b_bc, bFR], writes=[bbb[1]])
            S.op("dve", lambda e: e.scalar_tensor_tensor(out=bb[:, 1, :], in0=bre, scalar=FI[:, ti:ti + 1], in1=bb[:, 1, :], op0=ALU.mult, op1=ALU.add),
                 reads=[b_bc, bFI, bbb[1]], writes=[bbb[1]])
            for ri, dst in ((0, LBre), (1, LBim)):
                pt, pb = C.ps()
                S.op("pe", lambda e, ri=ri, pt=pt: e.matmul(pt[:, 0:P], lhsT=bb[:, ri, :], rhs=ident[:], start=True, stop=True),
                     reads=[bbb[ri], b_id], writes=[pb])
                S.op("act", lambda e, dst=dst, pt=pt: e.copy(out=dst[:, ti, :], in_=pt[:, 0:P]), reads=[pb], writes=[b_lb])
        S.op("act", lambda e: e.copy(out=CRE[:], in_=bc[:, 2, :, :]), reads=[b_bc], writes=[b_c])
        S.op("act", lambda e: e.mul(out=CIMn[:], in_=bc[:, 3, :, :], mul=-1.0), reads=[b_bc], writes=[b_c])
        ANG = C.sb([P, 8, TCOL], F32, "ANG", es_t); b_ang = Buf()
        for ti in range(8):
            S.op("dve", lambda e: e.tensor_scalar(out=ANG[:, ti, :], in0=iota[:], scalar1=THR[:, ti:ti + 1], scalar2=None, op0=ALU.mult),
                 reads=[b_io, bTHR], writes=[b_ang])
        angf = ANG[:].rearrange("p a b -> p (a b)")
        emit_sin(C, es_t, angf, b_ang, SIN[:].rearrange("p a b -> p (a b)"), b_sin, 8 * TCOL, 0.0)
        emit_sin(C, es_t, angf, b_ang, COS[:].rearrange("p a b -> p (a b)"), b_cos, 8 * TCOL, float(np.pi / 2))
        dump("sm", sm[:].rearrange("p a b -> p (a b)"), smb, [P, 128])
        dump("ang", angf, [b_ang], [P, 8 * TCOL])
        S.barrier()

    dump("cos", COS[:].rearrange("p a b -> p (a b)"), [b_cos], [P, 8 * TCOL])
    dump("sin", SIN[:].rearrange("p a b -> p (a b)"), [b_sin], [P, 8 * TCOL])
    dump("rr", RR[:], [b_rr], [P, 8])
    dump("lbre", LBre[:].rearrange("p a b -> p (a b)"), [b_lb], [P, 8 * P], cast=True)
    dump("lbim", LBim[:].rearrange("p a b -> p (a b)"), [b_lb], [P, 8 * P], cast=True)
    dump("cre", CRE[:].rearrange("p a b -> p (a b)"), [b_c], [P, 8 * P], cast=True)
    dump("cimn", CIMn[:].rearrange("p a b -> p (a b)"), [b_c], [P, 8 * P], cast=True)
    SB16 = C.sb([P, LTOT], BF16, "SB16"); YACC = C.sb([P, LTOT], F32, "YACC")
    nseg = 1 + nlat
    segs = [(0, CTX, 1)] + [(CTX + BLK * i, BLK, 0) for i in range(nlat)]
    SBb = [Buf() for _ in range(nseg)]; YAb = [Buf() for _ in range(nseg)]
    xv = x_d.rearrange("(ct p) n -> p ct n", p=P)
    with contextlib.ExitStack() as es_p:
        xs = [C.sb([P, NCT, BLK], F32, "xs", es_p) for _ in range(2)]; xsb = [Buf(), Buf()]
        ub = [C.sb([P, NCT, BLK], BF16, "ub", es_p) for _ in range(2)]; ubb = [Buf(), Buf()]
        for si, (c0, n, var) in enumerate(segs):
            vs = slice(var, var + 1)
            x_, xb_, u, u_b = xs[si % 2], xsb[si % 2], ub[si % 2], ubb[si % 2]
            S.dma(x_[:, :, 0:n], xv[:, :, c0:c0 + n], writes=[xb_])
            for ct in range(NCT):
                S.op("act", lambda e, ct=ct: e.activation(out=u[:, ct, 0:n], in_=x_[:, ct, 0:n], func=AF.Identity,
                                                          scale=M1P[:, 8 + ct, vs], bias=MODS[:, ct, vs]),
                     reads=[xb_, b_m1p, b_mods], writes=[u_b])
            pt, pb = C.ps()
            for kt in range(NCT):
                S.op("pe", lambda e, kt=kt: e.matmul(pt[:, :n], lhsT=wsb[:, kt, :], rhs=u[:, kt, 0:n], start=(kt == 0), stop=(kt == NCT - 1)),
                     reads=[b_ws, u_b], writes=[pb])
            S.op("pool", lambda e: e.memset(SB16[:, c0:c0 + 1], 0.0), writes=[SBb[si]]) if False else None
            S.op("dve", lambda e: e.tensor_copy(out=SB16[:, c0:c0 + n], in_=pt[:, :n]), reads=[pb], writes=[SBb[si]])
            S.op("dve", lambda e: e.tensor_scalar(out=YACC[:, c0:c0 + n], in0=pt[:, :n], scalar1=dsk[:, 0:1], scalar2=None, op0=ALU.mult),
                 reads=[pb, b_dsk], writes=[YAb[si]])
        S.barrier()

    dump("yacc", YACC[:], YAb, [P, LTOT])
    dump("sb16", SB16[:], SBb, [P, LTOT], cast=True)
    if debug == 1:
        S.finish()
        C.es.close()
        return nc
    NSET = 2
    W = [[C.sb([P, BLK], F32, "w%d" % i) for i in range(6)] for _ in range(NSET)]
    Wb = [[Buf() for _ in range(6)] for _ in range(NSET)]
    HB = [[C.sb([P, BLK], BF16, "hb%d" % i) for i in range(2)] for _ in range(NSET)]
    HBb = [[Buf(), Buf()] for _ in range(NSET)]
    INIT = C.sb([P, 8, 2], F32, "INIT"); INb = [Buf() for _ in range(8)]
    S.op("pool", lambda e: e.memset(INIT[:], 0.0), writes=INb)
    tn = C.sb([P, 8, 2], F32, "tn"); tnb = [Buf() for _ in range(8)]
    ZT = [C.sb([P, BLK], F32, "ZT") for _ in range(2)]; ZTb = [Buf(), Buf()]
    ZU = [C.sb([P, BLK], F32, "ZU") for _ in range(2)]; ZUb = [Buf(), Buf()]
    k = 0
    for d in range(2):
        if debug == 3 and d == 1:
            dump("yacc_f", YACC[:], YAb, [P, LTOT])
        order = [0] + ([1 + i for i in range(nlat)] if d == 0 else [1 + i for i in range(nlat - 1, -1, -1)])
        for si in order:
            c0, n, var = segs[si]
            C.ps_n = 6
            yps, ypb = C.ps_fixed(6 + (si % 2))
            for pair in range(4):
                ti = d * 4 + pair
                st = k % NSET
                k += 1
                MRE, MIM, TA, TB, GRE, GIM = [w[:, 0:n] for w in W[st]]
                bMRE, bMIM, bTA, bTB, bGRE, bGIM = Wb[st]
                HRE, HIM = HB[st][0][:, 0:n], HB[st][1][:, 0:n]
                bHRE, bHIM = HBb[st]
                xr, xrb = C.ps()
                xi, xib = C.ps()
                S.op("pe", lambda e: e.matmul(xr[:, :n], lhsT=LBre[:, ti, :], rhs=SB16[:, c0:c0 + n], start=True, stop=True),
                     reads=[b_lb, SBb[si]], writes=[xrb])
                S.op("pe", lambda e: e.matmul(xi[:, :n], lhsT=LBim[:, ti, :], rhs=SB16[:, c0:c0 + n], start=True, stop=True),
                     reads=[b_lb, SBb[si]], writes=[xib])
                cosv, sinv = COS[:, ti, 0:n], SIN[:, ti, 0:n]
                if d == 1:
                    cosv, sinv = rev_ap(cosv, n), rev_ap(sinv, n)
                S.op("dve", lambda e: e.tensor_tensor(out=MRE, in0=xr[:, :n], in1=cosv, op=ALU.mult), reads=[xrb, b_cos], writes=[bMRE])
                S.op("dve", lambda e: e.tensor_tensor(out=TA, in0=xi[:, :n], in1=sinv, op=ALU.mult), reads=[xib, b_sin], writes=[bTA])
                S.op("dve", lambda e: e.tensor_tensor(out=MRE, in0=MRE, in1=TA, op=ALU.add), reads=[bMRE, bTA], writes=[bMRE])
                S.op("dve", lambda e: e.tensor_tensor(out=MIM, in0=xi[:, :n], in1=cosv, op=ALU.mult), reads=[xib, b_cos], writes=[bMIM])
                S.op("dve", lambda e: e.tensor_tensor(out=TB, in0=xr[:, :n], in1=sinv, op=ALU.mult), reads=[xrb, b_sin], writes=[bTB])
                S.op("dve", lambda e: e.tensor_tensor(out=MIM, in0=MIM, in1=TB, op=ALU.subtract), reads=[bMIM, bTB], writes=[bMIM])
                rb = RR[:, ti:ti + 1].to_broadcast([P, n])
                if d == 0:
                    mre_s, mim_s, gre_s, gim_s = MRE, MIM, GRE, GIM
                else:
                    mre_s, mim_s, gre_s, gim_s = rev_ap(MRE, n), rev_ap(MIM, n), rev_ap(GRE, n), rev_ap(GIM, n)
                S.op("dve", lambda e: e.tensor_tensor_scan(out=gre_s, data0=rb, data1=mre_s, initial=INIT[:, ti, 0:1], op0=ALU.mult, op1=ALU.add),
                     reads=[bMRE, b_rr, INb[ti]], writes=[bGRE])
                S.op("dve", lambda e: e.tensor_tensor_scan(out=gim_s, data0=rb, data1=mim_s, initial=INIT[:, ti, 1:2], op0=ALU.mult, op1=ALU.add),
                     reads=[bMIM, b_rr, INb[ti]], writes=[bGIM])
                lastc = n - 1 if d == 0 else 0
                gl_re, gl_im = W[st][4][:, lastc:lastc + 1], W[st][5][:, lastc:lastc + 1]
                cn, sn = COS[:, ti, n:n + 1], SIN[:, ti, n:n + 1]
                S.op("dve", lambda e: e.tensor_tensor(out=tn[:, ti, 0:1], in0=gl_im, in1=sn, op=ALU.mult), reads=[bGIM, b_sin], writes=[tnb[ti]])
                S.op("dve", lambda e: e.tensor_tensor(out=tn[:, ti, 1:2], in0=gl_im, in1=cn, op=ALU.mult), reads=[bGIM, b_cos], writes=[tnb[ti]])
                S.op("dve", lambda e: e.scalar_tensor_tensor(out=INIT[:, ti, 0:1], in0=gl_re, scalar=cn, in1=tn[:, ti, 0:1], op0=ALU.mult, op1=ALU.subtract),
                     reads=[bGRE, b_cos, tnb[ti]], writes=[INb[ti]])
                S.op("dve", lambda e: e.scalar_tensor_tensor(out=INIT[:, ti, 1:2], in0=gl_re, scalar=sn, in1=tn[:, ti, 1:2], op0=ALU.mult, op1=ALU.add),
                     reads=[bGRE, b_sin, tnb[ti]], writes=[INb[ti]])
                S.op("pool", lambda e: e.tensor_tensor(out=TA, in0=GRE, in1=cosv, op=ALU.mult), reads=[bGRE, b_cos], writes=[bTA])
                S.op("pool", lambda e: e.tensor_tensor(out=TB, in0=GIM, in1=sinv, op=ALU.mult), reads=[bGIM, b_sin], writes=[bTB])
                S.op("pool", lambda e: e.tensor_tensor(out=HRE, in0=TA, in1=TB, op=ALU.subtract), reads=[bTA, bTB], writes=[bHRE])
                S.op("pool", lambda e: e.tensor_tensor(out=TA, in0=GRE, in1=sinv, op=ALU.mult), reads=[bGRE, b_sin, bTA], writes=[bTA])
                S.op("pool", lambda e: e.tensor_tensor(out=TB, in0=GIM, in1=cosv, op=ALU.mult), reads=[bGIM, b_cos, bTB], writes=[bTB])
                S.op("pool", lambda e: e.tensor_tensor(out=HIM, in0=TA, in1=TB, op=ALU.add), reads=[bTA, bTB], writes=[bHIM])
                S.op("pe", lambda e: e.matmul(yps[:, :n], lhsT=CRE[:, ti, :], rhs=HRE, start=(pair == 0), stop=False), reads=[b_c, bHRE], writes=[ypb])
                S.op("pe", lambda e: e.matmul(yps[:, :n], lhsT=CIMn[:, ti, :], rhs=HIM, start=False, stop=(pair == 3)), reads=[b_c, bHIM], writes=[ypb])
            ya = YACC[:, c0:c0 + n]
            if debug == 2 and si == 0 and d == 0:
                for st_ in range(2):
                    for wi in range(6):
                        dump("w%d_%d" % (st_, wi), W[st_][wi][:], [Wb[st_][wi]], [P, BLK])
                    for wi in range(2):
                        dump("h%d_%d" % (st_, wi), HB[st_][wi][:], [HBb[st_][wi]], [P, BLK], cast=True)
                dump("init", INIT[:].rearrange("p a b -> p (a b)"), INb, [P, 16])
                ydb = C.sb([P, BLK], F32, "ydb"); ydbb = Buf()
                S.op("dve", lambda e: e.tensor_copy(out=ydb[:], in_=yps[:, :]), reads=[ypb], writes=[ydbb])
                dump("yps", ydb[:], [ydbb], [P, BLK])
                S.finish()
                C.es.close()
                return nc
            if d == 0:
                S.op("dve", lambda e: e.tensor_tensor(out=ya, in0=yps[:, :n], in1=ya, op=ALU.add), reads=[ypb, YAb[si]], writes=[YAb[si]])
            else:
                zt, ztb = ZT[si % 2][:, 0:n], ZTb[si % 2]
                zu, zub = ZU[si % 2][:, 0:n], ZUb[si % 2]
                S.op("dve", lambda e: e.tensor_tensor(out=zt, in0=yps[:, :n], in1=ya, op=ALU.add), reads=[ypb, YAb[si]], writes=[ztb])
                S.op("act", lambda e: e.activation(out=zu, in_=zt, func=AF.Square), reads=[ztb], writes=[zub])
                S.op("dve", lambda e: e.tensor_scalar(out=zu, in0=zu, scalar1=0.044715, scalar2=1.0, op0=ALU.mult, op1=ALU.add), reads=[zub], writes=[zub])
                S.op("dve", lambda e: e.tensor_tensor(out=zu, in0=zu, in1=zt, op=ALU.mult), reads=[zub, ztb], writes=[zub])
                S.op("act", lambda e: e.activation(out=zu, in_=zu, func=AF.Sigmoid, scale=1.5957691216057308), reads=[zub], writes=[zub])
                S.op("dve", lambda e: e.tensor_tensor(out=zt, in0=zt, in1=zu, op=ALU.mult), reads=[zub, ztb], writes=[ztb])
                S.dma(z_d[:, c0:c0 + n], zt, reads=[ztb], is_output=True)
    S.finish()
    C.es.close()
    return nc


def phase1a_inputs(b, j, seqT_b, inp):
    ev_w = np.asarray(inp["ev_w_in"][0])
    ws = np.ascontiguousarray(ev_w[:, 672 + P * j: 672 + P * (j + 1)])
    par = np.zeros((P, 3, 8), np.float32)
    bcp = np.zeros((P, 4, 8, P), np.float32)
    lre, lim, ldt = inp["s5_lam_re"][0], inp["s5_lam_im"][0], inp["s5_log_dt"][0]
    bre, bim, cre, cim = inp["s5_b_re"][0], inp["s5_b_im"][0], inp["s5_c_re"][0], inp["s5_c_im"][0]
    for d in range(2):
        for pair in range(4):
            ti = d * 4 + pair
            for gi in range(2):
                g8 = 2 * pair + gi
                g = 8 * j + g8
                rows = slice(gi * 64, gi * 64 + 64)
                par[rows, 0, ti] = lre[d, g]
                par[rows, 1, ti] = lim[d, g]
                par[rows, 2, ti] = ldt[d, g]
                colsl = slice(g8 * 16, g8 * 16 + 16)
                bcp[rows, 0, ti, colsl] = bre[d, g]
                bcp[rows, 1, ti, colsl] = bim[d, g]
                bcp[rows, 2, ti, colsl] = cre[d, g].T
                bcp[rows, 3, ti, colsl] = cim[d, g].T
    dsk = np.ascontiguousarray(np.asarray(inp["s5_d"][0])[P * j:P * (j + 1)].reshape(P, 1))
    iota = np.ascontiguousarray(np.broadcast_to(np.arange(TCOL, dtype=np.float32)[None, :], (P, TCOL)))
    return {"xT": seqT_b, "cS": lay_cS(inp["c"][b], inp["c_ctx"]), "adaw": np.ascontiguousarray(inp["ada_w"][0]),
            "adab": lay_vec_cols(inp["ada_b"][0], 48), "ws": ws, "s5par": par, "s5bc": bcp, "dskip": dsk, "iota": iota,
            "ident": np.eye(P, dtype=np.float32)}


NKT = LTOT // P


def build_phase1b(nqb=SEQ // BLK, stage=0, nproj=None):
    nc = bass.Bass("TRN2", target_bir_lowering=False)
    C = Ctx(nc)
    C.init_psum()
    S = C.S
    x_d = C.dram_in("xT", [D_MODEL, LTOT])
    mods_d = C.dram_in("mods", [P, 48, 2])
    wA_d = C.dram_in("wA", [D_MODEL, 704])
    wuq_d = C.dram_in("wuq", [384, 256])
    wukv_d = C.dram_in("wukv", [256, 256])
    nrm_d = C.dram_in("norms", [P, 5])
    rope_d = C.dram_in("rope", [32, 2, SEQ])
    sel_d = C.dram_in("sel", [P, 64])
    att_d = C.dram_out("attT", [P, LTOT])

    MODS = C.sb([P, 48, 2], F32, "MODS"); b_mods = Buf()
    M1P = C.sb([P, 48, 2], F32, "M1P"); b_m1p = Buf()
    S.dma(MODS[:], mods_d[:, :, :], writes=[b_mods])
    S.op("dve", lambda e: e.tensor_scalar(out=M1P[:], in0=MODS[:], scalar1=1.0, scalar2=None, op0=ALU.add), reads=[b_mods], writes=[b_m1p])
    nrm = C.sb([P, 5], F32, "nrm"); b_nrm = Buf()
    S.dma(nrm[:], nrm_d[:, :], writes=[b_nrm])
    sel = C.sb([P, 64], F32, "sel"); b_sel = Buf()
    S.dma(sel[:], sel_d[:, :], writes=[b_sel])
    onesB = C.sb([P, P], BF16, "onesB"); b_ones = Buf()
    S.op("dve", lambda e: e.memset(onesB[:], 1.0), writes=[b_ones])
    wA = C.sb([P, NCT, 704], BF16, "wA"); b_wA = Buf()
    wuq = C.sb([P, 3, 256], BF16, "wuq"); b_wuq = Buf()
    wukv = C.sb([P, 2, 256], BF16, "wukv"); b_wukv = Buf()
    with contextlib.ExitStack() as es_t:
        wv = wA_d.rearrange("(kt p) n -> p kt n", p=P)
        for hlf in range(2):
            load_cast(C, wA[:, 4 * hlf:4 * hlf + 4, :], b_wA, wv[:, 4 * hlf:4 * hlf + 4, :], [P, 4, 704], es_t)
        load_cast(C, wuq[:], b_wuq, wuq_d.rearrange("(kt p) n -> p kt n", p=P)[:, :, :], [P, 3, 256], es_t)
        load_cast(C, wukv[:], b_wukv, wukv_d.rearrange("(kt p) n -> p kt n", p=P)[:, :, :], [P, 2, 256], es_t)
        S.barrier()
    QT = [C.sb([P, LTOT], BF16, "QT") for _ in range(2)]
    KT = [C.sb([P, LTOT], BF16, "KT") for _ in range(2)]
    VA = C.sb([P, NKT, 2, 65], BF16, "VA")
    nblk = 1 + SEQ // BLK
    blocks = [(0, CTX, 1, None)] + [(CTX + BLK * i, BLK, 0, i) for i in range(SEQ // BLK)]
    pblocks = blocks if nproj is None else blocks[:1 + nproj]
    QTb = [[Buf() for _ in range(nblk)] for _ in range(2)]
    KTb = [[Buf() for _ in range(nblk)] for _ in range(2)]
    VAb = [Buf() for _ in range(nblk)]
    b_va1 = Buf()
    import os as _os
    SK = _os.environ.get("P1B_SKIP", "").split(",")
    if "vamem" not in SK:
        S.op("dve", lambda e: e.memset(VA[:, :, :, 64:65], 1.0), writes=[b_va1])
    xv = x_d.rearrange("(ct p) n -> p ct n", p=P)
    CUT = int(_os.environ.get("P1B_CUT", "0"))
    cpn = [0]

    class StopEmit(Exception):
        pass

    def cp():
        cpn[0] += 1
        if CUT and cpn[0] >= CUT:
            S.mute = True
    try:
      with contextlib.ExitStack() as es_p:
        cp()
        xs = C.sb([P, NCT, BLK], F32, "xs", es_p); xsb = Buf()
        ub = [C.sb([P, NCT, BLK], BF16, "ub", es_p) for _ in range(2)]; ubb = [Buf(), Buf()]
        SQ = C.sb([P, 3, BLK], BF16, "SQ", es_p); SQb = [Buf() for _ in range(3)]
        CQ = C.sb([P, 3, BLK], F32, "CQ", es_p); CQb = [Buf() for _ in range(3)]
        CQN = C.sb([P, 3, BLK], BF16, "CQN", es_p); CQNb = [Buf() for _ in range(3)]
        CKVN = C.sb([P, 2, BLK], BF16, "CKVN", es_p); CKVNb = [Buf() for _ in range(2)]
        RS = C.sb([P, 2, BLK], F32, "RS", es_p); RSb = [Buf(), Buf()]
        ROPE = C.sb([P, 2, BLK], F32, "ROPE", es_p); b_rope = Buf()
        TR = C.sb([P, 4, BLK], F32, "TR", es_p); TRb = [Buf() for _ in range(4)]
        for bi, (c0, n, var, li) in enumerate(pblocks):
            vs = slice(var, var + 1)
            u, u_b = ub[bi % 2], ubb[bi % 2]
            ntile = n // P
            S.dma(xs[:, :, 0:n], xv[:, :, c0:c0 + n], writes=[xsb])
            if li is not None and "rope" not in SK:
                S.dma(ROPE[64:96, :, 0:n], rope_d[:, :, li * BLK:li * BLK + n], writes=[b_rope])
            for ct in range(NCT):
                S.op("act", lambda e, ct=ct: e.activation(out=u[:, ct, 0:n], in_=xs[:, ct, 0:n], func=AF.Identity,
                                                          scale=M1P[:, 8 + ct, vs], bias=MODS[:, ct, vs]),
                     reads=[xsb, b_m1p, b_mods], writes=[u_b])

            def rmsnorm(ntl, wc0, dim, gcol, DST, DSTb, rsi):
                for m in range(ntl):
                    pt, pb = C.ps()
                    for kt in range(NCT):
                        S.op("pe", lambda e, kt=kt, m=m, pt=pt: e.matmul(pt[:, :n], lhsT=wA[:, kt, wc0 + m * P:wc0 + (m + 1) * P], rhs=u[:, kt, 0:n],
                                                                       start=(kt == 0), stop=(kt == NCT - 1)), reads=[b_wA, u_b], writes=[pb])
                    S.op("dve", lambda e, m=m, pt=pt: e.tensor_copy(out=CQ[:, m, 0:n], in_=pt[:, :n]), reads=[pb], writes=[CQb[m]])
                    S.op("pool", lambda e, m=m: e.tensor_tensor(out=SQ[:, m, 0:n], in0=CQ[:, m, 0:n], in1=CQ[:, m, 0:n], op=ALU.mult),
                         reads=[CQb[m]], writes=[SQb[m]])
                cp()
                sp_, spb = C.ps()
                for m in range(ntl):
                    S.op("pe", lambda e, m=m: e.matmul(sp_[:, :n], lhsT=onesB[:], rhs=SQ[:, m, 0:n], start=(m == 0), stop=(m == ntl - 1)),
                         reads=[b_ones, SQb[m]], writes=[spb])
                rs, rsb = RS[:, rsi, 0:n], RSb[rsi]
                cp()
                S.op("act", lambda e: e.activation(out=rs, in_=sp_[:, :n], func=AF.Sqrt, scale=1.0 / dim, bias=NORM_EPS), reads=[spb], writes=[rsb])
                cp()
                S.op("dve", lambda e: e.reciprocal(out=rs, in_=rs), reads=[rsb], writes=[rsb])
                cp()
                for m in range(ntl):
                    S.op("dve", lambda e, m=m: e.scalar_tensor_tensor(out=DST[:, m, 0:n], in0=CQ[:, m, 0:n], scalar=nrm[:, gcol + m:gcol + m + 1],
                                                                       in1=rs, op0=ALU.mult, op1=ALU.mult),
                         reads=[CQb[m], b_nrm, rsb], writes=[DSTb[m]])

            cp()
            rmsnorm(3, 0, 384.0, 0, CQN, CQNb, 0)
            cp()
            for hh in range(2):
                pq, pqb = C.ps()
                pr, prb = C.ps()
                for m in range(3):
                    S.op("pe", lambda e, m=m: e.matmul(pq[0:96, :n], lhsT=wuq[:, m, hh * P:hh * P + 96], rhs=CQN[:, m, 0:n],
                                                       start=(m == 0), stop=(m == 2)), reads=[b_wuq, CQNb[m]], writes=[pqb])
                if li is not None:
                    for m in range(3):
                        S.op("pe", lambda e, m=m: e.matmul(pr[64:96, :n], lhsT=wuq[:, m, hh * P + 96:hh * P + P], rhs=CQN[:, m, 0:n],
                                                           start=(m == 0), stop=(m == 2)), reads=[b_wuq, CQNb[m]], writes=[prb])
                S.op("act", lambda e: e.copy(out=QT[hh][0:64, c0:c0 + n], in_=pq[0:64, :n]), reads=[pqb], writes=[QTb[hh][bi]])
                if li is None:
                    S.op("act", lambda e: e.copy(out=QT[hh][64:96, c0:c0 + n], in_=pq[64:96, :n]), reads=[pqb], writes=[QTb[hh][bi]])
                else:
                    S.op("dve", lambda e: e.tensor_tensor(out=TR[64:96, 0, 0:n], in0=pq[64:96, :n], in1=ROPE[64:96, 0, 0:n], op=ALU.mult),
                         reads=[pqb, b_rope], writes=[TRb[0]])
                    S.op("dve", lambda e: e.tensor_tensor(out=TR[64:96, 1, 0:n], in0=pr[64:96, :n], in1=ROPE[64:96, 1, 0:n], op=ALU.mult),
                         reads=[prb, b_rope], writes=[TRb[1]])
                    S.op("dve", lambda e: e.tensor_tensor(out=QT[hh][64:96, c0:c0 + n], in0=TR[64:96, 0, 0:n], in1=TR[64:96, 1, 0:n], op=ALU.add),
                         reads=[TRb[0], TRb[1]], writes=[QTb[hh][bi]])
            cp()
            rmsnorm(2, 384, 256.0, 3, CKVN, CKVNb, 1)
            cp()
            for hh in range(2):
                pk, pkb = C.ps()
                for m in range(2):
                    S.op("pe", lambda e, m=m: e.matmul(pk[0:64, :n], lhsT=wukv[:, m, hh * 64:hh * 64 + 64], rhs=CKVN[:, m, 0:n],
                                                       start=(m == 0), stop=(m == 1)), reads=[b_wukv, CKVNb[m]], writes=[pkb])
                S.op("act", lambda e: e.copy(out=KT[hh][0:64, c0:c0 + n], in_=pk[0:64, :n]), reads=[pkb], writes=[KTb[hh][bi]])
            cp()
            pkr, pkrb = C.ps()
            pkq, pkqb = C.ps()
            for kt in range(NCT):
                S.op("pe", lambda e, kt=kt: e.matmul(pkr[64:96, :n], lhsT=wA[:, kt, 640:672], rhs=u[:, kt, 0:n], start=(kt == 0), stop=(kt == NCT - 1)),
                     reads=[b_wA, u_b], writes=[pkrb])
            if li is None:
                for hh in range(2):
                    S.op("act", lambda e: e.copy(out=KT[hh][64:96, c0:c0 + n], in_=pkr[64:96, :n]), reads=[pkrb], writes=[KTb[hh][bi]])
            else:
                for kt in range(NCT):
                    S.op("pe", lambda e, kt=kt: e.matmul(pkq[64:96, :n], lhsT=wA[:, kt, 672:704], rhs=u[:, kt, 0:n], start=(kt == 0), stop=(kt == NCT - 1)),
                         reads=[b_wA, u_b], writes=[pkqb])
                S.op("dve", lambda e: e.tensor_tensor(out=TR[64:96, 2, 0:n], in0=pkr[64:96, :n], in1=ROPE[64:96, 0, 0:n], op=ALU.mult),
                     reads=[pkrb, b_rope], writes=[TRb[2]])
                S.op("dve", lambda e: e.tensor_tensor(out=TR[64:96, 3, 0:n], in0=pkq[64:96, :n], in1=ROPE[64:96, 1, 0:n], op=ALU.mult),
                     reads=[pkqb, b_rope], writes=[TRb[3]])
                for hh in range(2):
                    S.op("dve", lambda e: e.tensor_tensor(out=KT[hh][64:96, c0:c0 + n], in0=TR[64:96, 2, 0:n], in1=TR[64:96, 3, 0:n], op=ALU.add),
                         reads=[TRb[2], TRb[3]], writes=[KTb[hh][bi]])
            cp()
            pv, pvb = C.ps()
            for tt in range(ntile):
                for m in range(2):
                    S.op("pe", lambda e, m=m, tt=tt: e.matmul(pv[:, tt * P:(tt + 1) * P], lhsT=CKVN[:, m, tt * P:(tt + 1) * P], rhs=wukv[:, m, 128:256],
                                                              start=(m == 0), stop=(m == 1)), reads=[b_wukv, CKVNb[m]], writes=[pvb])
            gt0 = c0 // P
            pvv = pv[:, 0:n]
            pv4 = AP(pvv.tensor, pvv.offset, [list(pvv.ap[0]), [P, ntile], [64, 2], [1, 64]])
            if "vacopy" not in SK:
                S.op("act", lambda e: e.copy(out=VA[:, gt0:gt0 + ntile, :, 0:64], in_=pv4), reads=[pvb, b_va1], writes=[VAb[bi]])
        S.barrier()
    except StopEmit:
        S.barrier()
        S.finish()
        C.es.close()
        return nc
    if stage == 1:
        ND = 768
        for nm, T_, bl in (("q0", QT[0], QTb[0]), ("k0", KT[0], KTb[0]), ("q1", QT[1], QTb[1]), ("k1", KT[1], KTb[1])):
            o = C.dram_out("dbg_" + nm, [P, ND])
            tmpd = C.sb([P, ND], F32, "dbgc"); tb_ = Buf()
            S.op("dve", lambda e: e.tensor_copy(out=tmpd[0:96, :], in_=T_[0:96, 0:ND]), reads=bl, writes=[tb_])
            S.dma(o[0:96, :], tmpd[0:96, :], reads=[tb_], is_output=True)
        o = C.dram_out("dbg_va", [P, 6 * 130])
        tmpd = C.sb([P, 6 * 130], F32, "dbgv"); tb_ = Buf()
        S.op("dve", lambda e: e.tensor_copy(out=tmpd[:], in_=VA[:, 0:6, :, :].rearrange("p a b c -> p (a b c)")), reads=VAb + [b_va1], writes=[tb_])
        S.dma(o, tmpd[:], reads=[tb_], is_output=True)
        S.finish()
        C.es.close()
        return nc
    NPT = 4
    PT = [C.sb([P, BLK], BF16, "PT") for _ in range(NPT)]; PTb = [Buf() for _ in range(NPT)]
    OS = [C.sb([P, BLK], F32, "OS") for _ in range(2)]; OSb = [Buf(), Buf()]
    RC = [C.sb([P, BLK], F32, "RC") for _ in range(2)]; RCb = [Buf(), Buf()]
    AT = [C.sb([P, BLK], F32, "AT") for _ in range(2)]; ATb = [Buf(), Buf()]
    C.ps_n = 5
    C.ps_i = 0
    k = 0
    ip = 0
    qblocks = [blocks[0]] + blocks[1:1 + nqb]
    for hh in range(2):
        for (c0, n, var, li) in qblocks:
            kts = [0, 1] if li is None else list(range(NKT))
            bi = 0 if li is None else 1 + li
            ops_, opb = C.ps_fixed(6 + (k % 2))
            LOOK = 2
            stg = {}

            def qk(i):
                kt = kts[i]
                kb = 0 if kt < 2 else 1 + (kt - 2) // 4
                sps, spb = C.ps()
                S.op("pe", lambda e: e.matmul(sps[:, :n], lhsT=KT[hh][0:96, kt * P:(kt + 1) * P], rhs=QT[hh][0:96, c0:c0 + n], start=True, stop=True),
                     reads=[KTb[hh][kb], QTb[hh][bi]], writes=[spb])
                stg[i] = (sps, spb, kb, kt)

            for i in range(min(LOOK, len(kts))):
                qk(i)
            for i in range(len(kts)):
                if i + LOOK < len(kts):
                    qk(i + LOOK)
                sps, spb, kb, kt = stg.pop(i)
                pt_, ptb = PT[ip % NPT], PTb[ip % NPT]
                ip += 1
                S.op("act", lambda e: e.activation(out=pt_[:, 0:n], in_=sps[:, :n], func=AF.Exp, scale=MLA_SCALE), reads=[spb], writes=[ptb])
                S.op("pe", lambda e: e.matmul(ops_[0:65, :n], lhsT=VA[:, kt, hh, :], rhs=pt_[:, 0:n], start=(i == 0), stop=(i == len(kts) - 1)),
                     reads=[VAb[kb], b_va1, ptb], writes=[opb])
            os_, osb = OS[k % 2], OSb[k % 2]
            rc, rcb = RC[k % 2], RCb[k % 2]
            at, atb = AT[k % 2], ATb[k % 2]
            k += 1
            S.op("dve", lambda e: e.tensor_copy(out=os_[0:65, 0:n], in_=ops_[0:65, :n]), reads=[opb], writes=[osb])
            bps, bpb = C.ps()
            S.op("pe", lambda e: e.matmul(bps[0:64, :n], lhsT=sel[0:65, 0:64], rhs=os_[0:65, 0:n], start=True, stop=True), reads=[b_sel, osb], writes=[bpb])
            S.op("dve", lambda e: e.reciprocal(out=rc[0:64, 0:n], in_=bps[0:64, :n]), reads=[bpb], writes=[rcb])
            S.op("pool", lambda e: e.tensor_tensor(out=at[0:64, 0:n], in0=os_[0:64, 0:n], in1=rc[0:64, 0:n], op=ALU.mult), reads=[osb, rcb], writes=[atb])
            S.dma(att_d[hh * 64:(hh + 1) * 64, c0:c0 + n], at[0:64, 0:n], reads=[atb], is_output=True)
    S.finish()
    C.es.close()
    return nc


def rope_tables():
    t = np.arange(SEQ)
    row = (t // 64).astype(np.float32)
    col = (t % 64).astype(np.float32)
    inv = (10000.0 ** (-np.arange(8, dtype=np.float32) / 8)).astype(np.float32)
    ar = row[:, None] * inv
    ac = col[:, None] * inv
    ang = np.concatenate([ar, ar, ac, ac], axis=-1).astype(np.float32)
    cos = np.cos(ang).astype(np.float32).T
    sin = np.sin(ang).astype(np.float32).T
    sign = np.where((np.arange(32) % 16) < 8, -1.0, 1.0).astype(np.float32)[:, None]
    return np.ascontiguousarray(np.stack([cos, sign * sin], axis=1))


def rot_perm():
    d = np.arange(32)
    return np.where((d % 16) < 8, d + 8, d - 8)


def phase1b_inputs(b, j, seqT_b, mods0_b, inp):
    ev_w = np.asarray(inp["ev_w_in"][0])
    perm = rot_perm()
    wkr = ev_w[:, 640:672]
    wA = np.ascontiguousarray(np.concatenate([ev_w[:, 0:640], wkr, wkr[:, perm]], axis=1))
    wuq_full = np.asarray(inp["mla_w_uq"][0])
    wukv_full = np.asarray(inp["mla_w_ukv"][0])
    qcols, kcols, vcols = [], [], []
    for hh in range(2):
        h = 2 * j + hh
        wq = wuq_full[:, h * 96:(h + 1) * 96]
        qcols += [wq, wq[:, 64:96][:, perm]]
        kcols.append(wukv_full[:, h * 128:h * 128 + 64])
        vcols.append(wukv_full[:, h * 128 + 64:(h + 1) * 128])
    wuq = np.ascontiguousarray(np.concatenate(qcols, axis=1))
    wukv = np.ascontiguousarray(np.concatenate(kcols + vcols, axis=1))
    norms = np.ascontiguousarray(np.concatenate([lay_vec_cols(inp["mla_q_norm"][0], 3), lay_vec_cols(inp["mla_kv_norm"][0], 2)], axis=1))
    sel = np.zeros((P, 64), np.float32)
    sel[64, :] = 1.0
    return {"xT": seqT_b, "mods": mods0_b, "wA": wA, "wuq": wuq, "wukv": wukv, "norms": norms, "rope": rope_tables(), "sel": sel}


_PROGS = {}


def _prog(name, fn):
    if name not in _PROGS:
        _PROGS[name] = fn()
    return _PROGS[name]


def kernel(**inputs):
    inp = {k: np.asarray(v) for k, v in inputs.items()}
    x, ctx = inp["x"].astype(np.float32), inp["ctx"].astype(np.float32)
    B = x.shape[0]
    cores = list(range(8))
    bj = [(c // 4, c % 4) for c in cores]
    seqT = [np.ascontiguousarray(np.concatenate([ctx[b], x[b]], axis=0).T) for b in range(B)]
    r1a = run_bass_kernel_spmd(_prog("p1a", build_phase1a), [phase1a_inputs(b, j, seqT[b], inp) for (b, j) in bj], core_ids=cores).results
    mods0 = [np.ascontiguousarray(r1a[c]["mods0"]) for c in cores]
    r1b = run_bass_kernel_spmd(_prog("p1b", build_phase1b), [phase1b_inputs(b, j, seqT[b], mods0[c], inp) for c, (b, j) in enumerate(bj)],
                               core_ids=cores).results
    attzT = []
    for b in range(B):
        attzT.append(np.ascontiguousarray(np.concatenate([r1b[4 * b + j]["attT"] for j in range(4)] + [r1a[4 * b + j]["zT"] for j in range(4)], axis=0)))
    r2 = run_bass_kernel_spmd(_prog("p2", build_phase2), [phase2_inputs(b, j, attzT[b], seqT[b], mods0[c], inp) for c, (b, j) in enumerate(bj)],
                              core_ids=cores).results
    mods1 = [np.ascontiguousarray(r2[c]["mods1"]) for c in cores]
    x2T = []
    for b in range(B):
        x2T.append(np.ascontiguousarray(np.concatenate([r2[4 * b]["xo"][:, :CTX]] + [r2[4 * b + j]["xo"][:, CTX:] for j in range(4)], axis=1)))
    r3 = run_bass_kernel_spmd(_prog("p3", build_phase3), [phase3_inputs(b, j, x2T[b], mods1[c], inp) for c, (b, j) in enumerate(bj)],
                              core_ids=cores).results
    ogT = [np.ascontiguousarray(np.concatenate([r3[4 * b + j]["ogT"] for j in range(4)], axis=0)) for b in range(B)]
    r4 = run_bass_kernel_spmd(_prog("p4", build_phase4),
                              [phase4_inputs(b, j, ogT[b], np.ascontiguousarray(x2T[b][:, CTX:]), mods1[c], inp) for c, (b, j) in enumerate(bj)],
                              core_ids=cores).results
    out = np.zeros((B, SEQ, D_MODEL), np.float32)
    for c, (b, j) in enumerate(bj):
        out[b, QTR * j:QTR * (j + 1)] = r4[c]["xo"].T
    return out
```

```python
import contextlib
import numpy as np
import concourse.bass as bass
import concourse.mybir as mybir
from concourse.bass_utils import run_bass_kernel_spmd
from concourse.ap import AP

F32 = mybir.dt.float32
BF16 = mybir.dt.bfloat16
I32 = mybir.dt.int32
AF = mybir.ActivationFunctionType
ALU = mybir.AluOpType

P = 128
D_MODEL = 1024
NCT = 8
SEQ = 8192
CTX = 256
LTOT = SEQ + CTX
DEPTH = 2
DN_ALPHA = (2.0 * DEPTH) ** 0.25
NORM_EPS = 1e-6
FFN_H = 2816
NHT = FFN_H // P
MLA_SCALE = 96 ** -0.5
QTR = SEQ // 4
SPAN = 512


class Buf:
    __slots__ = ("w", "r", "name")

    def __init__(self, name=""):
        self.w = {}
        self.r = {}
        self.name = name


class Sched:
    NDMA = 24

    def __init__(self, nc, es):
        self.nc = nc
        self.engs = {"pe": nc.tensor, "dve": nc.vector, "act": nc.scalar, "pool": nc.gpsimd, "sp": nc.sync}
        self.sems = {}
        for k in self.engs:
            self.sems[k] = es.enter_context(nc.semaphore("s_" + k))
        self.cnt = {k: 0 for k in self.engs}
        self.seen = {k: {} for k in self.engs}
        for i in range(self.NDMA):
            self.sems[("d", i)] = es.enter_context(nc.semaphore("d%d" % i))
            self.cnt[("d", i)] = 0
        self.rr = 0
        self.out_deps = []

    def _wait(self, e, deps):
        best = {}
        for (k, c) in deps:
            if k == e and e == "pe":
                continue
            if c > best.get(k, 0):
                best[k] = c
        seen = self.seen[e]
        for k, c in best.items():
            if seen.get(k, 0) >= c:
                continue
            self.engs[e].wait_ge(self.sems[k], c)
            seen[k] = c

    def _deps(self, reads, writes):
        deps = []
        for b in reads:
            deps.extend(b.w.items())
        for b in writes:
            deps.extend(b.w.items())
            deps.extend(b.r.items())
        return deps

    def _mark(self, me, reads, writes):
        k, c = me
        for b in reads:
            if b.r.get(k, 0) < c:
                b.r[k] = c
        for b in writes:
            b.w[k] = c
            b.r = {}

    mute = False

    def op(self, e, fn, reads=(), writes=()):
        if self.mute:
            return
        self._wait(e, self._deps(reads, writes))
        ins = fn(self.engs[e])
        self.cnt[e] += 1
        ins.then_inc(self.sems[e], 1)
        self._mark((e, self.cnt[e]), reads, writes)

    def dma(self, out, in_, reads=(), writes=(), q="sp", is_output=False):
        if self.mute:
            return
        i = self.rr
        self.rr = (self.rr + 1) % self.NDMA
        key = ("d", i)
        deps = self._deps(reads, writes)
        if self.cnt[key] > 0:
            deps.append((key, self.cnt[key]))
        self._wait(q, deps)
        ins = self.engs[q].dma_start(out=out, in_=in_)
        self.cnt[key] += 16
        ins.then_inc(self.sems[key], 16)
        me = (key, self.cnt[key])
        self._mark(me, reads, writes)
        if is_output:
            self.out_deps.append(me)

    def barrier(self):
        allc = [(k, c) for k, c in self.cnt.items() if c > 0]
        for e in self.engs:
            self._wait(e, allc)

    def finish(self):
        self._wait("sp", self.out_deps)
        self.out_deps = []


class Ctx:
    def __init__(self, nc):
        self.nc = nc
        self.es = contextlib.ExitStack()
        self.S = Sched(nc, self.es)
        self.n = 0
        self.ps_tiles = []
        self.ps_i = 0

    def sb(self, shape, dt, name=None, es=None):
        self.n += 1
        t = (es or self.es).enter_context(self.nc.sbuf_tensor("%s_%d" % (name or "t", self.n), list(shape), dt))
        return t

    def init_psum(self, es=None):
        self.ps_tiles = []
        for i in range(8):
            t = (es or self.es).enter_context(self.nc.psum_tensor("ps%d_%d" % (i, self.n), [P, 512], F32))
            self.ps_tiles.append((t, Buf("ps%d" % i)))
        self.ps_i = 0
        self.ps_n = len(self.ps_tiles)

    def ps(self):
        t = self.ps_tiles[self.ps_i]
        self.ps_i = (self.ps_i + 1) % self.ps_n
        return t

    def ps_fixed(self, i):
        return self.ps_tiles[i]

    def dram_in(self, name, shape, dt=F32):
        return self.nc.dram_tensor(name, list(shape), dt, kind="ExternalInput").ap()

    def dram_out(self, name, shape, dt=F32):
        return self.nc.dram_tensor(name, list(shape), dt, kind="ExternalOutput").ap()


def bc_last(ap2, n):
    return ap2.to_broadcast([ap2.shape[0], n])


def emit_mods(C, cS_d, adaw_d, adab_d, MODS, MODS_b):
    S = C.S
    nc = C.nc
    with contextlib.ExitStack() as es:
        cS = C.sb([P, 8, 2], F32, "cS", es)
        sil = C.sb([P, 8, 2], F32, "sil", es)
        adab = C.sb([P, 48], F32, "adab", es)
        CH = 768
        wbuf = [C.sb([P, 8, CH], F32, "adaw", es) for _ in range(2)]
        wb = [Buf() for _ in range(2)]
        b_cS, b_sil, b_adab = Buf(), Buf(), Buf()
        pst, psb = C.ps()
        S.dma(cS[:], cS_d[:, :, :], writes=[b_cS])
        S.dma(adab[:], adab_d[:, :], writes=[b_adab])
        S.op("act", lambda e: e.activation(out=sil[:], in_=cS[:], func=AF.Silu), reads=[b_cS], writes=[b_sil])
        wv = adaw_d.rearrange("(kt p) n -> p kt n", p=P)
        nch = 6144 // CH
        for ch in range(nch):
            wt, wbf = wbuf[ch % 2], wb[ch % 2]
            S.dma(wt[:], wv[:, :, ch * CH:(ch + 1) * CH], writes=[wbf])
            for cc in range(CH // P):
                gcc = ch * (CH // P) + cc
                for kt in range(8):
                    S.op("pe", lambda e, kt=kt, cc=cc, gcc=gcc, wt=wt: e.matmul(
                        pst[:, gcc * 2:gcc * 2 + 2], lhsT=wt[:, kt, cc * P:(cc + 1) * P], rhs=sil[:, kt, :],
                        start=(kt == 0), stop=(kt == 7)), reads=[wbf, b_sil], writes=[psb])
        pv = pst[:, 0:96]
        pv3 = AP(pv.tensor, pv.offset, [list(pv.ap[0]), [2, 48], [1, 2]])
        ab = adab[:]
        ab3 = AP(ab.tensor, ab.offset, [list(ab.ap[0]), [1, 48], [0, 2]])
        S.op("dve", lambda e: e.tensor_tensor(out=MODS[:], in0=pv3, in1=ab3, op=ALU.add),
             reads=[psb, b_adab], writes=[MODS_b])
        S.barrier()


class LNRes:
    def __init__(self, C, es):
        self.onesM = C.sb([P, P], F32, "onesM", es)
        self.b_ones = Buf()
        self.tmp = C.sb([P, 2, 512], F32, "lntmp", es)
        self.tmpb = [Buf(), Buf()]
        self.stat = C.sb([P, 2, 3, 512], F32, "lnstat", es)
        self.statb = [[Buf(), Buf(), Buf()], [Buf(), Buf(), Buf()]]
        self.k = 0
        C.S.op("pool", lambda e: e.memset(self.onesM[:], 1.0 / D_MODEL), writes=[self.b_ones])


def emit_ln(C, L, V, Vb, c0, n, lng, lnb, b_lnp, which):
    S = C.S
    assert n <= 512
    mps, mpsb = C.ps()
    qps, qpsb = C.ps()
    par = L.k % 2
    L.k += 1
    onesM, b_ones = L.onesM, L.b_ones
    sl = slice(c0, c0 + n)
    for ct in range(NCT):
        S.op("pe", lambda e, ct=ct: e.matmul(mps[:, :n], lhsT=onesM[:], rhs=V[:, ct, sl], start=(ct == 0), stop=(ct == NCT - 1)),
             reads=[b_ones, Vb[ct]], writes=[mpsb])
    for ct in range(NCT):
        t, tb = L.tmp[:, ct % 2, :n], L.tmpb[ct % 2]
        S.op("act", lambda e, ct=ct, t=t: e.activation(out=t, in_=V[:, ct, sl], func=AF.Square), reads=[Vb[ct]], writes=[tb])
        S.op("pe", lambda e, ct=ct, t=t: e.matmul(qps[:, :n], lhsT=onesM[:], rhs=t, start=(ct == 0), stop=(ct == NCT - 1)),
             reads=[b_ones, tb], writes=[qpsb])
    stat, statb = L.stat, L.statb[par]
    mean, var, m2 = stat[:, par, 0, :n], stat[:, par, 1, :n], stat[:, par, 2, :n]
    S.op("act", lambda e: e.copy(out=mean, in_=mps[:, :n]), reads=[mpsb], writes=[statb[0]])
    S.op("pool", lambda e: e.tensor_tensor(out=m2, in0=mean, in1=mean, op=ALU.mult), reads=[statb[0]], writes=[statb[2]])
    S.op("dve", lambda e: e.tensor_tensor(out=var, in0=qps[:, :n], in1=m2, op=ALU.subtract), reads=[qpsb, statb[2]], writes=[statb[1]])
    S.op("act", lambda e: e.activation(out=var, in_=var, func=AF.Sqrt, bias=NORM_EPS), reads=[statb[1]], writes=[statb[1]])
    S.op("dve", lambda e: e.reciprocal(out=var, in_=var), reads=[statb[1]], writes=[statb[1]])
    for ct in range(NCT):
        S.op("dve", lambda e, ct=ct: e.tensor_tensor(out=V[:, ct, sl], in0=V[:, ct, sl], in1=mean, op=ALU.subtract),
             reads=[Vb[ct], statb[0]], writes=[Vb[ct]])
        S.op("pool", lambda e, ct=ct: e.tensor_tensor(out=V[:, ct, sl], in0=V[:, ct, sl], in1=var, op=ALU.mult),
             reads=[Vb[ct], statb[1]], writes=[Vb[ct]])
        S.op("act", lambda e, ct=ct: e.activation(out=V[:, ct, sl], in_=V[:, ct, sl], func=AF.Identity,
                                                  scale=lng[:, which, ct:ct + 1], bias=lnb[:, which, ct:ct + 1]),
             reads=[Vb[ct], b_lnp], writes=[Vb[ct]])


def load_cast(C, dst_bf, dst_b, src_ap3, shape, es_tmp, eng="pool", nm="stg"):
    S = C.S
    stg = C.sb(shape, F32, nm, es_tmp)
    sb_ = Buf()
    S.dma(stg[:], src_ap3, writes=[sb_])
    S.op(eng, lambda e: e.tensor_copy(out=dst_bf, in_=stg[:]), reads=[sb_], writes=[dst_b])


def chunks_of(n):
    if n <= 512:
        return [(0, n)]
    h = n // 2
    return [(0, h), (h, n - h)]


def emit_tokenlocal(C, cfg):
    S = C.S
    nc = C.nc
    es = contextlib.ExitStack()
    has_glu = cfg.get("wglu_d") is not None
    MODS, b_mods = cfg["MODS"], cfg["b_mods"]
    NW = SPAN + 2
    woutb = C.sb([P, NCT, D_MODEL], BF16, "woutb", es); b_wout = Buf()
    wob = C.sb([P, NHT, D_MODEL], BF16, "wob", es); b_wob = [Buf() for _ in range(NHT // 2)]
    convw = C.sb([P, NHT, 3], F32, "convw", es); convb = C.sb([P, NHT], F32, "convb", es); b_conv = Buf()
    lng = C.sb([P, 2, NCT], F32, "lng", es); lnb = C.sb([P, 2, NCT], F32, "lnb", es); b_lnp = Buf()
    hmask = C.sb([P, 2], F32, "hmask", es); b_hm = Buf()
    M1P = C.sb([P, 48, 2], F32, "M1P", es); b_m1p = Buf()
    if has_glu:
        wglub = C.sb([P, 4, 512], BF16, "wglub", es); b_wglu = Buf()
    S.dma(convw[:], cfg["convw_d"][:, :, :], writes=[b_conv])
    S.dma(convb[:], cfg["convb_d"][:, :], writes=[b_conv])
    S.dma(lng[:], cfg["lng_d"][:, :, :], writes=[b_lnp])
    S.dma(lnb[:], cfg["lnb_d"][:, :, :], writes=[b_lnp])
    S.dma(hmask[:], cfg["hmask_d"][:, :], writes=[b_hm])
    S.op("dve", lambda e: e.tensor_scalar(out=M1P[:], in0=MODS[:], scalar1=1.0, scalar2=None, op0=ALU.add),
         reads=[b_mods], writes=[b_m1p])
    with contextlib.ExitStack() as es_t:
        wv = cfg["wout_d"].rearrange("(kt p) n -> p kt n", p=P)
        for hlf in range(2):
            load_cast(C, woutb[:, hlf * 4:(hlf + 1) * 4, :], b_wout, wv[:, hlf * 4:(hlf + 1) * 4, :], [P, 4, D_MODEL], es_t)
        if has_glu:
            gv = cfg["wglu_d"].rearrange("(kt p) n -> p kt n", p=P)
            load_cast(C, wglub[:], b_wglu, gv[:, :, :], [P, 4, 512], es_t)
        ov = cfg["wo_d"].rearrange("(kt p) n -> p kt n", p=P)
        stg2 = [C.sb([P, 2, D_MODEL], F32, "wostg", es_t) for _ in range(2)]
        stg2b = [Buf(), Buf()]
        for i in range(NHT // 2):
            S.dma(stg2[i % 2][:], ov[:, 2 * i:2 * i + 2, :], writes=[stg2b[i % 2]])
            S.op("pool", lambda e, i=i: e.tensor_copy(out=wob[:, 2 * i:2 * i + 2, :], in_=stg2[i % 2][:]),
                 reads=[stg2b[i % 2]], writes=[b_wob[i]])
        S.barrier()
    A0 = C.sb([P, NCT, NW], F32, "A0", es); A0b = [Buf() for _ in range(NCT)]
    A1 = C.sb([P, NCT, NW], F32, "A1", es); A1b = [Buf() for _ in range(NCT)]
    B0 = C.sb([P, NCT, NW], BF16, "B0", es); B0b = [Buf() for _ in range(NCT)]
    H = C.sb([P, NHT, SPAN], BF16, "H", es); Hb = [Buf() for _ in range(NHT)]
    L = LNRes(C, es)
    if has_glu:
        S5O = C.sb([P, 4, NW], BF16, "S5O", es); S5Ob = [Buf() for _ in range(4)]
        sig = C.sb([P, 2, 512], F32, "sig", es); sigb = [Buf(), Buf()]
    wst = [C.sb([P, NCT, 256], F32, "wst", es) for _ in range(2)]; wstb = [[Buf(), Buf()], [Buf(), Buf()]]
    w16 = [C.sb([P, NCT, 256], BF16, "w16", es) for _ in range(2)]; w16b = [Buf(), Buf()]
    a_sb = C.sb([P, 2, NW + 2], F32, "a_sb", es); a_b = [Buf(), Buf()]
    c_sb = C.sb([P, 2, SPAN], F32, "c_sb", es); c_b = [Buf(), Buf()]
    winv = cfg["win_d"]
    mixv = cfg["mix_d"].rearrange("(ct p) n -> p ct n", p=P)
    xrv = cfg["xres_d"].rearrange("(ct p) n -> p ct n", p=P)
    xov = cfg["xo_d"].rearrange("(ct p) n -> p ct n", p=P)
    step = 0
    for sp in cfg["spans"]:
        c0, n, nin, lh, var = sp["c0"], sp["n"], sp["nin"], sp["lh"], sp["var"]
        halo = lh == 1
        ch_all = chunks_of(n)
        ch_in = chunks_of(nin)
        vs = slice(var, var + 1)
        S.dma(A0[:, :, 0:n], xrv[:, :, c0:c0 + n], writes=A0b)
        S.dma(A1[:, :, 0:n], mixv[:, :, c0:c0 + n], writes=A1b)
        for ct in range(NCT):
            S.op("pool", lambda e, ct=ct: e.tensor_copy(out=B0[:, ct, 0:n], in_=A1[:, ct, 0:n]), reads=[A1b[ct]], writes=[B0b[ct]])
        if has_glu:
            k = 0
            for ot in range(4):
                for (q0, qn) in ch_all:
                    pt, pb = C.ps()
                    for kt in range(4):
                        S.op("pe", lambda e, kt=kt, ot=ot, q0=q0, qn=qn, pt=pt: e.matmul(
                            pt[:, :qn], lhsT=wglub[:, kt, ot * P:(ot + 1) * P], rhs=B0[:, 4 + kt, q0:q0 + qn],
                            start=(kt == 0), stop=(kt == 3)), reads=[b_wglu, B0b[4 + kt]], writes=[pb])
                    sg, sgb = sig[:, k % 2, :qn], sigb[k % 2]
                    k += 1
                    S.op("act", lambda e, sg=sg, pt=pt, qn=qn: e.activation(out=sg, in_=pt[:, :qn], func=AF.Sigmoid), reads=[pb], writes=[sgb])
                    S.op("dve", lambda e, sg=sg, ot=ot, q0=q0, qn=qn: e.tensor_tensor(
                        out=S5O[:, ot, q0:q0 + qn], in0=A1[:, 4 + ot, q0:q0 + qn], in1=sg, op=ALU.mult),
                        reads=[sgb, A1b[4 + ot]], writes=[S5Ob[ot]])
            opnd = [(B0, kt, B0b[kt]) for kt in range(4)] + [(S5O, kt, S5Ob[kt]) for kt in range(4)]
        else:
            opnd = [(B0, kt, B0b[kt]) for kt in range(NCT)]
        for ct in range(NCT):
            S.op("act", lambda e, ct=ct: e.mul(out=A0[:, ct, 0:n], in_=A0[:, ct, 0:n], mul=DN_ALPHA), reads=[A0b[ct]], writes=[A0b[ct]])
        for (q0, qn) in ch_all:
            for ot in range(NCT):
                pt, pb = C.ps()
                for kt in range(NCT):
                    T, ti, tb = opnd[kt]
                    S.op("pe", lambda e, kt=kt, ot=ot, T=T, ti=ti, pt=pt, q0=q0, qn=qn: e.matmul(
                        pt[:, :qn], lhsT=woutb[:, kt, ot * P:(ot + 1) * P], rhs=T[:, ti, q0:q0 + qn],
                        start=(kt == 0), stop=(kt == NCT - 1)), reads=[b_wout, tb], writes=[pb])
                S.op("dve", lambda e, ot=ot, pt=pt, q0=q0, qn=qn: e.scalar_tensor_tensor(
                    out=A0[:, ot, q0:q0 + qn], in0=pt[:, :qn], scalar=MODS[:, 16 + ot, vs], in1=A0[:, ot, q0:q0 + qn],
                    op0=ALU.mult, op1=ALU.add), reads=[pb, b_mods, A0b[ot]], writes=[A0b[ot]])
            emit_ln(C, L, A0, A0b, q0, qn, lng, lnb, b_lnp, 0)
        for ct in range(NCT):
            S.op("act", lambda e, ct=ct: e.activation(out=B0[:, ct, 0:n], in_=A0[:, ct, 0:n], func=AF.Identity,
                                                      scale=M1P[:, 32 + ct, vs], bias=MODS[:, 24 + ct, vs]),
                 reads=[A0b[ct], b_m1p, b_mods], writes=[B0b[ct]])
        off = 0 if halo else 1
        for ht in range(NHT):
            par = step % 2
            step += 1
            S.dma(wst[par][:], winv[ht, :, :, :], writes=wstb[par])
            S.op("act", lambda e, par=par: e.copy(out=w16[par][:], in_=wst[par][:]), reads=wstb[par], writes=[w16b[par]])
            av = a_sb[:, par, :]
            for (q0, qn) in ch_all:
                pt, pb = C.ps()
                for kt in range(NCT):
                    S.op("pe", lambda e, kt=kt, par=par, pt=pt, q0=q0, qn=qn: e.matmul(
                        pt[:, :qn], lhsT=w16[par][:, kt, 0:P], rhs=B0[:, kt, q0:q0 + qn],
                        start=(kt == 0), stop=(kt == NCT - 1)), reads=[w16b[par], B0b[kt]], writes=[pb])
                S.op("act", lambda e, par=par, pt=pt, q0=q0, qn=qn: e.copy(out=a_sb[:, par, off + q0:off + q0 + qn], in_=pt[:, :qn]),
                     reads=[pb], writes=[a_b[par]])
            if not halo:
                S.op("pool", lambda e, par=par: e.memset(a_sb[:, par, 0:1], 0.0), writes=[a_b[par]])
                S.op("pool", lambda e, par=par: e.memset(a_sb[:, par, nin + 1:nin + 2], 0.0), writes=[a_b[par]])
            if sp.get("maskL"):
                S.op("dve", lambda e, par=par: e.tensor_scalar(out=a_sb[:, par, 0:1], in0=a_sb[:, par, 0:1], scalar1=hmask[:, 0:1],
                                                               scalar2=None, op0=ALU.mult), reads=[a_b[par], b_hm], writes=[a_b[par]])
            if sp.get("maskR"):
                S.op("dve", lambda e, par=par: e.tensor_scalar(out=a_sb[:, par, nin + 1:nin + 2], in0=a_sb[:, par, nin + 1:nin + 2],
                                                               scalar1=hmask[:, 1:2], scalar2=None, op0=ALU.mult),
                     reads=[a_b[par], b_hm], writes=[a_b[par]])
            cv = c_sb[:, par, 0:nin]
            S.op("dve", lambda e, par=par, ht=ht, cv=cv: e.tensor_scalar(
                out=cv, in0=a_sb[:, par, 1:nin + 1], scalar1=convw[:, ht, 1:2], scalar2=convb[:, ht:ht + 1],
                op0=ALU.mult, op1=ALU.add), reads=[a_b[par], b_conv], writes=[c_b[par]])
            S.op("dve", lambda e, par=par, ht=ht, cv=cv: e.scalar_tensor_tensor(
                out=cv, in0=a_sb[:, par, 0:nin], scalar=convw[:, ht, 0:1], in1=cv, op0=ALU.mult, op1=ALU.add),
                reads=[a_b[par], b_conv, c_b[par]], writes=[c_b[par]])
            S.op("dve", lambda e, par=par, ht=ht, cv=cv: e.scalar_tensor_tensor(
                out=cv, in0=a_sb[:, par, 2:nin + 2], scalar=convw[:, ht, 2:3], in1=cv, op0=ALU.mult, op1=ALU.add),
                reads=[a_b[par], b_conv, c_b[par]], writes=[c_b[par]])
            S.op("act", lambda e, cv=cv: e.activation(out=cv, in_=cv, func=AF.Silu), reads=[c_b[par]], writes=[c_b[par]])
            for (q0, qn) in ch_in:
                pt, pb = C.ps()
                for kt in range(NCT):
                    S.op("pe", lambda e, kt=kt, par=par, pt=pt, q0=q0, qn=qn: e.matmul(
                        pt[:, :qn], lhsT=w16[par][:, kt, P:2 * P], rhs=B0[:, kt, lh + q0:lh + q0 + qn],
                        start=(kt == 0), stop=(kt == NCT - 1)), reads=[w16b[par], B0b[kt]], writes=[pb])
                S.op("dve", lambda e, par=par, ht=ht, pt=pt, q0=q0, qn=qn: e.tensor_tensor(
                    out=H[:, ht, q0:q0 + qn], in0=c_sb[:, par, q0:q0 + qn], in1=pt[:, :qn], op=ALU.mult),
                    reads=[c_b[par], pb], writes=[Hb[ht]])
        for (q0, qn) in ch_in:
            for ot in range(NCT):
                S.op("act", lambda e, ot=ot, q0=q0, qn=qn: e.mul(out=A1[:, ot, q0:q0 + qn], in_=A0[:, ot, lh + q0:lh + q0 + qn], mul=DN_ALPHA),
                     reads=[A0b[ot]], writes=[A1b[ot]])
                pt, pb = C.ps()
                for ht in range(NHT):
                    S.op("pe", lambda e, ht=ht, ot=ot, pt=pt, q0=q0, qn=qn: e.matmul(
                        pt[:, :qn], lhsT=wob[:, ht, ot * P:(ot + 1) * P], rhs=H[:, ht, q0:q0 + qn],
                        start=(ht == 0), stop=(ht == NHT - 1)), reads=[b_wob[ht // 2], Hb[ht]], writes=[pb])
                S.op("dve", lambda e, ot=ot, pt=pt, q0=q0, qn=qn: e.scalar_tensor_tensor(
                    out=A1[:, ot, q0:q0 + qn], in0=pt[:, :qn], scalar=MODS[:, 40 + ot, vs], in1=A1[:, ot, q0:q0 + qn],
                    op0=ALU.mult, op1=ALU.add), reads=[pb, b_mods, A1b[ot]], writes=[A1b[ot]])
            emit_ln(C, L, A1, A1b, q0, qn, lng, lnb, b_lnp, 1)
        S.dma(xov[:, :, sp["o0"]:sp["o0"] + nin], A1[:, :, 0:nin], reads=A1b, is_output=True)
    return es


def lay_vec_cols(v, ncol):
    return np.ascontiguousarray(np.asarray(v, np.float32).reshape(ncol, P).T)


def lay_ln(ln_g_l):
    return np.ascontiguousarray(np.asarray(ln_g_l, np.float32).reshape(2, NCT, P).transpose(2, 0, 1))


def lay_conv(conv_w_l):
    return np.ascontiguousarray(np.asarray(conv_w_l, np.float32).reshape(3, NHT, P).transpose(2, 1, 0))


def lay_mods(mod_l_b, mod_c):
    return np.ascontiguousarray(np.stack([lay_vec_cols(mod_l_b, 48), lay_vec_cols(mod_c, 48)], axis=-1))


def lay_ffn_win(w_in_l):
    w = np.asarray(w_in_l, np.float32)
    a = w[:, :FFN_H].reshape(NCT, P, NHT, P)
    g = w[:, FFN_H:].reshape(NCT, P, NHT, P)
    ag = np.concatenate([a, g], axis=3)
    return np.ascontiguousarray(ag.transpose(2, 1, 0, 3))


def lat_slices(fullT, j):
    pad = np.pad(fullT, ((0, 0), (1, 1)))
    return np.ascontiguousarray(pad[:, QTR * j:QTR * j + QTR + 2])


def hmask_for(j):
    m = np.ones((P, 2), np.float32)
    if j == 0:
        m[:, 0] = 0.0
    if j == 3:
        m[:, 1] = 0.0
    return m


def lat_spans(base, obase):
    sp = []
    for i in range(QTR // SPAN):
        sp.append(dict(c0=base + SPAN * i, n=SPAN + 2, nin=SPAN, lh=1, var=0, maskL=(i == 0), maskR=(i == QTR // SPAN - 1),
                       o0=obase + SPAN * i))
    return sp


def build_phase4():
    nc = bass.Bass("TRN2", target_bir_lowering=False)
    C = Ctx(nc)
    C.init_psum()
    S = C.S
    d = {}
    d["mix_d"] = C.dram_in("mix", [D_MODEL, QTR + 2])
    d["xres_d"] = C.dram_in("xres", [D_MODEL, QTR + 2])
    d["xo_d"] = C.dram_out("xo", [D_MODEL, QTR])
    d["wout_d"] = C.dram_in("wout", [D_MODEL, D_MODEL])
    d["wglu_d"] = None
    d["lng_d"] = C.dram_in("lng", [P, 2, NCT])
    d["lnb_d"] = C.dram_in("lnb", [P, 2, NCT])
    d["win_d"] = C.dram_in("win", [NHT, P, NCT, 256])
    d["convw_d"] = C.dram_in("convw", [P, NHT, 3])
    d["convb_d"] = C.dram_in("convb", [P, NHT])
    d["wo_d"] = C.dram_in("wo", [FFN_H, D_MODEL])
    d["hmask_d"] = C.dram_in("hmask", [P, 2])
    mods_d = C.dram_in("mods", [P, 48, 2])
    MODS = C.sb([P, 48, 2], F32, "MODS")
    b_mods = Buf()
    S.dma(MODS[:], mods_d[:, :, :], writes=[b_mods])
    d["MODS"], d["b_mods"] = MODS, b_mods
    d["spans"] = lat_spans(0, 0)
    es = emit_tokenlocal(C, d)
    S.finish()
    es.close()
    C.es.close()
    return nc


def phase4_inputs(b, j, ogT_b, x2T_b, mods1_b, inp):
    return {
        "mix": lat_slices(ogT_b, j), "xres": lat_slices(x2T_b, j),
        "wout": np.ascontiguousarray(inp["hg_w_out"][0]), "lng": lay_ln(inp["ln_g"][1]), "lnb": lay_ln(inp["ln_b"][1]),
        "win": lay_ffn_win(inp["ffn_w_in"][1]), "convw": lay_conv(inp["ffn_conv_w"][1]),
        "convb": lay_vec_cols(inp["ffn_conv_b"][1], NHT), "wo": np.ascontiguousarray(inp["ffn_w_out"][1]),
        "hmask": hmask_for(j), "mods": mods1_b,
    }


HG_CH = 64
BLK = 512


def ap_chunk_last(t2, n, bcast):
    nchk = n // HG_CH
    if bcast:
        return AP(t2.tensor, t2.offset + HG_CH - 1, [list(t2.ap[0]), [HG_CH, nchk], [0, HG_CH]])
    return AP(t2.tensor, t2.offset + HG_CH - 1, [list(t2.ap[0]), [HG_CH, nchk]])


def ap_3d(t2, n):
    return AP(t2.tensor, t2.offset, [list(t2.ap[0]), [HG_CH, n // HG_CH], [1, HG_CH]])


def build_phase3():
    nc = bass.Bass("TRN2", target_bir_lowering=False)
    C = Ctx(nc)
    C.init_psum()
    S = C.S
    es = C.es
    x_d = C.dram_in("xT", [D_MODEL, LTOT])
    w_d = C.dram_in("w3", [D_MODEL, 1280])
    mods_d = C.dram_in("mods", [P, 48, 2])
    lb_d = C.dram_in("hglb", [P, 2, 4])
    ng_d = C.dram_in("hgnorm", [P, 1])
    cst_d = C.dram_in("cst3", [P, 4, 128])
    rm_d = C.dram_in("rmask", [P, BLK])
    og_d = C.dram_out("ogT", [2 * P, SEQ])

    MODS = C.sb([P, 48, 2], F32, "MODS"); b_mods = Buf()
    M1P = C.sb([P, 48, 2], F32, "M1P"); b_m1p = Buf()
    S.dma(MODS[:], mods_d[:, :, :], writes=[b_mods])
    S.op("dve", lambda e: e.tensor_scalar(out=M1P[:], in0=MODS[:], scalar1=1.0, scalar2=None, op0=ALU.add), reads=[b_mods], writes=[b_m1p])
    cst = C.sb([P, 4, 128], F32, "cst"); b_cst = Buf()
    S.dma(cst[:], cst_d[:, :, :], writes=[b_cst])
    identB = C.sb([P, P], BF16, "identB"); b_id = Buf()
    S.op("dve", lambda e: e.tensor_copy(out=identB[:], in_=cst[:, 0, :]), reads=[b_cst], writes=[b_id])
    onesM = C.sb([P, P], F32, "ones3"); b_ones = Buf()
    S.op("pool", lambda e: e.memset(onesM[:], 1.0 / P), writes=[b_ones])
    rmask = C.sb([P, BLK], F32, "rmask"); b_rm = Buf()
    S.dma(rmask[:], rm_d[:, :], writes=[b_rm])
    ng = C.sb([P, 1], F32, "ng"); b_ng = Buf()
    S.dma(ng[:], ng_d[:, :], writes=[b_ng])
    lbr = C.sb([P, 2, 4], F32, "lbr"); b_lbr = Buf()
    S.dma(lbr[:], lb_d[:, :, :], writes=[b_lbr])
    LB = C.sb([P, 4], F32, "LB"); OML = C.sb([P, 4], F32, "OML"); b_lb = Buf()
    den = C.sb([P, 4], F32, "den"); b_den = Buf()
    S.op("act", lambda e: e.activation(out=lbr[:], in_=lbr[:], func=AF.Exp), reads=[b_lbr], writes=[b_lbr])
    S.op("dve", lambda e: e.tensor_tensor(out=den[:], in0=lbr[:, 0, :], in1=lbr[:, 1, :], op=ALU.add), reads=[b_lbr], writes=[b_den])
    S.op("dve", lambda e: e.reciprocal(out=den[:], in_=den[:]), reads=[b_den], writes=[b_den])
    S.op("dve", lambda e: e.tensor_tensor(out=LB[:], in0=lbr[:, 1, :], in1=den[:], op=ALU.mult), reads=[b_lbr, b_den], writes=[b_lb])
    S.op("dve", lambda e: e.tensor_scalar(out=OML[:], in0=LB[:], scalar1=-1.0, scalar2=1.0, op0=ALU.mult, op1=ALU.add), reads=[b_lb], writes=[b_lb])
    Wb = C.sb([P, NCT, 1280], BF16, "W3b"); b_w = Buf()
    with contextlib.ExitStack() as es_t:
        wv = w_d.rearrange("(kt p) n -> p kt n", p=P)
        for q in range(4):
            load_cast(C, Wb[:, 2 * q:2 * q + 2, :], b_w, wv[:, 2 * q:2 * q + 2, :], [P, 2, 1280], es_t)
        S.barrier()
    OF = C.sb([P, 2, SEQ], F32, "OF"); OFb = [[Buf() for _ in range(SEQ // BLK)] for _ in range(2)]
    xs = C.sb([P, NCT, BLK], F32, "xs"); xsb = Buf()
    ub = [C.sb([P, NCT, BLK], BF16, "ub") for _ in range(2)]; ubb = [Buf(), Buf()]
    NT_ = 8
    T = [[C.sb([P, BLK], F32, "T%d" % i) for i in range(NT_)] for _ in range(2)]
    Tb = [[Buf() for _ in range(NT_)] for _ in range(2)]
    QE = [C.sb([P, BLK], BF16, "QE") for _ in range(2)]; KE = [C.sb([P, BLK], BF16, "KE") for _ in range(2)]
    KD = [C.sb([P, BLK], BF16, "KD") for _ in range(2)]
    QEb = [Buf(), Buf()]; KEb = [Buf(), Buf()]; KDb = [Buf(), Buf()]
    VT = [C.sb([P, BLK], BF16, "VT") for _ in range(2)]; VTb = [Buf(), Buf()]
    KDT = [C.sb([P, BLK], BF16, "KDT") for _ in range(2)]; KDTb = [Buf(), Buf()]
    SCT = [C.sb([P, 2, P], BF16, "SCT") for _ in range(2)]; SCTb = [[Buf(), Buf()], [Buf(), Buf()]]
    EL = [C.sb([P, 8], F32, "EL") for _ in range(2)]; ELb = [Buf(), Buf()]
    Sf = [C.sb([P, P], F32, "Sf") for _ in range(2)]; Sfb = [Buf(), Buf()]
    Sb = [C.sb([P, P], BF16, "Sb") for _ in range(2)]; Sbb = [Buf(), Buf()]
    xv = x_d.rearrange("(ct p) n -> p ct n", p=P)
    nblk = SEQ // BLK
    bi = 0
    C.ps_n = 6
    C.ps_i = 0
    for d in range(2):
        for hh in range(2):
            S.op("dve", lambda e, hh=hh: e.memset(Sf[hh][:], 0.0), writes=[Sfb[hh]])
            S.op("dve", lambda e, hh=hh: e.memset(Sb[hh][:], 0.0), writes=[Sbb[hh]])
        blocks = [(0, CTX, 1, None)]
        order = range(nblk) if d == 0 else range(nblk - 1, -1, -1)
        blocks += [(CTX + BLK * i, BLK, 0, i) for i in order]
        mask = cst[:, 1 + d, :]
        for (c0, n, var, li) in blocks:
            u = ub[bi % 2]; u_b = ubb[bi % 2]
            bi += 1
            vs = slice(var, var + 1)
            ntile = n // P
            S.dma(xs[:, :, 0:n], xv[:, :, c0:c0 + n], writes=[xsb])
            for ct in range(NCT):
                S.op("act", lambda e, ct=ct, u=u: e.activation(out=u[:, ct, 0:n], in_=xs[:, ct, 0:n], func=AF.Identity,
                                                              scale=M1P[:, 8 + ct, vs], bias=MODS[:, ct, vs]),
                     reads=[xsb, b_m1p, b_mods], writes=[u_b])
            hst = {}
            for hh in range(2):
                Th, Tbh = T[hh], Tb[hh]
                wc = hh * 640
                qps, qpb = C.ps()
                fps, fpb = C.ps()
                vps, vpb = C.ps()
                for kt in range(NCT):
                    S.op("pe", lambda e, kt=kt: e.matmul(qps[:, :n], lhsT=Wb[:, kt, wc:wc + P], rhs=u[:, kt, 0:n],
                                                         start=(kt == 0), stop=(kt == NCT - 1)), reads=[b_w, u_b], writes=[qpb])
                fo = wc + P * (1 + d)
                for kt in range(NCT):
                    S.op("pe", lambda e, kt=kt: e.matmul(fps[:, :n], lhsT=Wb[:, kt, fo:fo + P], rhs=u[:, kt, 0:n],
                                                         start=(kt == 0), stop=(kt == NCT - 1)), reads=[b_w, u_b], writes=[fpb])
                for tt in range(ntile):
                    for kt in range(NCT):
                        S.op("pe", lambda e, kt=kt, tt=tt: e.matmul(vps[:, tt * P:(tt + 1) * P], lhsT=u[:, kt, tt * P:(tt + 1) * P],
                                                                    rhs=Wb[:, kt, wc + 3 * P:wc + 4 * P],
                                                                    start=(kt == 0), stop=(kt == NCT - 1)), reads=[b_w, u_b], writes=[vpb])
                need_o = li is not None
                if need_o and d == 1:
                    gps, gpb = C.ps()
                    for kt in range(NCT):
                        S.op("pe", lambda e, kt=kt: e.matmul(gps[:, :n], lhsT=Wb[:, kt, wc + 4 * P:wc + 5 * P], rhs=u[:, kt, 0:n],
                                                             start=(kt == 0), stop=(kt == NCT - 1)), reads=[b_w, u_b], writes=[gpb])
                f_, lf, kk, cum, E, X6, X7, G = [Th[i][:, 0:n] for i in range(8)]
                bf, blf, bk, bcum, bE, b6, b7, bG = Tbh
                li4 = d * 2 + hh
                S.op("act", lambda e: e.activation(out=f_, in_=fps[:, :n], func=AF.Sigmoid), reads=[fpb], writes=[bf])
                S.op("dve", lambda e: e.tensor_scalar(out=f_, in0=f_, scalar1=OML[:, li4:li4 + 1], scalar2=LB[:, li4:li4 + 1],
                                                      op0=ALU.mult, op1=ALU.add), reads=[bf, b_lb], writes=[bf])
                S.op("act", lambda e: e.activation(out=lf, in_=f_, func=AF.Ln), reads=[bf], writes=[blf])
                S.op("dve", lambda e: e.tensor_scalar(out=kk, in0=f_, scalar1=-1.0, scalar2=1.0, op0=ALU.mult, op1=ALU.add),
                     reads=[bf], writes=[bk])
                S.op("dve", lambda e: e.tensor_tensor_scan(out=cum, data0=rmask[:, 0:n], data1=lf, initial=0.0, op0=ALU.mult, op1=ALU.add),
                     reads=[b_rm, blf], writes=[bcum])
                last_b = ap_chunk_last(cum, n, True)
                last_s = ap_chunk_last(cum, n, False)
                nchk = n // HG_CH
                S.op("act", lambda e: e.activation(out=EL[hh][:, 0:nchk], in_=last_s, func=AF.Exp), reads=[bcum], writes=[ELb[hh]])
                if d == 0:
                    cq = cum
                    bcq = bcum
                else:
                    S.op("dve", lambda e: e.tensor_tensor(out=ap_3d(X6, n), in0=last_b, in1=ap_3d(lf, n), op=ALU.add),
                         reads=[bcum, blf], writes=[b6])
                    S.op("dve", lambda e: e.tensor_tensor(out=X6, in0=X6, in1=cum, op=ALU.subtract), reads=[b6, bcum], writes=[b6])
                    cq = X6
                    bcq = b6
                if need_o:
                    S.op("act", lambda e: e.activation(out=E, in_=cq, func=AF.Exp), reads=[bcq], writes=[bE])
                    S.op("dve", lambda e: e.tensor_tensor(out=QE[hh][:, 0:n], in0=qps[:, :n], in1=E, op=ALU.mult),
                         reads=[qpb, bE], writes=[QEb[hh]])
                    S.op("act", lambda e: e.activation(out=E, in_=cq, func=AF.Exp, scale=-1.0), reads=[bcq, QEb[hh]], writes=[bE])
                    S.op("dve", lambda e: e.tensor_tensor(out=KE[hh][:, 0:n], in0=kk, in1=E, op=ALU.mult),
                         reads=[bk, bE], writes=[KEb[hh]])
                if d == 0:
                    S.op("dve", lambda e: e.tensor_tensor(out=ap_3d(X7, n), in0=last_b, in1=ap_3d(cum, n), op=ALU.subtract),
                         reads=[bcum], writes=[b7])
                else:
                    S.op("dve", lambda e: e.tensor_tensor(out=X7, in0=cum, in1=lf, op=ALU.subtract), reads=[bcum, blf], writes=[b7])
                S.op("act", lambda e: e.activation(out=X7, in_=X7, func=AF.Exp), reads=[b7], writes=[b7])
                S.op("dve", lambda e: e.tensor_tensor(out=KD[hh][:, 0:n], in0=kk, in1=X7, op=ALU.mult), reads=[bk, b7], writes=[KDb[hh]])
                S.op("act", lambda e: e.copy(out=VT[hh][:, 0:n], in_=vps[:, 0:n]), reads=[vpb], writes=[VTb[hh]])
                tps, tpb = C.ps()
                for tt in range(ntile):
                    S.op("pe", lambda e, tt=tt: e.matmul(tps[:, tt * P:(tt + 1) * P], lhsT=KD[hh][:, tt * P:(tt + 1) * P], rhs=identB[:],
                                                         start=True, stop=True), reads=[KDb[hh], b_id], writes=[tpb])
                S.op("dve", lambda e: e.tensor_copy(out=KDT[hh][:, 0:n], in_=tps[:, 0:n]), reads=[tpb], writes=[KDTb[hh]])
                if need_o and d == 1:
                    S.op("act", lambda e: e.activation(out=G, in_=gps[:, :n], func=AF.Silu), reads=[gpb], writes=[bG])
                hst[hh] = dict(X6=X6, X7=X7, G=G, b6=b6, b7=b7, bG=bG)
            need_o = li is not None
            tiles = range(ntile) if d == 0 else range(ntile - 1, -1, -1)
            chs = (0, 1) if d == 0 else (1, 0)
            for tt in tiles:
                cs = tt * P
                tl = {}
                if need_o:
                    for hh in range(2):
                        sps, spb = C.ps()
                        ops_, opb = C.ps_fixed(6 + hh)
                        sct, sctb = SCT[hh][:, tt % 2, :], SCTb[hh][tt % 2]
                        S.op("pe", lambda e: e.matmul(sps[:, 0:P], lhsT=KE[hh][:, cs:cs + P], rhs=QE[hh][:, cs:cs + P],
                                                      start=True, stop=True), reads=[KEb[hh], QEb[hh]], writes=[spb])
                        S.op("dve", lambda e: e.tensor_tensor(out=sct, in0=sps[:, 0:P], in1=mask, op=ALU.mult),
                             reads=[spb, b_cst], writes=[sctb])
                        tl[hh] = (ops_, opb, sct, sctb)
                    for hh in range(2):
                        ops_, opb, sct, sctb = tl[hh]
                        S.op("pe", lambda e: e.matmul(ops_[:, 0:P], lhsT=VT[hh][:, tt * P:(tt + 1) * P], rhs=sct, start=True, stop=False),
                             reads=[VTb[hh], sctb], writes=[opb])
                for ci, c in enumerate(chs):
                    col = cs + c * HG_CH
                    gch = col // HG_CH
                    kl = {}
                    for hh in range(2):
                        if need_o:
                            ops_, opb, sct, sctb = tl[hh]
                            S.op("pe", lambda e: e.matmul(ops_[:, c * HG_CH:(c + 1) * HG_CH], lhsT=Sb[hh][:],
                                                          rhs=QE[hh][:, col:col + HG_CH], start=False, stop=(ci == 1)),
                                 reads=[Sbb[hh], QEb[hh]], writes=[opb])
                        kps, kpb = C.ps()
                        S.op("pe", lambda e: e.matmul(kps[:, 0:P], lhsT=KDT[hh][c * HG_CH:(c + 1) * HG_CH, tt * P:(tt + 1) * P],
                                                      rhs=VT[hh][c * HG_CH:(c + 1) * HG_CH, tt * P:(tt + 1) * P], start=True, stop=True),
                             reads=[KDTb[hh], VTb[hh]], writes=[kpb])
                        kl[hh] = (kps, kpb)
                    for hh in range(2):
                        kps, kpb = kl[hh]
                        S.op("dve", lambda e: e.scalar_tensor_tensor(out=Sf[hh][:], in0=Sf[hh][:], scalar=EL[hh][:, gch:gch + 1],
                                                                      in1=kps[:, 0:P], op0=ALU.mult, op1=ALU.add),
                             reads=[Sfb[hh], ELb[hh], kpb], writes=[Sfb[hh]])
                        S.op("act", lambda e: e.copy(out=Sb[hh][:], in_=Sf[hh][:]), reads=[Sfb[hh]], writes=[Sbb[hh]])
                if need_o:
                    oc = li * BLK + cs
                    for hh in range(2):
                        ops_, opb, sct, sctb = tl[hh]
                        if d == 0:
                            S.op("act", lambda e: e.copy(out=OF[:, hh, oc:oc + P], in_=ops_[:, 0:P]), reads=[opb], writes=[OFb[hh][li]])
                        else:
                            S.op("dve", lambda e: e.tensor_tensor(out=OF[:, hh, oc:oc + P], in0=ops_[:, 0:P], in1=OF[:, hh, oc:oc + P],
                                                                  op=ALU.add), reads=[opb, OFb[hh][li]], writes=[OFb[hh][li]])
            if need_o and d == 1:
                for hh in range(2):
                    X6, X7, G, b6, b7, bG = [hst[hh][k_] for k_ in ("X6", "X7", "G", "b6", "b7", "bG")]
                    o_ = OF[:, hh, li * BLK:(li + 1) * BLK]
                    ob = OFb[hh][li]
                    mps, mpb = C.ps()
                    S.op("act", lambda e: e.activation(out=X6, in_=o_, func=AF.Square), reads=[ob], writes=[b6])
                    S.op("pe", lambda e: e.matmul(mps[:, :n], lhsT=onesM[:], rhs=X6, start=True, stop=True), reads=[b_ones, b6], writes=[mpb])
                    S.op("act", lambda e: e.activation(out=X7, in_=mps[:, :n], func=AF.Sqrt, bias=NORM_EPS), reads=[mpb], writes=[b7])
                    S.op("dve", lambda e: e.reciprocal(out=X7, in_=X7), reads=[b7], writes=[b7])
                    S.op("dve", lambda e: e.tensor_tensor(out=X6, in0=o_, in1=X7, op=ALU.mult), reads=[ob, b7], writes=[b6])
                    S.op("dve", lambda e: e.scalar_tensor_tensor(out=X6, in0=X6, scalar=ng[:, 0:1], in1=G, op0=ALU.mult, op1=ALU.mult),
                         reads=[b6, b_ng, bG], writes=[b6])
                    S.dma(og_d[hh * P:(hh + 1) * P, li * BLK:(li + 1) * BLK], X6, reads=[b6], q="pool", is_output=True)
    S._wait("sp", [])
    S._wait("pool", S.out_deps)
    S.finish()
    C.es.close()
    return nc


def tri_masks():
    m = np.zeros((P, 4, P), np.float32)
    m[:, 0, :] = np.eye(P, dtype=np.float32)
    s = np.arange(P)[:, None]
    t = np.arange(P)[None, :]
    same = (s // HG_CH) == (t // HG_CH)
    m[:, 1, :] = (same & (s <= t)).astype(np.float32)
    m[:, 2, :] = (same & (s >= t)).astype(np.float32)
    return m


def reset_mask():
    r = np.ones((P, BLK), np.float32)
    r[:, ::HG_CH] = 0.0
    return r


def phase3_inputs(b, j, x2T_full_b, mods1_b, inp):
    hg_w = np.asarray(inp["hg_w_in"][0])
    cols = []
    for hh in range(2):
        h = 2 * j + hh
        for part in range(5):
            cols.append(hg_w[:, part * 1024 + h * P: part * 1024 + (h + 1) * P])
    w3 = np.ascontiguousarray(np.concatenate(cols, axis=1))
    lb = np.asarray(inp["hg_lb"])
    hglb = np.zeros((P, 2, 4), np.float32)
    for d in range(2):
        for hh in range(2):
            h = 2 * j + hh
            hglb[:, :, d * 2 + hh] = lb[:, d, h * P:(h + 1) * P].T
    return {"xT": np.ascontiguousarray(x2T_full_b), "w3": w3, "mods": mods1_b, "hglb": hglb,
            "hgnorm": np.ascontiguousarray(np.asarray(inp["hg_norm"][0]).reshape(P, 1)), "cst3": tri_masks(), "rmask": reset_mask()}


NT2 = CTX + QTR + 2


def build_phase2():
    nc = bass.Bass("TRN2", target_bir_lowering=False)
    C = Ctx(nc)
    C.init_psum()
    S = C.S
    d = {}
    d["mix_d"] = C.dram_in("mix", [D_MODEL, NT2])
    d["xres_d"] = C.dram_in("xres", [D_MODEL, NT2])
    d["xo_d"] = C.dram_out("xo", [D_MODEL, CTX + QTR])
    d["wout_d"] = C.dram_in("wout", [D_MODEL, D_MODEL])
    d["wglu_d"] = C.dram_in("wglu", [512, 512])
    d["lng_d"] = C.dram_in("lng", [P, 2, NCT])
    d["lnb_d"] = C.dram_in("lnb", [P, 2, NCT])
    d["win_d"] = C.dram_in("win", [NHT, P, NCT, 256])
    d["convw_d"] = C.dram_in("convw", [P, NHT, 3])
    d["convb_d"] = C.dram_in("convb", [P, NHT])
    d["wo_d"] = C.dram_in("wo", [FFN_H, D_MODEL])
    d["hmask_d"] = C.dram_in("hmask", [P, 2])
    mods_d = C.dram_in("mods", [P, 48, 2])
    cS_d = C.dram_in("cS", [P, 8, 2])
    adaw_d = C.dram_in("adaw", [D_MODEL, 6 * D_MODEL])
    adab_d = C.dram_in("adab", [P, 48])
    mods1_d = C.dram_out("mods1", [P, 48, 2])
    MODS = C.sb([P, 48, 2], F32, "MODS")
    b_mods = Buf()
    S.dma(MODS[:], mods_d[:, :, :], writes=[b_mods])
    MODS1 = C.sb([P, 48, 2], F32, "MODS1")
    b_mods1 = Buf()
    emit_mods(C, cS_d, adaw_d, adab_d, MODS1, b_mods1)
    S.dma(mods1_d[:, :, :], MODS1[:], reads=[b_mods1], is_output=True)
    d["MODS"], d["b_mods"] = MODS, b_mods
    d["spans"] = [dict(c0=0, n=CTX, nin=CTX, lh=0, var=1, o0=0)] + lat_spans(CTX, CTX)
    es = emit_tokenlocal(C, d)
    S.finish()
    es.close()
    C.es.close()
    return nc


def lay_cS(c_b, c_ctx):
    return np.ascontiguousarray(np.stack([lay_vec_cols(c_b, 8), lay_vec_cols(c_ctx, 8)], axis=-1))


def phase2_inputs(b, j, attzT_b, xresT_b, mods0_b, inp):
    def cols(fullT):
        return np.ascontiguousarray(np.concatenate([fullT[:, :CTX], lat_slices(fullT[:, CTX:], j)], axis=1))
    return {
        "mix": cols(attzT_b), "xres": cols(xresT_b),
        "wout": np.ascontiguousarray(inp["ev_w_out"][0]), "wglu": np.ascontiguousarray(inp["s5_w_glu"][0]),
        "lng": lay_ln(inp["ln_g"][0]), "lnb": lay_ln(inp["ln_b"][0]),
        "win": lay_ffn_win(inp["ffn_w_in"][0]), "convw": lay_conv(inp["ffn_conv_w"][0]),
        "convb": lay_vec_cols(inp["ffn_conv_b"][0], NHT), "wo": np.ascontiguousarray(inp["ffn_w_out"][0]),
        "hmask": hmask_for(j), "mods": mods0_b,
        "cS": lay_cS(inp["c"][b], inp["c_ctx"]), "adaw": np.ascontiguousarray(inp["ada_w"][1]),
        "adab": lay_vec_cols(inp["ada_b"][1], 48),
    }


TWO_PI = float(2.0 * np.pi)
TCOL = 513


def rev_ap(t2, n):
    st = t2.ap[-1][0]
    return AP(t2.tensor, t2.offset + (n - 1) * st, [list(t2.ap[0]), [-st, n]])


def emit_sin(C, es_t, ANG, b_ang, OUT, b_out, ncol, shift):
    S = C.S
    t = C.sb([P, ncol], F32, "sr_t", es_t); bt = Buf()
    ki = C.sb([P, ncol], I32, "sr_k", es_t); bk = Buf()
    S.op("dve", lambda e: e.tensor_scalar(out=t[:], in0=ANG, scalar1=shift, scalar2=1.0 / TWO_PI, op0=ALU.add, op1=ALU.mult),
         reads=[b_ang], writes=[bt])
    S.op("dve", lambda e: e.tensor_copy(out=ki[:], in_=t[:]), reads=[bt], writes=[bk])
    S.op("dve", lambda e: e.tensor_copy(out=t[:], in_=ki[:]), reads=[bk], writes=[bt])
    S.op("dve", lambda e: e.scalar_tensor_tensor(out=t[:], in0=t[:], scalar=-TWO_PI, in1=ANG, op0=ALU.mult, op1=ALU.add),
         reads=[bt, b_ang], writes=[bt])
    S.op("dve", lambda e: e.tensor_scalar(out=t[:], in0=t[:], scalar1=shift, scalar2=-3.1415925, op0=ALU.add, op1=ALU.max),
         reads=[bt], writes=[bt])
    S.op("dve", lambda e: e.tensor_scalar(out=t[:], in0=t[:], scalar1=3.1415925, scalar2=None, op0=ALU.min), reads=[bt], writes=[bt])
    S.op("act", lambda e: e.activation(out=OUT, in_=t[:], func=AF.Sin), reads=[bt], writes=[b_out])


def build_phase1a(debug=False, nlat=SEQ // BLK):
    nc = bass.Bass("TRN2", target_bir_lowering=False)
    C = Ctx(nc)
    C.init_psum()
    S = C.S
    def dump(name, ap, bufs, shape, cast=False):
        if not debug:
            return
        o = C.dram_out("dbg_" + name, shape)
        if cast:
            tmpd = C.sb(shape, F32, "dbgc")
            tb_ = Buf()
            S.op("dve", lambda e: e.tensor_copy(out=tmpd[:], in_=ap), reads=bufs, writes=[tb_])
            S.dma(o, tmpd[:], reads=[tb_], is_output=True)
        else:
            S.dma(o, ap, reads=bufs, is_output=True)
    x_d = C.dram_in("xT", [D_MODEL, LTOT])
    cS_d = C.dram_in("cS", [P, 8, 2])
    adaw_d = C.dram_in("adaw", [D_MODEL, 6 * D_MODEL])
    adab_d = C.dram_in("adab", [P, 48])
    ws_d = C.dram_in("ws", [D_MODEL, P])
    par_d = C.dram_in("s5par", [P, 3, 8])
    bc_d = C.dram_in("s5bc", [P, 4, 8, P])
    dsk_d = C.dram_in("dskip", [P, 1])
    iota_d = C.dram_in("iota", [P, TCOL])
    id_d = C.dram_in("ident", [P, P])
    z_d = C.dram_out("zT", [P, LTOT])
    mods_o = C.dram_out("mods0", [P, 48, 2])

    MODS = C.sb([P, 48, 2], F32, "MODS"); b_mods = Buf()
    emit_mods(C, cS_d, adaw_d, adab_d, MODS, b_mods)
    S.dma(mods_o[:, :, :], MODS[:], reads=[b_mods], is_output=True)
    M1P = C.sb([P, 48, 2], F32, "M1P"); b_m1p = Buf()
    S.op("dve", lambda e: e.tensor_scalar(out=M1P[:], in0=MODS[:], scalar1=1.0, scalar2=None, op0=ALU.add), reads=[b_mods], writes=[b_m1p])

    COS = C.sb([P, 8, TCOL], F32, "COS"); SIN = C.sb([P, 8, TCOL], F32, "SIN"); b_cos = Buf(); b_sin = Buf()
    RR = C.sb([P, 8], F32, "RR"); b_rr = Buf()
    LBre = C.sb([P, 8, P], BF16, "LBre"); LBim = C.sb([P, 8, P], BF16, "LBim"); b_lb = Buf()
    CRE = C.sb([P, 8, P], BF16, "CRE"); CIMn = C.sb([P, 8, P], BF16, "CIMn"); b_c = Buf()
    dsk = C.sb([P, 1], F32, "dsk"); b_dsk = Buf()
    S.dma(dsk[:], dsk_d[:, :], writes=[b_dsk])
    wsb = C.sb([P, NCT, P], BF16, "wsb"); b_ws = Buf()
    with contextlib.ExitStack() as es_t:
        load_cast(C, wsb[:], b_ws, ws_d.rearrange("(kt p) n -> p kt n", p=P)[:, :, :], [P, NCT, P], es_t)
        par = C.sb([P, 3, 8], F32, "par", es_t); b_par = Buf()
        S.dma(par[:], par_d[:, :, :], writes=[b_par])
        bc = C.sb([P, 4, 8, P], F32, "bc", es_t); b_bc = Buf()
        S.dma(bc[:], bc_d[:, :, :, :], writes=[b_bc])
        ident = C.sb([P, P], F32, "ident", es_t); b_id = Buf()
        S.dma(ident[:], id_d[:, :], writes=[b_id])
        iota = C.sb([P, TCOL], F32, "iota", es_t); b_io = Buf()
        S.dma(iota[:], iota_d[:, :], writes=[b_io])
        sm = C.sb([P, 16, 8], F32, "sm", es_t)
        smb = [Buf() for _ in range(16)]
        DT, A_, TH, C1, S1, LRE, LIM, DEN, NR, FR, FI, NFI, T0, T1, THR, _ = [sm[:, i, :] for i in range(16)]
        bDT, bA, bTH, bC1, bS1, bLRE, bLIM, bDEN, bNR, bFR, bFI, bNFI, bT0, bT1, bTHR, _ = smb
        lre, lim, ldt = par[:, 0, :], par[:, 1, :], par[:, 2, :]
        S.op("act", lambda e: e.activation(out=DT, in_=ldt, func=AF.Exp), reads=[b_par], writes=[bDT])
        S.op("dve", lambda e: e.tensor_tensor(out=A_, in0=lre, in1=DT, op=ALU.mult), reads=[b_par, bDT], writes=[bA])
        S.op("act", lambda e: e.activation(out=RR[:], in_=A_, func=AF.Exp), reads=[bA], writes=[b_rr])
        S.op("dve", lambda e: e.tensor_tensor(out=TH, in0=lim, in1=DT, op=ALU.mult), reads=[b_par, bDT], writes=[bTH])
        emit_sin(C, es_t, TH, bTH, S1, bS1, 8, 0.0)
        emit_sin(C, es_t, TH, bTH, C1, bC1, 8, float(np.pi / 2))
        ki = C.sb([P, 8], I32, "ki8", es_t); bki = Buf()
        S.op("dve", lambda e: e.tensor_scalar(out=T0, in0=TH, scalar1=1.0 / TWO_PI, scalar2=None, op0=ALU.mult), reads=[bTH], writes=[bT0])
        S.op("dve", lambda e: e.tensor_copy(out=ki[:], in_=T0), reads=[bT0], writes=[bki])
        S.op("dve", lambda e: e.tensor_copy(out=T0, in_=ki[:]), reads=[bki], writes=[bT0])
        S.op("dve", lambda e: e.scalar_tensor_tensor(out=THR, in0=T0, scalar=-TWO_PI, in1=TH, op0=ALU.mult, op1=ALU.add),
             reads=[bT0, bTH], writes=[bTHR])
        S.op("dve", lambda e: e.tensor_tensor(out=LRE, in0=RR[:], in1=C1, op=ALU.mult), reads=[b_rr, bC1], writes=[bLRE])
        S.op("dve", lambda e: e.tensor_tensor(out=LIM, in0=RR[:], in1=S1, op=ALU.mult), reads=[b_rr, bS1], writes=[bLIM])
        S.op("dve", lambda e: e.tensor_tensor(out=DEN, in0=lre, in1=lre, op=ALU.mult), reads=[b_par], writes=[bDEN])
        S.op("dve", lambda e: e.tensor_tensor(out=T0, in0=lim, in1=lim, op=ALU.mult), reads=[b_par, bT0], writes=[bT0])
        S.op("dve", lambda e: e.tensor_tensor(out=DEN, in0=DEN, in1=T0, op=ALU.add), reads=[bDEN, bT0], writes=[bDEN])
        S.op("dve", lambda e: e.reciprocal(out=DEN, in_=DEN), reads=[bDEN], writes=[bDEN])
        S.op("dve", lambda e: e.tensor_scalar(out=NR, in0=LRE, scalar1=-1.0, scalar2=None, op0=ALU.add), reads=[bLRE], writes=[bNR])
        S.op("dve", lambda e: e.tensor_tensor(out=T0, in0=NR, in1=lre, op=ALU.mult), reads=[bNR, b_par, bT0], writes=[bT0])
        S.op("dve", lambda e: e.tensor_tensor(out=T1, in0=LIM, in1=lim, op=ALU.mult), reads=[bLIM, b_par], writes=[bT1])
        S.op("dve", lambda e: e.tensor_tensor(out=FR, in0=T0, in1=T1, op=ALU.add), reads=[bT0, bT1], writes=[bFR])
        S.op("dve", lambda e: e.tensor_tensor(out=FR, in0=FR, in1=DEN, op=ALU.mult), reads=[bFR, bDEN], writes=[bFR])
        S.op("dve", lambda e: e.tensor_tensor(out=T0, in0=LIM, in1=lre, op=ALU.mult), reads=[bLIM, b_par, bT0], writes=[bT0])
        S.op("dve", lambda e: e.tensor_tensor(out=T1, in0=NR, in1=lim, op=ALU.mult), reads=[bNR, b_par, bT1], writes=[bT1])
        S.op("dve", lambda e: e.tensor_tensor(out=FI, in0=T0, in1=T1, op=ALU.subtract), reads=[bT0, bT1], writes=[bFI])
        S.op("dve", lambda e: e.tensor_tensor(out=FI, in0=FI, in1=DEN, op=ALU.mult), reads=[bFI, bDEN], writes=[bFI])
        S.op("dve", lambda e: e.tensor_scalar(out=NFI, in0=FI, scalar1=-1.0, scalar2=None, op0=ALU.mult), reads=[bFI], writes=[bNFI])
        bb = C.sb([P, 2, P], F32, "bb", es_t); bbb = [Buf(), Buf()]
        for ti in range(8):
            bre, bim = bc[:, 0, ti, :], bc[:, 1, ti, :]
            S.op("dve", lambda e: e.tensor_scalar(out=bb[:, 0, :], in0=bre, scalar1=FR[:, ti:ti + 1], scalar2=None, op0=ALU.mult),
                 reads=[b_bc, bFR], writes=[bbb[0]])
            S.op("dve", lambda e: e.scalar_tensor_tensor(out=bb[:, 0, :], in0=bim, scalar=NFI[:, ti:ti + 1], in1=bb[:, 0, :], op0=ALU.mult, op1=ALU.add),
                 reads=[b_bc, bNFI, bbb[0]], writes=[bbb[0]])
            S.op("dve", lambda e: e.tensor_scalar(out=bb[:, 1, :], in0=bim, scalar1=FR[:, ti:ti + 1], scalar2=None, op0=ALU.mult),
                 reads=[b_bc, bFR], writes=[bbb[1]])
            S.op("dve", lambda e: e.scalar_tensor_tensor(out=bb[:, 1, :], in0=bre, scalar=FI[:, ti:ti + 1], in1=bb[:, 1, :], op0=ALU.mult, op1=ALU.add),
                 reads=[b_bc, bFI, bbb[1]], writes=[bbb[1]])
            for ri, dst in ((0, LBre), (1, LBim)):
                pt, pb = C.ps()
                S.op("pe", lambda e, ri=ri, pt=pt: e.matmul(pt[:, 0:P], lhsT=bb[:, ri, :], rhs=ident[:], start=True, stop=True),
                     reads=[bbb[ri], b_id], writes=[pb])
                S.op("act", lambda e, dst=dst, pt=pt: e.copy(out=dst[:, ti, :], in_=pt[:, 0:P]), reads=[pb], writes=[b_lb])
        S.op("act", lambda e: e.copy(out=CRE[:], in_=bc[:, 2, :, :]), reads=[b_bc], writes=[b_c])
        S.op("act", lambda e: e.mul(out=CIMn[:], in_=bc[:, 3, :, :], mul=-1.0), reads=[b_bc], writes=[b_c])
        ANG = C.sb([P, 8, TCOL], F32, "ANG", es_t); b_ang = Buf()
        for ti in range(8):
            S.op("dve", lambda e: e.tensor_scalar(out=ANG[:, ti, :], in0=iota[:], scalar1=THR[:, ti:ti + 1], scalar2=None, op0=ALU.mult),
                 reads=[b_io, bTHR], writes=[b_ang])
        angf = ANG[:].rearrange("p a b -> p (a b)")
        emit_sin(C, es_t, angf, b_ang, SIN[:].rearrange("p a b -> p (a b)"), b_sin, 8 * TCOL, 0.0)
        emit_sin(C, es_t, angf, b_ang, COS[:].rearrange("p a b -> p (a b)"), b_cos, 8 * TCOL, float(np.pi / 2))
        dump("sm", sm[:].rearrange("p a b -> p (a b)"), smb, [P, 128])
        dump("ang", angf, [b_ang], [P, 8 * TCOL])
        S.barrier()

    dump("cos", COS[:].rearrange("p a b -> p (a b)"), [b_cos], [P, 8 * TCOL])
    dump("sin", SIN[:].rearrange("p a b -> p (a b)"), [b_sin], [P, 8 * TCOL])
    dump("rr", RR[:], [b_rr], [P, 8])
    dump("lbre", LBre[:].rearrange("p a b -> p (a b)"), [b_lb], [P, 8 * P], cast=True)
    dump("lbim", LBim[:].rearrange("p a b -> p (a b)"), [b_lb], [P, 8 * P], cast=True)
    dump("cre", CRE[:].rearrange("p a b -> p (a b)"), [b_c], [P, 8 * P], cast=True)
    dump("cimn", CIMn[:].rearrange("p a b -> p (a b)"), [b_c], [P, 8 * P], cast=True)
    SB16 = C.sb([P, LTOT], BF16, "SB16"); YACC = C.sb([P, LTOT], F32, "YACC")
    nseg = 1 + nlat
    segs = [(0, CTX, 1)] + [(CTX + BLK * i, BLK, 0) for i in range(nlat)]
    SBb = [Buf() for _ in range(nseg)]; YAb = [Buf() for _ in range(nseg)]
    xv = x_d.rearrange("(ct p) n -> p ct n", p=P)
    with contextlib.ExitStack() as es_p:
        xs = [C.sb([P, NCT, BLK], F32, "xs", es_p) for _ in range(2)]; xsb = [Buf(), Buf()]
        ub = [C.sb([P, NCT, BLK], BF16, "ub", es_p) for _ in range(2)]; ubb = [Buf(), Buf()]
        for si, (c0, n, var) in enumerate(segs):
            vs = slice(var, var + 1)
            x_, xb_, u, u_b = xs[si % 2], xsb[si % 2], ub[si % 2], ubb[si % 2]
            S.dma(x_[:, :, 0:n], xv[:, :, c0:c0 + n], writes=[xb_])
            for ct in range(NCT):
                S.op("act", lambda e, ct=ct: e.activation(out=u[:, ct, 0:n], in_=x_[:, ct, 0:n], func=AF.Identity,
                                                          scale=M1P[:, 8 + ct, vs], bias=MODS[:, ct, vs]),
                     reads=[xb_, b_m1p, b_mods], writes=[u_b])
            pt, pb = C.ps()
            for kt in range(NCT):
                S.op("pe", lambda e, kt=kt: e.matmul(pt[:, :n], lhsT=wsb[:, kt, :], rhs=u[:, kt, 0:n], start=(kt == 0), stop=(kt == NCT - 1)),
                     reads=[b_ws, u_b], writes=[pb])
            S.op("pool", lambda e: e.memset(SB16[:, c0:c0 + 1], 0.0), writes=[SBb[si]]) if False else None
            S.op("dve", lambda e: e.tensor_copy(out=SB16[:, c0:c0 + n], in_=pt[:, :n]), reads=[pb], writes=[SBb[si]])
            S.op("dve", lambda e: e.tensor_scalar(out=YACC[:, c0:c0 + n], in0=pt[:, :n], scalar1=dsk[:, 0:1], scalar2=None, op0=ALU.mult),
                 reads=[pb, b_dsk], writes=[YAb[si]])
        S.barrier()

    dump("yacc", YACC[:], YAb, [P, LTOT])
    dump("sb16", SB16[:], SBb, [P, LTOT], cast=True)
    if debug == 1:
        S.finish()
        C.es.close()
        return nc
    NSET = 2
    W = [[C.sb([P, BLK], F32, "w%d" % i) for i in range(6)] for _ in range(NSET)]
    Wb = [[Buf() for _ in range(6)] for _ in range(NSET)]
    HB = [[C.sb([P, BLK], BF16, "hb%d" % i) for i in range(2)] for _ in range(NSET)]
    HBb = [[Buf(), Buf()] for _ in range(NSET)]
    INIT = C.sb([P, 8, 2], F32, "INIT"); INb = [Buf() for _ in range(8)]
    S.op("pool", lambda e: e.memset(INIT[:], 0.0), writes=INb)
    tn = C.sb([P, 8, 2], F32, "tn"); tnb = [Buf() for _ in range(8)]
    ZT = [C.sb([P, BLK], F32, "ZT") for _ in range(2)]; ZTb = [Buf(), Buf()]
    ZU = [C.sb([P, BLK], F32, "ZU") for _ in range(2)]; ZUb = [Buf(), Buf()]
    k = 0
    for d in range(2):
        if debug == 3 and d == 1:
            dump("yacc_f", YACC[:], YAb, [P, LTOT])
        order = [0] + ([1 + i for i in range(nlat)] if d == 0 else [1 + i for i in range(nlat - 1, -1, -1)])
        for si in order:
            c0, n, var = segs[si]
            C.ps_n = 6
            yps, ypb = C.ps_fixed(6 + (si % 2))
            for pair in range(4):
                ti = d * 4 + pair
                st = k % NSET
                k += 1
                MRE, MIM, TA, TB, GRE, GIM = [w[:, 0:n] for w in W[st]]
                bMRE, bMIM, bTA, bTB, bGRE, bGIM = Wb[st]
                HRE, HIM = HB[st][0][:, 0:n], HB[st][1][:, 0:n]
                bHRE, bHIM = HBb[st]
                xr, xrb = C.ps()
                xi, xib = C.ps()
                S.op("pe", lambda e: e.matmul(xr[:, :n], lhsT=LBre[:, ti, :], rhs=SB16[:, c0:c0 + n], start=True, stop=True),
                     reads=[b_lb, SBb[si]], writes=[xrb])
                S.op("pe", lambda e: e.matmul(xi[:, :n], lhsT=LBim[:, ti, :], rhs=SB16[:, c0:c0 + n], start=True, stop=True),
                     reads=[b_lb, SBb[si]], writes=[xib])
                cosv, sinv = COS[:, ti, 0:n], SIN[:, ti, 0:n]
                if d == 1:
                    cosv, sinv = rev_ap(cosv, n), rev_ap(sinv, n)
                S.op("dve", lambda e: e.tensor_tensor(out=MRE, in0=xr[:, :n], in1=cosv, op=ALU.mult), reads=[xrb, b_cos], writes=[bMRE])
                S.op("dve", lambda e: e.tensor_tensor(out=TA, in0=xi[:, :n], in1=sinv, op=ALU.mult), reads=[xib, b_sin], writes=[bTA])
                S.op("dve", lambda e: e.tensor_tensor(out=MRE, in0=MRE, in1=TA, op=ALU.add), reads=[bMRE, bTA], writes=[bMRE])
                S.op("dve", lambda e: e.tensor_tensor(out=MIM, in0=xi[:, :n], in1=cosv, op=ALU.mult), reads=[xib, b_cos], writes=[bMIM])
                S.op("dve", lambda e: e.tensor_tensor(out=TB, in0=xr[:, :n], in1=sinv, op=ALU.mult), reads=[xrb, b_sin], writes=[bTB])
                S.op("dve", lambda e: e.tensor_tensor(out=MIM, in0=MIM, in1=TB, op=ALU.subtract), reads=[bMIM, bTB], writes=[bMIM])
                rb = RR[:, ti:ti + 1].to_broadcast([P, n])
                if d == 0:
                    mre_s, mim_s, gre_s, gim_s = MRE, MIM, GRE, GIM
                else:
                    mre_s, mim_s, gre_s, gim_s = rev_ap(MRE, n), rev_ap(MIM, n), rev_ap(GRE, n), rev_ap(GIM, n)
                S.op("dve", lambda e: e.tensor_tensor_scan(out=gre_s, data0=rb, data1=mre_s, initial=INIT[:, ti, 0:1], op0=ALU.mult, op1=ALU.add),
                     reads=[bMRE, b_rr, INb[ti]], writes=[bGRE])
                S.op("dve", lambda e: e.tensor_tensor_scan(out=gim_s, data0=rb, data1=mim_s, initial=INIT[:, ti, 1:2], op0=ALU.mult, op1=ALU.add),
                     reads=[bMIM, b_rr, INb[ti]], writes=[bGIM])
                lastc = n - 1 if d == 0 else 0
                gl_re, gl_im = W[st][4][:, lastc:lastc + 1], W[st][5][:, lastc:lastc + 1]
                cn, sn = COS[:, ti, n:n + 1], SIN[:, ti, n:n + 1]
                S.op("dve", lambda e: e.tensor_tensor(out=tn[:, ti, 0:1], in0=gl_im, in1=sn, op=ALU.mult), reads=[bGIM, b_sin], writes=[tnb[ti]])
                S.op("dve", lambda e: e.tensor_tensor(out=tn[:, ti, 1:2], in0=gl_im, in1=cn, op=ALU.mult), reads=[bGIM, b_cos], writes=[tnb[ti]])
                S.op("dve", lambda e: e.scalar_tensor_tensor(out=INIT[:, ti, 0:1], in0=gl_re, scalar=cn, in1=tn[:, ti, 0:1], op0=ALU.mult, op1=ALU.subtract),
                     reads=[bGRE, b_cos, tnb[ti]], writes=[INb[ti]])
                S.op("dve", lambda e: e.scalar_tensor_tensor(out=INIT[:, ti, 1:2], in0=gl_re, scalar=sn, in1=tn[:, ti, 1:2], op0=ALU.mult, op1=ALU.add),
                     reads=[bGRE, b_sin, tnb[ti]], writes=[INb[ti]])
                S.op("pool", lambda e: e.tensor_tensor(out=TA, in0=GRE, in1=cosv, op=ALU.mult), reads=[bGRE, b_cos], writes=[bTA])
                S.op("dve", lambda e: e.tensor_tensor(out=TB, in0=GIM, in1=sinv, op=ALU.mult), reads=[bGIM, b_sin], writes=[bTB])
                S.op("dve", lambda e: e.tensor_tensor(out=HRE, in0=TA, in1=TB, op=ALU.subtract), reads=[bTA, bTB], writes=[bHRE])
                S.op("pool", lambda e: e.tensor_tensor(out=TA, in0=GRE, in1=sinv, op=ALU.mult), reads=[bGRE, b_sin, bTA], writes=[bTA])
                S.op("dve", lambda e: e.tensor_tensor(out=TB, in0=GIM, in1=cosv, op=ALU.mult), reads=[bGIM, b_cos, bTB], writes=[bTB])
                S.op("dve", lambda e: e.tensor_tensor(out=HIM, in0=TA, in1=TB, op=ALU.add), reads=[bTA, bTB], writes=[bHIM])
                S.op("pe", lambda e: e.matmul(yps[:, :n], lhsT=CRE[:, ti, :], rhs=HRE, start=(pair == 0), stop=False), reads=[b_c, bHRE], writes=[ypb])
                S.op("pe", lambda e: e.matmul(yps[:, :n], lhsT=CIMn[:, ti, :], rhs=HIM, start=False, stop=(pair == 3)), reads=[b_c, bHIM], writes=[ypb])
            ya = YACC[:, c0:c0 + n]
            if debug == 2 and si == 0 and d == 0:
                for st_ in range(2):
                    for wi in range(6):
                        dump("w%d_%d" % (st_, wi), W[st_][wi][:], [Wb[st_][wi]], [P, BLK])
                    for wi in range(2):
                        dump("h%d_%d" % (st_, wi), HB[st_][wi][:], [HBb[st_][wi]], [P, BLK], cast=True)
                dump("init", INIT[:].rearrange("p a b -> p (a b)"), INb, [P, 16])
                ydb = C.sb([P, BLK], F32, "ydb"); ydbb = Buf()
                S.op("dve", lambda e: e.tensor_copy(out=ydb[:], in_=yps[:, :]), reads=[ypb], writes=[ydbb])
                dump("yps", ydb[:], [ydbb], [P, BLK])
                S.finish()
                C.es.close()
                return nc
            if d == 0:
                S.op("dve", lambda e: e.tensor_tensor(out=ya, in0=yps[:, :n], in1=ya, op=ALU.add), reads=[ypb, YAb[si]], writes=[YAb[si]])
            else:
                zt, ztb = ZT[si % 2][:, 0:n], ZTb[si % 2]
                zu, zub = ZU[si % 2][:, 0:n], ZUb[si % 2]
                S.op("dve", lambda e: e.tensor_tensor(out=zt, in0=yps[:, :n], in1=ya, op=ALU.add), reads=[ypb, YAb[si]], writes=[ztb])
                S.op("act", lambda e: e.activation(out=zu, in_=zt, func=AF.Square), reads=[ztb], writes=[zub])
                S.op("dve", lambda e: e.tensor_scalar(out=zu, in0=zu, scalar1=0.044715, scalar2=1.0, op0=ALU.mult, op1=ALU.add), reads=[zub], writes=[zub])
                S.op("dve", lambda e: e.tensor_tensor(out=zu, in0=zu, in1=zt, op=ALU.mult), reads=[zub, ztb], writes=[zub])
                S.op("act", lambda e: e.activation(out=zu, in_=zu, func=AF.Sigmoid, scale=1.5957691216057308), reads=[zub], writes=[zub])
                S.op("dve", lambda e: e.tensor_tensor(out=zt, in0=zt, in1=zu, op=ALU.mult), reads=[zub, ztb], writes=[ztb])
                S.dma(z_d[:, c0:c0 + n], zt, reads=[ztb], is_output=True)
    S.finish()
    C.es.close()
    return nc


def phase1a_inputs(b, j, seqT_b, inp):
    ev_w = np.asarray(inp["ev_w_in"][0])
    ws = np.ascontiguousarray(ev_w[:, 672 + P * j: 672 + P * (j + 1)])
    par = np.zeros((P, 3, 8), np.float32)
    bcp = np.zeros((P, 4, 8, P), np.float32)
    lre, lim, ldt = inp["s5_lam_re"][0], inp["s5_lam_im"][0], inp["s5_log_dt"][0]
    bre, bim, cre, cim = inp["s5_b_re"][0], inp["s5_b_im"][0], inp["s5_c_re"][0], inp["s5_c_im"][0]
    for d in range(2):
        for pair in range(4):
            ti = d * 4 + pair
            for gi in range(2):
                g8 = 2 * pair + gi
                g = 8 * j + g8
                rows = slice(gi * 64, gi * 64 + 64)
                par[rows, 0, ti] = lre[d, g]
                par[rows, 1, ti] = lim[d, g]
                par[rows, 2, ti] = ldt[d, g]
                colsl = slice(g8 * 16, g8 * 16 + 16)
                bcp[rows, 0, ti, colsl] = bre[d, g]
                bcp[rows, 1, ti, colsl] = bim[d, g]
                bcp[rows, 2, ti, colsl] = cre[d, g].T
                bcp[rows, 3, ti, colsl] = cim[d, g].T
    dsk = np.ascontiguousarray(np.asarray(inp["s5_d"][0])[P * j:P * (j + 1)].reshape(P, 1))
    iota = np.ascontiguousarray(np.broadcast_to(np.arange(TCOL, dtype=np.float32)[None, :], (P, TCOL)))
    return {"xT": seqT_b, "cS": lay_cS(inp["c"][b], inp["c_ctx"]), "adaw": np.ascontiguousarray(inp["ada_w"][0]),
            "adab": lay_vec_cols(inp["ada_b"][0], 48), "ws": ws, "s5par": par, "s5bc": bcp, "dskip": dsk, "iota": iota,
            "ident": np.eye(P, dtype=np.float32)}


NKT = LTOT // P


def build_phase1b(nqb=SEQ // BLK, stage=0, nproj=None):
    nc = bass.Bass("TRN2", target_bir_lowering=False)
    C = Ctx(nc)
    C.init_psum()
    S = C.S
    x_d = C.dram_in("xT", [D_MODEL, LTOT])
    mods_d = C.dram_in("mods", [P, 48, 2])
    wA_d = C.dram_in("wA", [D_MODEL, 704])
    wuq_d = C.dram_in("wuq", [384, 256])
    wukv_d = C.dram_in("wukv", [256, 256])
    nrm_d = C.dram_in("norms", [P, 5])
    rope_d = C.dram_in("rope", [32, 2, SEQ])
    sel_d = C.dram_in("sel", [P, 64])
    att_d = C.dram_out("attT", [P, LTOT])

    MODS = C.sb([P, 48, 2], F32, "MODS"); b_mods = Buf()
    M1P = C.sb([P, 48, 2], F32, "M1P"); b_m1p = Buf()
    S.dma(MODS[:], mods_d[:, :, :], writes=[b_mods])
    S.op("dve", lambda e: e.tensor_scalar(out=M1P[:], in0=MODS[:], scalar1=1.0, scalar2=None, op0=ALU.add), reads=[b_mods], writes=[b_m1p])
    nrm = C.sb([P, 5], F32, "nrm"); b_nrm = Buf()
    S.dma(nrm[:], nrm_d[:, :], writes=[b_nrm])
    sel = C.sb([P, 64], F32, "sel"); b_sel = Buf()
    S.dma(sel[:], sel_d[:, :], writes=[b_sel])
    onesB = C.sb([P, P], BF16, "onesB"); b_ones = Buf()
    S.op("dve", lambda e: e.memset(onesB[:], 1.0), writes=[b_ones])
    wA = C.sb([P, NCT, 704], BF16, "wA"); b_wA = Buf()
    wuq = C.sb([P, 3, 256], BF16, "wuq"); b_wuq = Buf()
    wukv = C.sb([P, 2, 256], BF16, "wukv"); b_wukv = Buf()
    with contextlib.ExitStack() as es_t:
        wv = wA_d.rearrange("(kt p) n -> p kt n", p=P)
        for hlf in range(2):
            load_cast(C, wA[:, 4 * hlf:4 * hlf + 4, :], b_wA, wv[:, 4 * hlf:4 * hlf + 4, :], [P, 4, 704], es_t)
        load_cast(C, wuq[:], b_wuq, wuq_d.rearrange("(kt p) n -> p kt n", p=P)[:, :, :], [P, 3, 256], es_t)
        load_cast(C, wukv[:], b_wukv, wukv_d.rearrange("(kt p) n -> p kt n", p=P)[:, :, :], [P, 2, 256], es_t)
        S.barrier()
    QT = [C.sb([P, LTOT], BF16, "QT") for _ in range(2)]
    KT = [C.sb([P, LTOT], BF16, "KT") for _ in range(2)]
    VA = C.sb([P, NKT, 2, 65], BF16, "VA")
    nblk = 1 + SEQ // BLK
    blocks = [(0, CTX, 1, None)] + [(CTX + BLK * i, BLK, 0, i) for i in range(SEQ // BLK)]
    pblocks = blocks if nproj is None else blocks[:1 + nproj]
    QTb = [[Buf() for _ in range(nblk)] for _ in range(2)]
    KTb = [[Buf() for _ in range(nblk)] for _ in range(2)]
    VAb = [Buf() for _ in range(nblk)]
    b_va1 = Buf()
    import os as _os
    SK = _os.environ.get("P1B_SKIP", "").split(",")
    if "vamem" not in SK:
        S.op("dve", lambda e: e.memset(VA[:, :, :, 64:65], 1.0), writes=[b_va1])
    xv = x_d.rearrange("(ct p) n -> p ct n", p=P)
    CUT = int(_os.environ.get("P1B_CUT", "0"))
    cpn = [0]

    class StopEmit(Exception):
        pass

    def cp():
        cpn[0] += 1
        if CUT and cpn[0] >= CUT:
            S.mute = True
    try:
      with contextlib.ExitStack() as es_p:
        cp()
        xs = C.sb([P, NCT, BLK], F32, "xs", es_p); xsb = Buf()
        ub = [C.sb([P, NCT, BLK], BF16, "ub", es_p) for _ in range(2)]; ubb = [Buf(), Buf()]
        SQ = C.sb([P, 3, BLK], BF16, "SQ", es_p); SQb = [Buf() for _ in range(3)]
        CQ = C.sb([P, 3, BLK], F32, "CQ", es_p); CQb = [Buf() for _ in range(3)]
        CQN = C.sb([P, 3, BLK], BF16, "CQN", es_p); CQNb = [Buf() for _ in range(3)]
        CKVN = C.sb([P, 2, BLK], BF16, "CKVN", es_p); CKVNb = [Buf() for _ in range(2)]
        RS = C.sb([P, 2, BLK], F32, "RS", es_p); RSb = [Buf(), Buf()]
        ROPE = C.sb([P, 2, BLK], F32, "ROPE", es_p); b_rope = Buf()
        TR = C.sb([P, 4, BLK], F32, "TR", es_p); TRb = [Buf() for _ in range(4)]
        for bi, (c0, n, var, li) in enumerate(pblocks):
            vs = slice(var, var + 1)
            u, u_b = ub[bi % 2], ubb[bi % 2]
            ntile = n // P
            S.dma(xs[:, :, 0:n], xv[:, :, c0:c0 + n], writes=[xsb])
            if li is not None and "rope" not in SK:
                S.dma(ROPE[64:96, :, 0:n], rope_d[:, :, li * BLK:li * BLK + n], writes=[b_rope])
            for ct in range(NCT):
                S.op("act", lambda e, ct=ct: e.activation(out=u[:, ct, 0:n], in_=xs[:, ct, 0:n], func=AF.Identity,
                                                          scale=M1P[:, 8 + ct, vs], bias=MODS[:, ct, vs]),
                     reads=[xsb, b_m1p, b_mods], writes=[u_b])

            def rmsnorm(ntl, wc0, dim, gcol, DST, DSTb, rsi):
                for m in range(ntl):
                    pt, pb = C.ps()
                    for kt in range(NCT):
                        S.op("pe", lambda e, kt=kt, m=m, pt=pt: e.matmul(pt[:, :n], lhsT=wA[:, kt, wc0 + m * P:wc0 + (m + 1) * P], rhs=u[:, kt, 0:n],
                                                                       start=(kt == 0), stop=(kt == NCT - 1)), reads=[b_wA, u_b], writes=[pb])
                    S.op("dve", lambda e, m=m, pt=pt: e.tensor_copy(out=CQ[:, m, 0:n], in_=pt[:, :n]), reads=[pb], writes=[CQb[m]])
                    S.op("pool", lambda e, m=m: e.tensor_tensor(out=SQ[:, m, 0:n], in0=CQ[:, m, 0:n], in1=CQ[:, m, 0:n], op=ALU.mult),
                         reads=[CQb[m]], writes=[SQb[m]])
                cp()
                sp_, spb = C.ps()
                for m in range(ntl):
                    S.op("pe", lambda e, m=m: e.matmul(sp_[:, :n], lhsT=onesB[:], rhs=SQ[:, m, 0:n], start=(m == 0), stop=(m == ntl - 1)),
                         reads=[b_ones, SQb[m]], writes=[spb])
                rs, rsb = RS[:, rsi, 0:n], RSb[rsi]
                cp()
                S.op("act", lambda e: e.activation(out=rs, in_=sp_[:, :n], func=AF.Sqrt, scale=1.0 / dim, bias=NORM_EPS), reads=[spb], writes=[rsb])
                cp()
                S.op("dve", lambda e: e.reciprocal(out=rs, in_=rs), reads=[rsb], writes=[rsb])
                cp()
                for m in range(ntl):
                    S.op("dve", lambda e, m=m: e.scalar_tensor_tensor(out=DST[:, m, 0:n], in0=CQ[:, m, 0:n], scalar=nrm[:, gcol + m:gcol + m + 1],
                                                                       in1=rs, op0=ALU.mult, op1=ALU.mult),
                         reads=[CQb[m], b_nrm, rsb], writes=[DSTb[m]])

            cp()
            rmsnorm(3, 0, 384.0, 0, CQN, CQNb, 0)
            cp()
            for hh in range(2):
                pq, pqb = C.ps()
                pr, prb = C.ps()
                for m in range(3):
                    S.op("pe", lambda e, m=m: e.matmul(pq[0:96, :n], lhsT=wuq[:, m, hh * P:hh * P + 96], rhs=CQN[:, m, 0:n],
                                                       start=(m == 0), stop=(m == 2)), reads=[b_wuq, CQNb[m]], writes=[pqb])
                if li is not None:
                    for m in range(3):
                        S.op("pe", lambda e, m=m: e.matmul(pr[64:96, :n], lhsT=wuq[:, m, hh * P + 96:hh * P + P], rhs=CQN[:, m, 0:n],
                                                           start=(m == 0), stop=(m == 2)), reads=[b_wuq, CQNb[m]], writes=[prb])
                S.op("act", lambda e: e.copy(out=QT[hh][0:64, c0:c0 + n], in_=pq[0:64, :n]), reads=[pqb], writes=[QTb[hh][bi]])
                if li is None:
                    S.op("act", lambda e: e.copy(out=QT[hh][64:96, c0:c0 + n], in_=pq[64:96, :n]), reads=[pqb], writes=[QTb[hh][bi]])
                else:
                    S.op("dve", lambda e: e.tensor_tensor(out=TR[64:96, 0, 0:n], in0=pq[64:96, :n], in1=ROPE[64:96, 0, 0:n], op=ALU.mult),
                         reads=[pqb, b_rope], writes=[TRb[0]])
                    S.op("dve", lambda e: e.tensor_tensor(out=TR[64:96, 1, 0:n], in0=pr[64:96, :n], in1=ROPE[64:96, 1, 0:n], op=ALU.mult),
                         reads=[prb, b_rope], writes=[TRb[1]])
                    S.op("dve", lambda e: e.tensor_tensor(out=QT[hh][64:96, c0:c0 + n], in0=TR[64:96, 0, 0:n], in1=TR[64:96, 1, 0:n], op=ALU.add),
                         reads=[TRb[0], TRb[1]], writes=[QTb[hh][bi]])
            cp()
            rmsnorm(2, 384, 256.0, 3, CKVN, CKVNb, 1)
            cp()
            for hh in range(2):
                pk, pkb = C.ps()
                for m in range(2):
                    S.op("pe", lambda e, m=m: e.matmul(pk[0:64, :n], lhsT=wukv[:, m, hh * 64:hh * 64 + 64], rhs=CKVN[:, m, 0:n],
                                                       start=(m == 0), stop=(m == 1)), reads=[b_wukv, CKVNb[m]], writes=[pkb])
                S.op("act", lambda e: e.copy(out=KT[hh][0:64, c0:c0 + n], in_=pk[0:64, :n]), reads=[pkb], writes=[KTb[hh][bi]])
            cp()
            pkr, pkrb = C.ps()
            pkq, pkqb = C.ps()
            for kt in range(NCT):
                S.op("pe", lambda e, kt=kt: e.matmul(pkr[64:96, :n], lhsT=wA[:, kt, 640:672], rhs=u[:, kt, 0:n], start=(kt == 0), stop=(kt == NCT - 1)),
                     reads=[b_wA, u_b], writes=[pkrb])
            if li is None:
                for hh in range(2):
                    S.op("act", lambda e: e.copy(out=KT[hh][64:96, c0:c0 + n], in_=pkr[64:96, :n]), reads=[pkrb], writes=[KTb[hh][bi]])
            else:
                for kt in range(NCT):
                    S.op("pe", lambda e, kt=kt: e.matmul(pkq[64:96, :n], lhsT=wA[:, kt, 672:704], rhs=u[:, kt, 0:n], start=(kt == 0), stop=(kt == NCT - 1)),
                         reads=[b_wA, u_b], writes=[pkqb])
                S.op("dve", lambda e: e.tensor_tensor(out=TR[64:96, 2, 0:n], in0=pkr[64:96, :n], in1=ROPE[64:96, 0, 0:n], op=ALU.mult),
                     reads=[pkrb, b_rope], writes=[TRb[2]])
                S.op("dve", lambda e: e.tensor_tensor(out=TR[64:96, 3, 0:n], in0=pkq[64:96, :n], in1=ROPE[64:96, 1, 0:n], op=ALU.mult),
                     reads=[pkqb, b_rope], writes=[TRb[3]])
                for hh in range(2):
                    S.op("dve", lambda e: e.tensor_tensor(out=KT[hh][64:96, c0:c0 + n], in0=TR[64:96, 2, 0:n], in1=TR[64:96, 3, 0:n], op=ALU.add),
                         reads=[TRb[2], TRb[3]], writes=[KTb[hh][bi]])
            cp()
            pv, pvb = C.ps()
            for tt in range(ntile):
                for m in range(2):
                    S.op("pe", lambda e, m=m, tt=tt: e.matmul(pv[:, tt * P:(tt + 1) * P], lhsT=CKVN[:, m, tt * P:(tt + 1) * P], rhs=wukv[:, m, 128:256],
                                                              start=(m == 0), stop=(m == 1)), reads=[b_wukv, CKVNb[m]], writes=[pvb])
            gt0 = c0 // P
            pvv = pv[:, 0:n]
            pv4 = AP(pvv.tensor, pvv.offset, [list(pvv.ap[0]), [P, ntile], [64, 2], [1, 64]])
            if "vacopy" not in SK:
                S.op("act", lambda e: e.copy(out=VA[:, gt0:gt0 + ntile, :, 0:64], in_=pv4), reads=[pvb, b_va1], writes=[VAb[bi]])
        S.barrier()
    except StopEmit:
        S.barrier()
        S.finish()
        C.es.close()
        return nc
    if stage == 1:
        ND = 768
        for nm, T_, bl in (("q0", QT[0], QTb[0]), ("k0", KT[0], KTb[0]), ("q1", QT[1], QTb[1]), ("k1", KT[1], KTb[1])):
            o = C.dram_out("dbg_" + nm, [P, ND])
            tmpd = C.sb([P, ND], F32, "dbgc"); tb_ = Buf()
            S.op("dve", lambda e: e.tensor_copy(out=tmpd[0:96, :], in_=T_[0:96, 0:ND]), reads=bl, writes=[tb_])
            S.dma(o[0:96, :], tmpd[0:96, :], reads=[tb_], is_output=True)
        o = C.dram_out("dbg_va", [P, 6 * 130])
        tmpd = C.sb([P, 6 * 130], F32, "dbgv"); tb_ = Buf()
        S.op("dve", lambda e: e.tensor_copy(out=tmpd[:], in_=VA[:, 0:6, :, :].rearrange("p a b c -> p (a b c)")), reads=VAb + [b_va1], writes=[tb_])
        S.dma(o, tmpd[:], reads=[tb_], is_output=True)
        S.finish()
        C.es.close()
        return nc
    NPT = 4
    PT = [C.sb([P, BLK], BF16, "PT") for _ in range(NPT)]; PTb = [Buf() for _ in range(NPT)]
    OS = [C.sb([P, BLK], F32, "OS") for _ in range(2)]; OSb = [Buf(), Buf()]
    RC = [C.sb([P, BLK], F32, "RC") for _ in range(2)]; RCb = [Buf(), Buf()]
    AT = [C.sb([P, BLK], F32, "AT") for _ in range(2)]; ATb = [Buf(), Buf()]
    C.ps_n = 5
    C.ps_i = 0
    k = 0
    ip = 0
    qblocks = [blocks[0]] + blocks[1:1 + nqb]
    for hh in range(2):
        for (c0, n, var, li) in qblocks:
            kts = [0, 1] if li is None else list(range(NKT))
            bi = 0 if li is None else 1 + li
            ops_, opb = C.ps_fixed(6 + (k % 2))
            LOOK = 2
            stg = {}

            def qk(i):
                kt = kts[i]
                kb = 0 if kt < 2 else 1 + (kt - 2) // 4
                sps, spb = C.ps()
                S.op("pe", lambda e: e.matmul(sps[:, :n], lhsT=KT[hh][0:96, kt * P:(kt + 1) * P], rhs=QT[hh][0:96, c0:c0 + n], start=True, stop=True),
                     reads=[KTb[hh][kb], QTb[hh][bi]], writes=[spb])
                stg[i] = (sps, spb, kb, kt)

            for i in range(min(LOOK, len(kts))):
                qk(i)
            for i in range(len(kts)):
                if i + LOOK < len(kts):
                    qk(i + LOOK)
                sps, spb, kb, kt = stg.pop(i)
                pt_, ptb = PT[ip % NPT], PTb[ip % NPT]
                ip += 1
                S.op("act", lambda e: e.activation(out=pt_[:, 0:n], in_=sps[:, :n], func=AF.Exp, scale=MLA_SCALE), reads=[spb], writes=[ptb])
                S.op("pe", lambda e: e.matmul(ops_[0:65, :n], lhsT=VA[:, kt, hh, :], rhs=pt_[:, 0:n], start=(i == 0), stop=(i == len(kts) - 1)),
                     reads=[VAb[kb], b_va1, ptb], writes=[opb])
            os_, osb = OS[k % 2], OSb[k % 2]
            rc, rcb = RC[k % 2], RCb[k % 2]
            at, atb = AT[k % 2], ATb[k % 2]
            k += 1
            S.op("dve", lambda e: e.tensor_copy(out=os_[0:65, 0:n], in_=ops_[0:65, :n]), reads=[opb], writes=[osb])
            bps, bpb = C.ps()
            S.op("pe", lambda e: e.matmul(bps[0:64, :n], lhsT=sel[0:65, 0:64], rhs=os_[0:65, 0:n], start=True, stop=True), reads=[b_sel, osb], writes=[bpb])
            S.op("dve", lambda e: e.reciprocal(out=rc[0:64, 0:n], in_=bps[0:64, :n]), reads=[bpb], writes=[rcb])
            S.op("pool", lambda e: e.tensor_tensor(out=at[0:64, 0:n], in0=os_[0:64, 0:n], in1=rc[0:64, 0:n], op=ALU.mult), reads=[osb, rcb], writes=[atb])
            S.dma(att_d[hh * 64:(hh + 1) * 64, c0:c0 + n], at[0:64, 0:n], reads=[atb], is_output=True)
    S.finish()
    C.es.close()
    return nc


def rope_tables():
    t = np.arange(SEQ)
    row = (t // 64).astype(np.float32)
    col = (t % 64).astype(np.float32)
    inv = (10000.0 ** (-np.arange(8, dtype=np.float32) / 8)).astype(np.float32)
    ar = row[:, None] * inv
    ac = col[:, None] * inv
    ang = np.concatenate([ar, ar, ac, ac], axis=-1).astype(np.float32)
    cos = np.cos(ang).astype(np.float32).T
    sin = np.sin(ang).astype(np.float32).T
    sign = np.where((np.arange(32) % 16) < 8, -1.0, 1.0).astype(np.float32)[:, None]
    return np.ascontiguousarray(np.stack([cos, sign * sin], axis=1))


def rot_perm():
    d = np.arange(32)
    return np.where((d % 16) < 8, d + 8, d - 8)


def phase1b_inputs(b, j, seqT_b, mods0_b, inp):
    ev_w = np.asarray(inp["ev_w_in"][0])
    perm = rot_perm()
    wkr = ev_w[:, 640:672]
    wA = np.ascontiguousarray(np.concatenate([ev_w[:, 0:640], wkr, wkr[:, perm]], axis=1))
    wuq_full = np.asarray(inp["mla_w_uq"][0])
    wukv_full = np.asarray(inp["mla_w_ukv"][0])
    qcols, kcols, vcols = [], [], []
    for hh in range(2):
        h = 2 * j + hh
        wq = wuq_full[:, h * 96:(h + 1) * 96]
        qcols += [wq, wq[:, 64:96][:, perm]]
        kcols.append(wukv_full[:, h * 128:h * 128 + 64])
        vcols.append(wukv_full[:, h * 128 + 64:(h + 1) * 128])
    wuq = np.ascontiguousarray(np.concatenate(qcols, axis=1))
    wukv = np.ascontiguousarray(np.concatenate(kcols + vcols, axis=1))
    norms = np.ascontiguousarray(np.concatenate([lay_vec_cols(inp["mla_q_norm"][0], 3), lay_vec_cols(inp["mla_kv_norm"][0], 2)], axis=1))
    sel = np.zeros((P, 64), np.float32)
    sel[64, :] = 1.0
    return {"xT": seqT_b, "mods": mods0_b, "wA": wA, "wuq": wuq, "wukv": wukv, "norms": norms, "rope": rope_tables(), "sel": sel}


_PROGS = {}


def _prog(name, fn):
    if name not in _PROGS:
        _PROGS[name] = fn()
    return _PROGS[name]


def kernel(**inputs):
    inp = {k: np.asarray(v) for k, v in inputs.items()}
    x, ctx = inp["x"].astype(np.float32), inp["ctx"].astype(np.float32)
    B = x.shape[0]
    cores = list(range(8))
    bj = [(c // 4, c % 4) for c in cores]
    seqT = [np.ascontiguousarray(np.concatenate([ctx[b], x[b]], axis=0).T) for b in range(B)]
    r1a = run_bass_kernel_spmd(_prog("p1a", build_phase1a), [phase1a_inputs(b, j, seqT[b], inp) for (b, j) in bj], core_ids=cores).results
    mods0 = [np.ascontiguousarray(r1a[c]["mods0"]) for c in cores]
    r1b = run_bass_kernel_spmd(_prog("p1b", build_phase1b), [phase1b_inputs(b, j, seqT[b], mods0[c], inp) for c, (b, j) in enumerate(bj)],
                               core_ids=cores).results
    attzT = []
    for b in range(B):
        attzT.append(np.ascontiguousarray(np.concatenate([r1b[4 * b + j]["attT"] for j in range(4)] + [r1a[4 * b + j]["zT"] for j in range(4)], axis=0)))
    r2 = run_bass_kernel_spmd(_prog("p2", build_phase2), [phase2_inputs(b, j, attzT[b], seqT[b], mods0[c], inp) for c, (b, j) in enumerate(bj)],
                              core_ids=cores).results
    mods1 = [np.ascontiguousarray(r2[c]["mods1"]) for c in cores]
    x2T = []
    for b in range(B):
        x2T.append(np.ascontiguousarray(np.concatenate([r2[4 * b]["xo"][:, :CTX]] + [r2[4 * b + j]["xo"][:, CTX:] for j in range(4)], axis=1)))
    r3 = run_bass_kernel_spmd(_prog("p3", build_phase3), [phase3_inputs(b, j, x2T[b], mods1[c], inp) for c, (b, j) in enumerate(bj)],
                              core_ids=cores).results
    ogT = [np.ascontiguousarray(np.concatenate([r3[4 * b + j]["ogT"] for j in range(4)], axis=0)) for b in range(B)]
    r4 = run_bass_kernel_spmd(_prog("p4", build_phase4),
                              [phase4_inputs(b, j, ogT[b], np.ascontiguousarray(x2T[b][:, CTX:]), mods1[c], inp) for c, (b, j) in enumerate(bj)],
                              core_ids=cores).results
    out = np.zeros((B, SEQ, D_MODEL), np.float32)
    for c, (b, j) in enumerate(bj):
        out[b, QTR * j:QTR * (j + 1)] = r4[c]["xo"].T
    return out
```
